# Optimizing a Trainium2 kernel written in Bass

```python
import math
import jax, jax.numpy as jnp
from jax import lax
import numpy as np

D_MODEL = 1024
BATCH = 16
SEQ = 256
DEPTH = 1
DEC_BATCH = 4
DEC_SEQ = 4096
PAST_LEN = 512

GRID_W = 64
HEAD_DIM = 64
N_HEADS = 8
KV_HEADS = 2
REP = N_HEADS // KV_HEADS
ATTN_WIDTH = N_HEADS * HEAD_DIM
KV_WIDTH = KV_HEADS * HEAD_DIM
WINDOW = 128
BLOCK = 128
ROPE_THETA = 10000.0
ROPE_FREQS = HEAD_DIM // 4
SSM_CH = 16
SSM_WIDTH = D_MODEL - ATTN_WIDTH
SSM_GROUPS = SSM_WIDTH // SSM_CH
SSM_STATE = 64
N_DIRS = 2
IN_WIDTH = ATTN_WIDTH + 2 * KV_WIDTH + SSM_WIDTH
D_FF = 2816
CONV_W = 3
EPS = 1e-6
NEG_INF = -1e30

kernel_name = "hymba_s5_prefix_dit_step"


def rms_norm(x, g):
    xf = x.astype(jnp.float32)
    y = xf * lax.rsqrt(jnp.mean(xf * xf, axis=-1, keepdims=True) + EPS)
    return (y * g.astype(jnp.float32)).astype(x.dtype)


def adaln(cond, w_ada, b_ada):
    mod = jax.nn.silu(cond) @ w_ada + b_ada
    return jnp.split(mod, 6, axis=-1)


def modulate(x, g, shift, scale):
    return rms_norm(x, g) * (1 + scale[:, None, :]) + shift[:, None, :]


def in_projection(h, w_in):
    b, n = h.shape[:2]
    proj = h @ w_in
    q, k, v, u = jnp.split(proj, [ATTN_WIDTH, ATTN_WIDTH + KV_WIDTH, ATTN_WIDTH + 2 * KV_WIDTH], axis=-1)
    return (q.reshape(b, n, N_HEADS, HEAD_DIM), k.reshape(b, n, KV_HEADS, HEAD_DIM),
            v.reshape(b, n, KV_HEADS, HEAD_DIM), u)


def axial_rope_angles(n_tokens):
    rows = n_tokens // GRID_W
    row = jnp.repeat(jnp.arange(rows, dtype=jnp.float32), GRID_W)
    col = jnp.tile(jnp.arange(GRID_W, dtype=jnp.float32), rows)
    inv_freq = ROPE_THETA ** (-jnp.arange(ROPE_FREQS, dtype=jnp.float32) / ROPE_FREQS)
    return row[:, None] * inv_freq[None, :], col[:, None] * inv_freq[None, :]


def _rotate_half(v, ang):
    v1, v2 = jnp.split(v, 2, axis=-1)
    cos = jnp.cos(ang)[None, :, None, :]
    sin = jnp.sin(ang)[None, :, None, :]
    return jnp.concatenate([v1 * cos - v2 * sin, v1 * sin + v2 * cos], axis=-1)


def apply_axial_rope(x, ang_row, ang_col):
    xf = x.astype(jnp.float32)
    half = HEAD_DIM // 2
    out = jnp.concatenate([_rotate_half(xf[..., :half], ang_row),
                           _rotate_half(xf[..., half:], ang_col)], axis=-1)
    return out.astype(x.dtype)


def _sink_column(sink, b):
    s = sink.astype(jnp.float32).reshape(1, KV_HEADS, REP, 1, 1)
    return jnp.broadcast_to(s, (b, KV_HEADS, REP, BLOCK, 1))


def context_attention(q, k, v, sink):
    b, lc = q.shape[:2]
    nq = lc // BLOCK
    scale = HEAD_DIM ** -0.5
    qb = q.reshape(b, nq, BLOCK, KV_HEADS, REP, HEAD_DIM).transpose(1, 0, 2, 3, 4, 5)
    sink_col = _sink_column(sink, b)

    def one_block(qi):
        s = jnp.einsum("bqkrd,bskd->bkrqs", qi, k).astype(jnp.float32) * scale
        p = jax.nn.softmax(jnp.concatenate([s, sink_col], axis=-1), axis=-1)[..., :-1]
        return jnp.einsum("bkrqs,bskd->bqkrd", p.astype(v.dtype), v)

    o = lax.map(one_block, qb)
    return o.transpose(1, 0, 2, 3, 4, 5).reshape(b, lc, ATTN_WIDTH)


def latent_attention(q, k, v, ck, cv, sink):
    b, n = q.shape[:2]
    nb = n // BLOCK
    scale = HEAD_DIM ** -0.5
    qb = q.reshape(b, nb, BLOCK, KV_HEADS, REP, HEAD_DIM).transpose(1, 0, 2, 3, 4, 5)

    def windows(t):
        tp = jnp.pad(t, ((0, 0), (BLOCK, BLOCK), (0, 0), (0, 0))).reshape(b, nb + 2, BLOCK, KV_HEADS, HEAD_DIM)
        w = jnp.concatenate([tp[:, :-2], tp[:, 1:-1], tp[:, 2:]], axis=2)
        return w.transpose(1, 0, 2, 3, 4)

    kw, vw = windows(k), windows(v)
    r_idx = jnp.arange(BLOCK)[:, None]
    w_idx = jnp.arange(3 * BLOCK)[None, :]
    band = (w_idx >= r_idx + BLOCK - WINDOW) & (w_idx <= r_idx + BLOCK + WINDOW)
    sink_col = _sink_column(sink, b)

    def one_block(args):
        qi, ki, vi, blk = args
        j = (blk - 1) * BLOCK + w_idx
        valid = band & (j >= 0) & (j < n)
        s_loc = jnp.einsum("bqkrd,bskd->bkrqs", qi, ki).astype(jnp.float32) * scale
        s_loc = jnp.where(valid[None, None, None], s_loc, NEG_INF)
        s_ctx = jnp.einsum("bqkrd,bskd->bkrqs", qi, ck).astype(jnp.float32) * scale
        p = jax.nn.softmax(jnp.concatenate([s_loc, s_ctx, sink_col], axis=-1), axis=-1)
        p_loc = p[..., :3 * BLOCK].astype(vi.dtype)
        p_ctx = p[..., 3 * BLOCK:-1].astype(cv.dtype)
        return (jnp.einsum("bkrqs,bskd->bqkrd", p_loc, vi)
                + jnp.einsum("bkrqs,bskd->bqkrd", p_ctx, cv))

    o = lax.map(one_block, (qb, kw, vw, jnp.arange(nb)))
    return o.transpose(1, 0, 2, 3, 4, 5).reshape(b, n, ATTN_WIDTH)


def zoh_discretize(lam_re, lam_im, log_dt, b_re, b_im):
    lam_re = lam_re.astype(jnp.float32)
    lam_im = lam_im.astype(jnp.float32)
    dt = jnp.exp(log_dt.astype(jnp.float32))[:, None]
    mag = jnp.exp(lam_re * dt)
    ang = lam_im * dt
    a_re, a_im = mag * jnp.cos(ang), mag * jnp.sin(ang)
    den = lam_re * lam_re + lam_im * lam_im
    num_re = a_re - 1.0
    coef_re = (num_re * lam_re + a_im * lam_im) / den
    coef_im = (a_im * lam_re - num_re * lam_im) / den
    br, bi = b_re.astype(jnp.float32), b_im.astype(jnp.float32)
    bb_re = coef_re[..., None] * br - coef_im[..., None] * bi
    bb_im = coef_re[..., None] * bi + coef_im[..., None] * br
    return a_re, a_im, bb_re, bb_im


def _combine(e1, e2):
    a1r, a1i, b1r, b1i = e1
    a2r, a2i, b2r, b2i = e2
    return (a2r * a1r - a2i * a1i, a2r * a1i + a2i * a1r,
            a2r * b1r - a2i * b1i + b2r, a2r * b1i + a2i * b1r + b2i)


def diagonal_scan(u, a_re, a_im, bb_re, bb_im, h0, reverse):
    bu_re = jnp.einsum("bngp,gsp->bngs", u, bb_re)
    bu_im = jnp.einsum("bngp,gsp->bngs", u, bb_im)
    if h0 is not None:
        h0_re, h0_im = h0
        edge = -1 if reverse else 0
        bu_re = bu_re.at[:, edge].add(a_re * h0_re - a_im * h0_im)
        bu_im = bu_im.at[:, edge].add(a_re * h0_im + a_im * h0_re)
    ar = jnp.broadcast_to(a_re, bu_re.shape)
    ai = jnp.broadcast_to(a_im, bu_im.shape)
    _, _, h_re, h_im = lax.associative_scan(_combine, (ar, ai, bu_re, bu_im), reverse=reverse, axis=1)
    return h_re, h_im


def s5_bidirectional(u, params, h0_re, h0_im, return_final):
    lam_re, lam_im, log_dt, b_re, b_im, c_re, c_im, d, w_glu = params
    b, n = u.shape[:2]
    uf = u.astype(jnp.float32).reshape(b, n, SSM_GROUPS, SSM_CH)
    y = uf * d.astype(jnp.float32)
    fin_re, fin_im = [], []
    for dr, reverse in enumerate((False, True)):
        a_re, a_im, bb_re, bb_im = zoh_discretize(lam_re[dr], lam_im[dr], log_dt[dr], b_re[dr], b_im[dr])
        h0 = None if h0_re is None else (h0_re[:, dr].astype(jnp.float32), h0_im[:, dr].astype(jnp.float32))
        h_re, h_im = diagonal_scan(uf, a_re, a_im, bb_re, bb_im, h0, reverse)
        y = (y + jnp.einsum("bngs,gps->bngp", h_re, c_re[dr].astype(jnp.float32))
             - jnp.einsum("bngs,gps->bngp", h_im, c_im[dr].astype(jnp.float32)))
        if return_final:
            edge = 0 if reverse else -1
            fin_re.append(h_re[:, edge])
            fin_im.append(h_im[:, edge])
    x = jax.nn.gelu(y)
    out = x * jax.nn.sigmoid(jnp.einsum("bngp,gpq->bngq", x, w_glu.astype(jnp.float32)))
    out = out.reshape(b, n, SSM_WIDTH).astype(u.dtype)
    if return_final:
        return out, jnp.stack(fin_re, axis=1).astype(u.dtype), jnp.stack(fin_im, axis=1).astype(u.dtype)
    return out


def merge_groups(attn, ssm, g_attn, g_ssm, w_out):
    return jnp.concatenate([rms_norm(attn, g_attn), rms_norm(ssm, g_ssm)], axis=-1) @ w_out


def conv_ffn(h, w_up, conv_w, conv_b, w_down):
    n = h.shape[1]
    pad = CONV_W // 2
    up = jnp.pad(h @ w_up, ((0, 0), (pad, pad), (0, 0)))
    acc = up[:, 0:n] * conv_w[0]
    for i in range(1, CONV_W):
        acc = acc + up[:, i:i + n] * conv_w[i]
    g, val = jnp.split(acc + conv_b, 2, axis=-1)
    return (jax.nn.silu(g) * val) @ w_down


def setup_inputs(seed: int = 0) -> dict:
    key = jax.random.key(seed)
    ks = jax.random.split(key, 40)
    f32 = jnp.float32

    def nrm(k, shape, s):
        return jax.random.normal(k, shape, f32) * s

    n_idx = jnp.arange(SSM_STATE, dtype=f32)
    ssm_shape = (DEPTH, N_DIRS, SSM_GROUPS, SSM_STATE)
    return {
        "x_prompt": nrm(ks[0], (BATCH, SEQ, D_MODEL), 1.0),
        "x_sample": nrm(ks[1], (DEC_BATCH, DEC_SEQ, D_MODEL), 1.0),
        "cache_k": nrm(ks[2], (DEC_BATCH, DEPTH, PAST_LEN, KV_HEADS, HEAD_DIM), 1.0),
        "cache_v": nrm(ks[3], (DEC_BATCH, DEPTH, PAST_LEN, KV_HEADS, HEAD_DIM), 1.0),
        "state_ssm_re": nrm(ks[4], (DEC_BATCH, DEPTH, N_DIRS, SSM_GROUPS, SSM_STATE), 0.5),
        "state_ssm_im": nrm(ks[5], (DEC_BATCH, DEPTH, N_DIRS, SSM_GROUPS, SSM_STATE), 0.5),
        "c": nrm(ks[6], (DEC_BATCH, D_MODEL), 1.0),
        "c_ctx": nrm(ks[7], (D_MODEL,), 1.0),
        "w_ada": nrm(ks[8], (DEPTH, D_MODEL, 6 * D_MODEL), 0.5 * D_MODEL ** -0.5),
        "b_ada": nrm(ks[9], (DEPTH, 6 * D_MODEL), 0.01),
        "g_norm1": 1.0 + nrm(ks[10], (DEPTH, D_MODEL), 0.02),
        "g_norm2": 1.0 + nrm(ks[11], (DEPTH, D_MODEL), 0.02),
        "w_in": nrm(ks[12], (DEPTH, D_MODEL, IN_WIDTH), D_MODEL ** -0.5),
        "attn_sink": nrm(ks[13], (DEPTH, N_HEADS), 0.5),
        "ssm_lam_re": -0.5 + nrm(ks[14], ssm_shape, 0.01),
        "ssm_lam_im": math.pi * n_idx + nrm(ks[15], ssm_shape, 0.01),
        "ssm_log_dt": jax.random.uniform(ks[16], (DEPTH, N_DIRS, SSM_GROUPS), f32, math.log(1e-3), math.log(1e-1)),
        "ssm_b_re": nrm(ks[17], (DEPTH, N_DIRS, SSM_GROUPS, SSM_STATE, SSM_CH), (2 * SSM_CH) ** -0.5),
        "ssm_b_im": nrm(ks[18], (DEPTH, N_DIRS, SSM_GROUPS, SSM_STATE, SSM_CH), (2 * SSM_CH) ** -0.5),
        "ssm_c_re": nrm(ks[19], (DEPTH, N_DIRS, SSM_GROUPS, SSM_CH, SSM_STATE), (2 * SSM_STATE) ** -0.5),
        "ssm_c_im": nrm(ks[20], (DEPTH, N_DIRS, SSM_GROUPS, SSM_CH, SSM_STATE), (2 * SSM_STATE) ** -0.5),
        "ssm_d": nrm(ks[21], (DEPTH, SSM_GROUPS, SSM_CH), 1.0),
        "ssm_w_glu": nrm(ks[22], (DEPTH, SSM_GROUPS, SSM_CH, SSM_CH), SSM_CH ** -0.5),
        "g_out_attn": 1.0 + nrm(ks[23], (DEPTH, ATTN_WIDTH), 0.02),
        "g_out_ssm": 1.0 + nrm(ks[24], (DEPTH, SSM_WIDTH), 0.02),
        "w_out": nrm(ks[25], (DEPTH, D_MODEL, D_MODEL), D_MODEL ** -0.5),
        "w_up": nrm(ks[26], (DEPTH, D_MODEL, 2 * D_FF), D_MODEL ** -0.5),
        "conv_w": nrm(ks[27], (DEPTH, CONV_W, 2 * D_FF), CONV_W ** -0.5),
        "conv_b": nrm(ks[28], (DEPTH, 2 * D_FF), 0.01),
        "w_down": nrm(ks[29], (DEPTH, D_FF, D_MODEL), D_FF ** -0.5),
        "g_final": 1.0 + nrm(ks[30], (D_MODEL,), 0.02),
    }


def reference(x_prompt, x_sample, cache_k, cache_v, state_ssm_re, state_ssm_im, c, c_ctx,
              w_ada, b_ada, g_norm1, g_norm2, w_in, attn_sink,
              ssm_lam_re, ssm_lam_im, ssm_log_dt, ssm_b_re, ssm_b_im, ssm_c_re, ssm_c_im,
              ssm_d, ssm_w_glu, g_out_attn, g_out_ssm, w_out, w_up, conv_w, conv_b, w_down, g_final):
    cond_ctx = jnp.broadcast_to(c_ctx[None, :], (x_prompt.shape[0], D_MODEL))
    ang_row, ang_col = axial_rope_angles(x_sample.shape[1])
    xp, xs = x_prompt, x_sample
    new_k, new_v, new_s_re, new_s_im = [], [], [], []
    for l in range(DEPTH):
        ssm_l = (ssm_lam_re[l], ssm_lam_im[l], ssm_log_dt[l], ssm_b_re[l], ssm_b_im[l],
                 ssm_c_re[l], ssm_c_im[l], ssm_d[l], ssm_w_glu[l])

        sh1, sc1, gt1, sh2, sc2, gt2 = adaln(cond_ctx, w_ada[l], b_ada[l])
        q, k, v, u = in_projection(modulate(xp, g_norm1[l], sh1, sc1), w_in[l])
        attn = context_attention(q, k, v, attn_sink[l])
        ssm, fin_re, fin_im = s5_bidirectional(u, ssm_l, None, None, True)
        xp = xp + gt1[:, None, :] * merge_groups(attn, ssm, g_out_attn[l], g_out_ssm[l], w_out[l])
        h = modulate(xp, g_norm2[l], sh2, sc2)
        xp = xp + gt2[:, None, :] * conv_ffn(h, w_up[l], conv_w[l], conv_b[l], w_down[l])
        new_k.append(k)
        new_v.append(v)
        new_s_re.append(fin_re)
        new_s_im.append(fin_im)

        sh1, sc1, gt1, sh2, sc2, gt2 = adaln(c, w_ada[l], b_ada[l])
        q, k, v, u = in_projection(modulate(xs, g_norm1[l], sh1, sc1), w_in[l])
        q = apply_axial_rope(q, ang_row, ang_col)
        k = apply_axial_rope(k, ang_row, ang_col)
        attn = latent_attention(q, k, v, cache_k[:, l], cache_v[:, l], attn_sink[l])
        ssm = s5_bidirectional(u, ssm_l, state_ssm_re[:, l], state_ssm_im[:, l], False)
        xs = xs + gt1[:, None, :] * merge_groups(attn, ssm, g_out_attn[l], g_out_ssm[l], w_out[l])
        h = modulate(xs, g_norm2[l], sh2, sc2)
        xs = xs + gt2[:, None, :] * conv_ffn(h, w_up[l], conv_w[l], conv_b[l], w_down[l])

    y_prompt = rms_norm(xp, g_final)
    y_sample = rms_norm(xs, g_final)
    return (y_prompt, y_sample, jnp.stack(new_k, axis=1), jnp.stack(new_v, axis=1),
            jnp.stack(new_s_re, axis=1), jnp.stack(new_s_im, axis=1))
```

```python
import numpy as np
from contextlib import ExitStack
import concourse.bass as bass
import concourse.mybir as mybir
from concourse.bass_utils import run_bass_kernel_spmd

F32 = mybir.dt.float32
BF16 = mybir.dt.bfloat16
AF = mybir.ActivationFunctionType
ALU = mybir.AluOpType


class Prog:
    ENGS = ("pe", "act", "dve", "pool", "sp")

    def __init__(self, nc, stack, tag, dma_sems):
        self.nc = nc
        self.tag = tag
        self.ops = {e: [] for e in self.ENGS}
        if "__esem" not in dma_sems:
            dma_sems["__esem"] = {e: stack.enter_context(nc.semaphore(f"eng_{e}")) for e in self.ENGS}
            dma_sems["__ecnt"] = {e: 0 for e in self.ENGS}
        self.esem = dma_sems["__esem"]
        self.ecnt = dma_sems["__ecnt"]
        self.dma_sems = dma_sems
        self.stack = stack
        self.lastw = {}
        self.readers = {}
        self.waited = {}
        self.used_keys = set()
        self.keymap = {}

    def _deps(self, eng, reads, writes):
        toks = []
        relaxed = getattr(self, "relaxed", False)
        for r in reads:
            t = self.lastw.get(r)
            if t is not None:
                toks.append(t)
        for w in writes:
            t = self.lastw.get(w)
            if t is not None:
                toks.append(t)
            toks.extend(self.readers.get(w, ()))
        waits = {}
        for (sem, val, teng) in toks:
            if teng == "pe" and eng == "pe":
                continue
            if relaxed and teng == eng:
                continue
            k = (eng, id(sem))
            if self.waited.get(k, 0) >= val:
                continue
            if waits.get(id(sem), (None, 0))[1] < val:
                waits[id(sem)] = (sem, val)
        for (sem, val) in waits.values():
            self.waited[(eng, id(sem))] = val
        return list(waits.values())

    def _commit(self, tok, reads, writes):
        for r in reads:
            self.readers.setdefault(r, []).append(tok)
        for w in writes:
            self.lastw[w] = tok
            self.readers[w] = []

    def op(self, eng, fn, reads=(), writes=()):
        reads = tuple(reads); writes = tuple(writes)
        waits = self._deps(eng, reads, writes)
        self.ecnt[eng] += 1
        tok = (self.esem[eng], self.ecnt[eng], eng)
        self.ops[eng].append((waits, fn, (self.esem[eng], 1)))
        self._commit(tok, reads, writes)

    def dma(self, eng, out, in_, key, reads=(), writes=(), **kw):
        reads = tuple(reads); writes = tuple(w for w in writes if not (w.endswith("_d") or w.startswith("new_") or w.startswith("y_")))
        waits = self._deps(eng, reads, writes)
        pool = self.dma_sems.setdefault("__pool", [])
        if key not in self.keymap:
            idx = len(self.keymap)
            if idx >= len(pool):
                pool.append([self.stack.enter_context(self.nc.semaphore(f"dq_{idx}")), 0])
            self.keymap[key] = pool[idx]
        ent = self.keymap[key]
        ent[1] += 16
        self.used_keys.add(key)
        tok = (ent[0], ent[1], "dma")
        fn = (lambda e, o=out, i=in_, kw=kw: e.dma_start(out=o, in_=i, **kw))
        self.ops[eng].append((waits, fn, (ent[0], 16)))
        self._commit(tok, reads, writes)

    def mm(self, out, lhsT, rhs, start=True, stop=True, reads=(), writes=(), **kw):
        self.op("pe", lambda e: e.matmul(out, lhsT, rhs, start=start, stop=stop, **kw), reads, writes)

    def tr(self, out, in_, ident, reads=(), writes=()):
        self.op("pe", lambda e: e.transpose(out, in_, ident), reads, writes)

    def act(self, out, in_, func, reads=(), writes=(), eng="act", **kw):
        self.op(eng, lambda e: e.activation(out=out, in_=in_, func=func, **kw), reads, writes)

    def tt(self, out, a, b, op, reads=(), writes=(), eng="dve"):
        self.op(eng, lambda e: e.tensor_tensor(out=out, in0=a, in1=b, op=op), reads, writes)

    def ts(self, out, a, s1, s2, op0, op1=None, reads=(), writes=(), eng="dve"):
        if op1 is None:
            self.op(eng, lambda e: e.tensor_scalar(out=out, in0=a, scalar1=s1, scalar2=None, op0=op0), reads, writes)
        else:
            self.op(eng, lambda e: e.tensor_scalar(out=out, in0=a, scalar1=s1, scalar2=s2, op0=op0, op1=op1), reads, writes)

    def stt(self, out, in0, scalar, in1, op0, op1, reads=(), writes=()):
        self.op("dve", lambda e: e.scalar_tensor_tensor(out=out, in0=in0, scalar=scalar, in1=in1, op0=op0, op1=op1), reads, writes)

    def copy(self, out, in_, reads=(), writes=(), eng="dve"):
        self.op(eng, lambda e: e.tensor_copy(out=out, in_=in_), reads, writes)

    def memset(self, ap, val, writes=(), eng="dve"):
        self.op(eng, lambda e: e.memset(ap, val), (), writes)

    def recip(self, out, in_, reads=(), writes=()):
        self.op("dve", lambda e: e.reciprocal(out=out, in_=in_), reads, writes)

    def emit(self, final_keys=None):
        nc = self.nc
        flush = [(self.keymap[k][0], self.keymap[k][1]) for k in sorted(self.used_keys)]
        with nc.Block() as block:
            def body(engname):
                def f(e):
                    for (waits, fn, inc) in self.ops[engname]:
                        for (sem, val) in waits:
                            e.wait_ge(sem, val)
                        fn(e).then_inc(inc[0], inc[1])
                    if engname == "sp":
                        for (sem, val) in flush:
                            e.wait_ge(sem, val)
                return f
            block.tensor(body("pe"))
            block.scalar(body("act"))
            block.vector(body("dve"))
            block.gpsimd(body("pool"))
            block.sync(body("sp"))

LS = 4096; LP = 1024; NT = LS + LP; D = 1024; EPS = 1e-6
NQT = NT // 128


class Ctx:
    pass


def build(upto=99, debug=()):
    nc = bass.Bass("TRN2", target_bir_lowering=False)
    st = ExitStack()
    dsems = {}
    C = Ctx(); C.nc = nc; C.st = st; C.dsems = dsems

    def din(name, shape, dt=F32):
        return nc.dram_tensor(name, list(shape), dt, kind="ExternalInput").ap()

    def dout(name, shape, dt=F32):
        return nc.dram_tensor(name, list(shape), dt, kind="ExternalOutput").ap()

    def dscr(name, shape, dt=F32):
        kind = "ExternalOutput" if name in debug else "Internal"
        return nc.dram_tensor(name, list(shape), dt, kind=kind).ap()
    C.din, C.dout, C.dscr = din, dout, dscr

    C.x_all = din("x_all", [NT, D])
    C.cpair = din("cpair", [2, D])
    C.w_ada = din("w_ada", [D, 6 * D])
    C.b_ada = din("b_ada", [6 * D])
    C.g_norm1 = din("g_norm1", [D])
    C.ident = din("ident", [128, 128])
    C.mod_d = dscr("mod_d", [2, 6 * D])
    C.h1T_d = dscr("h1T_d", [D, NT], BF16)

    C.w_inx = din("w_inx", [D, NWC])
    C.rope_cos = din("rope_cos", [128, LS]); C.rope_sin = din("rope_sin", [128, LS])
    C.masks = din("masks", [128, 2, 256])
    C.attn_sink = din("attn_sink", [8])
    C.cache_k = din("cache_k", [512, 128]); C.cache_v = din("cache_v", [512, 128])
    C.qT_d = dscr("qT_d", [512, NT], BF16); C.kT_d = dscr("kT_d", [128, NT], BF16)
    C.V_d = dscr("V_d", [NT, 128], BF16); C.uc_d = dscr("uc_d", [NT // 8, 4096])
    C.aT_d = dscr("aT_d", [512, NT], BF16)
    C.new_k = dout("new_k", [LP, 128]); C.new_v = dout("new_v", [LP, 128])
    for nm in ("lamA_re", "lamA_im", "ldtA", "h0A_re", "h0A_im"):
        setattr(C, nm, din(nm, [128, 32]))
    for nm in ("lamB_re", "lamB_im", "ldtB"):
        setattr(C, nm, din(nm, [4096]))
    for nm in ("BpadT_re", "BpadT_im", "CTpad_re", "CTpad_im"):
        setattr(C, nm, din(nm, [128, 32, 128]))
    C.Dcol = din("Dcol", [128, 4]); C.wglu_bd = din("wglu_bd", [128, 4, 128])
    C.sT_d = dscr("sT_d", [512, NT], BF16)
    C.U_d = dscr("U_d", [128, 32, NCH]); C.S_d = dscr("S_d", [128, NCH, 64]); C.Hin_d = dscr("Hin_d", [128, NCH, 64])
    for nm in ("B_A_re", "B_A_im", "CT_A_re", "CT_A_im"):
        setattr(C, nm, din(nm, [128, 512]))
    C.wglu_k = din("wglu_k", [128, 32, 128]); C.maskFB = din("maskFB", [128, 2, 128]); C.Dsp = din("Dsp", [128, 32])
    C.new_sre = dout("new_sre", [4, 2, 32, 64]); C.new_sim = dout("new_sim", [4, 2, 32, 64])
    C.g_norm2 = din("g_norm2", [D]); C.gout_col = din("gout_col", [128, 8]); C.w_outx = din("w_outx", [D, D])
    C.w_up = din("w_up", [D, 5632]); C.w_down = din("w_down", [2816, D])
    C.convw_col = din("convw_col", [128, 3, 44]); C.convb_col = din("convb_col", [128, 44]); C.g_final = din("g_final", [D])
    C.xmid_d = dscr("xmid_d", [NT, D]); C.h2T_d = dscr("h2T_d", [D, H2COLS], BF16)
    C.y_out = dout("y_out", [NT, D])
    phases = [phase0, phase1, phase2, phase4, phase5, phase6]
    for i, ph in enumerate(phases):
        if i > upto:
            break
        ph(C)
    st.close()
    return nc


def mk(nc, ph, tag):
    sb = lambda name, shape, dt=F32: ph.enter_context(nc.sbuf_tensor(f"{tag}_{name}", shape, dt))
    ps = lambda name, shape, dt=F32: ph.enter_context(nc.psum_tensor(f"{tag}_{name}", shape, dt))
    return sb, ps


def phase0(C):
    nc = C.nc
    with ExitStack() as ph:
        P = Prog(nc, C.st, "p0", C.dsems)
        sb, ps = mk(nc, ph, "p0")
        cT = sb("cT", [128, 2, 8]); sT = sb("sT", [128, 2, 8])
        bada = sb("bada", [2, 6 * D]); modrow = sb("modrow", [2, 6 * D])
        wa = [sb(f"wa{i}", [128, 8, 512]) for i in range(2)]
        pm = [ps(f"pm{i}", [128, 512]) for i in range(2)]
        for c in range(2):
            P.dma("sp", cT[:, c, :], C.cpair[c].rearrange("(k p) -> p k", p=128), key="cT", writes=["cT"],
                  allow_slow_non_contiguous=True)
        P.dma("sp", bada[:], C.b_ada.partition_broadcast(2), key="bada", writes=["bada"])
        P.act(sT[:], cT[:], AF.Silu, reads=["cT"], writes=["sT"])
        for n in range(12):
            s = n % 2
            P.dma("sp", wa[s][:], C.w_ada[:, n * 512:(n + 1) * 512].rearrange("(k p) n -> p k n", p=128),
                  key=f"wa{s}", writes=[f"wa{s}"])
            for k in range(8):
                P.mm(pm[s][0:2, :], sT[:, :, k], wa[s][:, k, :], start=(k == 0), stop=(k == 7),
                     reads=["sT", f"wa{s}"], writes=[f"pm{s}"])
            P.tt(modrow[0:2, n * 512:(n + 1) * 512], pm[s][0:2, :], bada[0:2, n * 512:(n + 1) * 512], ALU.add,
                 reads=[f"pm{s}", "bada"], writes=[f"modrow{n}"])
        P.dma("sp", C.mod_d[:, :], modrow[0:2, :], key="modst", reads=[f"modrow{n}" for n in range(12)],
              writes=["mod_d"])
        P.emit()


def load_modcols(C, P, sb, gname_ap, j_shift, j_scale, tag):
    modcol = sb(f"modcol{tag}", [128, 2, 48]); gcol = sb(f"gcol{tag}", [128, 8])
    msc = sb(f"msc{tag}", [128, 2, 8])
    for c in range(2):
        P.dma("sp", modcol[:, c, :], C.mod_d[c].rearrange("(j p) -> p j", p=128), key=f"modcol{tag}{c}",
              writes=[f"modcol{c}"], allow_slow_non_contiguous=True)
    P.dma("sp", gcol[:], gname_ap.rearrange("(k p) -> p k", p=128), key=f"gcol{tag}", writes=["gcol"],
          allow_slow_non_contiguous=True)
    for c in range(2):
        P.ts(msc[:, c, :], modcol[:, c, j_scale:j_scale + 8], 1.0, None, ALU.add, reads=[f"modcol{c}"],
             writes=[f"msc{c}"])
        P.tt(msc[:, c, :], msc[:, c, :], gcol[:], ALU.mult, reads=[f"msc{c}", "gcol"], writes=[f"msc{c}"])
    return msc, modcol


def norm_to_T(C, P, xt, xres, ss, rstd, xn, junk, pT, hT, msc, msh_ap_fn, cond, ident, sfx):
    P.act(junk[:], xt, AF.Square, reads=[xres], writes=["junk" + sfx, "ss" + sfx], accum_out=ss[:])
    P.ts(ss[:], ss[:], 1.0 / D, EPS, ALU.mult, ALU.add, reads=["ss" + sfx], writes=["ss" + sfx])
    P.act(ss[:], ss[:], AF.Sqrt, reads=["ss" + sfx], writes=["ss" + sfx])
    P.recip(rstd[:], ss[:], reads=["ss" + sfx], writes=["rstd" + sfx])
    P.ts(xn[:], xt, rstd[:, 0:1], None, ALU.mult, reads=[xres, "rstd" + sfx], writes=["xn" + sfx])
    for k in range(8):
        P.tr(pT[:, k * 128:(k + 1) * 128], xn[:, k * 128:(k + 1) * 128], ident[:], reads=["xn" + sfx, "ident"],
             writes=[f"pT{sfx}{k // 4}"])
    for k in range(8):
        P.act(hT[:, k, :], pT[:, k * 128:(k + 1) * 128], AF.Identity, reads=[f"pT{sfx}{k // 4}", "msc0", "msc1", "modcol0", "modcol1"],
              writes=[f"hT{sfx}{k}"], scale=msc[:, cond, k:k + 1], bias=msh_ap_fn(cond, k))


def pipeline(n, stages):
    ns = len(stages)
    for step in range(n + ns - 1):
        for k in reversed(range(ns)):
            i = step - k
            if 0 <= i < n:
                stages[k](i)


def phase1(C):
    nc = C.nc
    with ExitStack() as ph:
        P = Prog(nc, C.st, "p1", C.dsems)
        sb, ps = mk(nc, ph, "p1")
        ident = sb("ident", [128, 128])
        P.dma("sp", ident[:], C.ident, key="ident", writes=["ident"])
        msc, modcol = load_modcols(C, P, sb, C.g_norm1, 0, 8, "a")
        NX = 6
        xs = [sb(f"x{i}", [128, D]) for i in range(NX)]
        junk = sb("junk", [128, D]); xn = [sb(f"xn{i}", [128, D]) for i in range(2)]
        ss = [sb(f"ss{i}", [128, 1]) for i in range(8)]; rstd = [sb(f"rstd{i}", [128, 1]) for i in range(8)]
        hT = [sb(f"hT{i}", [128, 8, 512], BF16) for i in range(2)]
        pT = [ps(f"pT{i}", [128, 1024]) for i in range(2)]
        cond_of = lambda ti: 0 if ti < LS // 128 else 1

        def s0(i):
            P.dma("sp", xs[i % NX][:], C.x_all[i * 128:(i + 1) * 128, :], key=f"x{i % NX}", writes=[f"x{i % NX}"])

        def s1(i):
            P.act(junk[:], xs[i % NX][:], AF.Square, reads=[f"x{i % NX}"], writes=["junk", f"ss{i % 8}"], accum_out=ss[i % 8][:])

        def s2(i):
            P.ts(ss[i % 8][:], ss[i % 8][:], 1.0 / D, EPS, ALU.mult, ALU.add, reads=[f"ss{i % 8}"], writes=[f"ss{i % 8}"])

        def s3(i):
            P.act(ss[i % 8][:], ss[i % 8][:], AF.Sqrt, reads=[f"ss{i % 8}"], writes=[f"ss{i % 8}"])

        def s4(i):
            P.recip(rstd[i % 8][:], ss[i % 8][:], reads=[f"ss{i % 8}"], writes=[f"rstd{i % 8}"])
            P.ts(xn[i % 2][:], xs[i % NX][:], rstd[i % 8][:, 0:1], None, ALU.mult, reads=[f"x{i % NX}", f"rstd{i % 8}"],
                 writes=[f"xn{i % 2}"])

        def s5(i):
            for k in range(8):
                P.tr(pT[i % 2][:, k * 128:(k + 1) * 128], xn[i % 2][:, k * 128:(k + 1) * 128], ident[:],
                     reads=[f"xn{i % 2}", "ident"], writes=[f"pT{i % 2}_{k // 4}"])

        def s6(i):
            cond = cond_of(i); hb_ = (i // 4) % 2; h = hT[hb_]; pos = i % 4
            for k in range(8):
                P.act(h[:, k, pos * 128:(pos + 1) * 128], pT[i % 2][:, k * 128:(k + 1) * 128], AF.Identity,
                      reads=[f"pT{i % 2}_{k // 4}", "msc0", "msc1", "modcol0", "modcol1"], writes=[f"hT{hb_}"],
                      scale=msc[:, cond, k:k + 1], bias=modcol[:, cond, k:k + 1])
            if pos == 3:
                P.dma("sp", C.h1T_d[:, (i - 3) * 128:(i + 1) * 128].rearrange("(k p) t -> p k t", p=128), h[:], key=f"hTst{hb_}",
                      reads=[f"hT{hb_}"])
        pipeline(NQT, [s0, s1, s2, s3, s4, s5, s6])
        P.emit()


NWC = 1920


def phase2(C):
    nc = C.nc
    with ExitStack() as ph:
        P = Prog(nc, C.st, "p2", C.dsems)
        sb, ps = mk(nc, ph, "p2")
        w = sb("w", [128, 8, NWC], BF16)
        wst = [sb(f"wst{i}", [128, 960]) for i in range(2)]
        n = 0
        for k in range(8):
            for hh in range(2):
                s = n % 2; n += 1
                P.dma("sp", wst[s][:], C.w_inx[k * 128:(k + 1) * 128, hh * 960:(hh + 1) * 960], key=f"wst{s}",
                      writes=[f"wst{s}"])
                P.copy(w[:, k, hh * 960:(hh + 1) * 960], wst[s][:], reads=[f"wst{s}"], writes=["w"])
        hT = [sb(f"hT{i}", [128, 8, 512], BF16) for i in range(2)]
        cs = [sb(f"cs{i}", [128, 2, 512]) for i in range(2)]
        t1 = sb("t1", [128, 512]); t2 = sb("t2", [128, 512])
        qo = [sb(f"qo{i}", [128, 512], BF16) for i in range(2)]
        uall = [sb(f"uall{i}", [64, 32, 8, 16]) for i in range(2)]
        vo = [sb(f"vo{i}", [128, 128], BF16) for i in range(2)]
        kvo = [sb(f"kvo{i}", [128, 256]) for i in range(2)]
        pa = [ps(f"pa{i}", [128, 512]) for i in range(4)]
        pb = [ps(f"pb{i}", [128, 512]) for i in range(2)]
        ia = 0; ib = 0; iq = 0; iu = 0; iv = 0; ikv = 0
        for b in range(NT // 512):
            s = b % 2; t0 = b * 512; sample = t0 < LS
            P.dma("sp", hT[s][:], C.h1T_d[:, t0:t0 + 512].rearrange("(k p) t -> p k t", p=128), key=f"hTl{s}",
                  writes=[f"hT{s}"])
            if sample:
                P.dma("sp", cs[s][:, 0, :], C.rope_cos[:, t0:t0 + 512], key=f"cs{s}", writes=[f"cs{s}"])
                P.dma("sp", cs[s][:, 1, :], C.rope_sin[:, t0:t0 + 512], key=f"cs{s}", writes=[f"cs{s}"])

            def proj(col0, pt):
                for k in range(8):
                    P.mm(pt[:, :], w[:, k, col0:col0 + 128], hT[s][:, k, :], start=(k == 0), stop=(k == 7),
                         reads=["w", f"hT{s}"], writes=[pt.name])
            import os as _os
            _parts = _os.environ.get("P2PARTS", "quv")
            for j in range(5 if "q" in _parts else 0):
                c0 = j * 128 if j < 4 else 1152; c1 = 512 + j * 128 if j < 4 else 1024
                p0 = pa[ia % 4]; ia += 1
                proj(c0, p0)
                o = qo[iq % 2]; on = f"qo{iq % 2}"; iq += 1
                if sample:
                    p1 = pa[ia % 4]; ia += 1
                    proj(c1, p1)
                    P.tt(t1[:], p0[:, :], cs[s][:, 0, :], ALU.mult, reads=[p0.name, f"cs{s}"], writes=["t1"])
                    P.tt(t2[:], p1[:, :], cs[s][:, 1, :], ALU.mult, reads=[p1.name, f"cs{s}"], writes=["t2"])
                    P.tt(o[:], t1[:], t2[:], ALU.add, reads=["t1", "t2"], writes=[on])
                else:
                    P.act(o[:], p0[:, :], AF.Copy, reads=[p0.name], writes=[on])
                dst = C.qT_d[j * 128:(j + 1) * 128, t0:t0 + 512] if j < 4 else C.kT_d[:, t0:t0 + 512]
                P.dma("sp", dst, o[:], key=f"qst{(iq - 1) % 2}", reads=[on], writes=["qkT_d"])
            ua = uall[b % 2]; uan = f"uall{b % 2}"
            for sx in range(8):
                p0 = pa[ia % 4]; ia += 1
                for k in range(8):
                    P.mm(p0[0:64, :], hT[s][:, k, sx:512:8], w[:, k, 1408:1920], start=(k == 0), stop=(k == 7),
                         reads=["w", f"hT{s}"], writes=[p0.name])
                P.act(ua[0:64, :, sx, :], p0[0:64, :].rearrange("c (g x) -> c g x", g=32), AF.Copy, reads=[p0.name], writes=[uan])
            P.dma("sp", C.uc_d[t0 // 8:t0 // 8 + 64, :], ua[0:64].rearrange("c g s x -> c (g s x)"), key=f"ust{b % 2}",
                  reads=[uan], writes=["uc_d"])
            for i in range(4 if ("v" in _parts and not ("S" in _parts and not sample) and not ("P" in _parts and sample)) else 0):
                tk = t0 + i * 128
                p0 = pb[ib % 2]; ib += 1
                vc = 0 if sample else 128
                for k in range(8):
                    if sample:
                        P.mm(p0[:, 0:128], hT[s][:, k, i * 128:(i + 1) * 128], w[:, k, 1280:1408], start=(k == 0),
                             stop=(k == 7), reads=["w", f"hT{s}"], writes=[p0.name])
                    else:
                        P.mm(p0[:, 0:256], hT[s][:, k, i * 128:(i + 1) * 128], w[:, k, 1152:1408], start=(k == 0),
                             stop=(k == 7), reads=["w", f"hT{s}"], writes=[p0.name])
                o = vo[iv % 2]; on = f"vo{iv % 2}"; iv += 1
                P.act(o[:], p0[:, vc:vc + 128], AF.Copy, reads=[p0.name], writes=[on])
                P.dma("sp", C.V_d[tk:tk + 128, :], o[:], key=f"vst{(iv - 1) % 2}", reads=[on], writes=["V_d"])
                if not sample:
                    o2 = kvo[ikv % 2]; on2 = f"kvo{ikv % 2}"; ikv += 1
                    if "C" not in _parts:
                        P.act(o2[:], p0[:, 0:256], AF.Copy, reads=[p0.name], writes=[on2])
                if (not sample) and "N" not in _parts:
                    P.dma("sp", C.new_k[tk - LS:tk - LS + 128, :], o2[:, 0:128], key=f"kvst{(ikv - 1) % 2}",
                          reads=[on2], writes=["new_k"])
                    P.dma("sp", C.new_v[tk - LS:tk - LS + 128, :], o2[:, 128:256], key=f"kvst{(ikv - 1) % 2}",
                          reads=[on2], writes=["new_v"])
        P.emit()


def attn_ops(C, P, sb, ps):
    if True:
        nc = C.nc
        identf = sb("identf", [128, 128]); identb = sb("identb", [128, 128], BF16); onesb = sb("onesb", [128, 128], BF16)
        P.dma("sp", identf[:], C.ident, key="identf", writes=["identf"])
        P.copy(identb[:], identf[:], reads=["identf"], writes=["identb"])
        P.memset(onesb[:], 1.0, writes=["onesb"])
        maskf = sb("maskf", [128, 2, 256]); maskb = sb("maskb", [128, 2, 256], BF16)
        P.dma("sp", maskf[:], C.masks, key="maskf", writes=["maskf"])
        P.copy(maskb[:], maskf[:], reads=["maskf"], writes=["maskb"])
        es = sb("es", [128, 8])
        P.dma("sp", es[:], C.attn_sink.partition_broadcast(128), key="es", writes=["es"])
        P.act(es[:], es[:], AF.Exp, reads=["es"], writes=["es"])
        ckf = sb("ckf", [128, 4, 128]); cvf = sb("cvf", [128, 4, 128])
        ckT = sb("ckT", [128, 512], BF16); cv = sb("cv", [128, 4, 128], BF16)
        P.dma("sp", ckf[:], C.cache_k.rearrange("(b p) d -> p b d", p=128), key="ckf", writes=["ckf"])
        P.dma("sp", cvf[:], C.cache_v.rearrange("(b p) d -> p b d", p=128), key="cvf", writes=["cvf"])
        P.copy(cv[:], cvf[:], reads=["cvf"], writes=["cv"])
        pSS = [ps(f"pS{i}", [128, 1024]) for i in range(2)]
        pO = [ps(f"pO{i}", [128, 512]) for i in range(2)]; pD = [ps(f"pD{i}", [128, 512]) for i in range(2)]
        for b in range(4):
            P.tr(pSS[0][:, b * 128:(b + 1) * 128], ckf[:, b, :], identf[:], reads=["ckf", "identf"], writes=["pS0"])
        P.act(ckT[:], pSS[0][:, 0:512], AF.Copy, reads=["pS0"], writes=["ckT"])
        kT = sb("kT", [128, LS], BF16); V = sb("V", [128, LS // 128, 128], BF16)
        qT = [sb(f"qT{i}", [128, 4, 128], BF16) for i in range(2)]
        PT = [sb(f"PT{i}", [128, 256], BF16) for i in range(3)]
        rd = [sb(f"rd{i}", [128, 256]) for i in range(2)]; aT = [sb(f"aT{i}", [128, 4, 128], BF16) for i in range(2)]
        seqs = [(0, LS, True)] + [(LS + 256 * i, 256, False) for i in range(4)]
        iq = 0
        items = []
        for (s0, L, sample) in seqs:
            nb = L // 128
            for qb in range(nb):
                if sample:
                    tiles = [("b", kb, (0 if kb < qb else (1 if kb > qb else None))) for kb in (qb - 1, qb, qb + 1)
                             if 0 <= kb < nb] + [("c", kb, None) for kb in range(4)]
                else:
                    tiles = [("b", kb, None) for kb in range(nb)]
                for j in range(4):
                    for ti, tl in enumerate(tiles):
                        items.append(dict(s0=s0, L=L, nb=nb, qb=qb, j=j, ti=ti, nt=len(tiles), tile=tl,
                                          newseq=(qb == 0 and j == 0 and ti == 0), newq=(j == 0 and ti == 0)))
        state = dict(iq=0, ipair=-1)

        def front(n, it):
            if it["newseq"]:
                P.dma("sp", kT[:, 0:it["L"]], C.kT_d[:, it["s0"]:it["s0"] + it["L"]], key="kT", writes=["kT"])
                P.dma("sp", V[:, 0:it["nb"], :], C.V_d[it["s0"]:it["s0"] + it["L"], :].rearrange("(b p) d -> p b d", p=128), key="V",
                      writes=["V"])
            if it["newq"]:
                state["iq"] += 1
                qi = state["iq"] % 2
                tq = it["s0"] + it["qb"] * 128
                P.dma("sp", qT[qi][:], C.qT_d[:, tq:tq + 128].rearrange("(j p) t -> p j t", p=128), key=f"qT{qi}", writes=[f"qT{qi}"])
            if it["ti"] == 0:
                state["ipair"] += 1
            it["qi"] = state["iq"] % 2; it["pi"] = state["ipair"] % 2
            q = qT[it["qi"]]; qn = f"qT{it['qi']}"
            kind, kb, mk_ = it["tile"]; j = it["j"]
            SS = pSS[n % 2]; SSn = f"pS{n % 2}"
            ksrc = kT if kind == "b" else ckT; ksn = "kT" if kind == "b" else "ckT"
            for (c0, r0) in ((0, 0), (512, 64)):
                first = True
                if mk_ is not None:
                    P.mm(SS[:, c0:c0 + 128], identb[:], maskb[:, mk_, 0:128], start=True, stop=False, reads=["identb", "maskb"],
                         writes=[SSn]); first = False
                P.mm(SS[:, c0:c0 + 128], ksrc[r0:r0 + 64, kb * 128:(kb + 1) * 128], q[r0:r0 + 64, j, :], start=first, stop=True,
                     reads=[ksn, qn], writes=[SSn])
            pt = PT[n % 3]
            P.act(pt[:].rearrange("p (a b) -> p a b", a=2), mkap(SS[:, 0:128], [[512, 2], [1, 128]]), AF.Exp, reads=[SSn],
                  writes=[f"PT{n % 3}"], scale=0.125)

        def back(n, it):
            kind, kb, mk_ = it["tile"]; j = it["j"]; pi = it["pi"]
            pt = PT[n % 3]; ptn = f"PT{n % 3}"
            vsrc = V[:, kb, :] if kind == "b" else cv[:, kb, :]; vsn = "V" if kind == "b" else "cv"
            if it["ti"] == 0:
                for dd in [d_ for d_ in deferred if d_[1]["pi"] == pi]:
                    deferred.remove(dd); norm(dd[1])
            P.mm(pO[pi][:, 0:256], vsrc, pt[:], start=(it["ti"] == 0), stop=(it["ti"] == it["nt"] - 1), reads=[vsn, ptn], writes=[f"pO{pi}"])
            P.mm(pD[pi][:, 0:256], onesb[:], pt[:], start=(it["ti"] == 0), stop=(it["ti"] == it["nt"] - 1), reads=["onesb", ptn],
                 writes=[f"pD{pi}"])
            if it["ti"] == it["nt"] - 1:
                deferred.append([4, it])

        deferred = []

        def norm(it):
            if True:
                j = it["j"]; pi = it["pi"]
                a = aT[it["qi"]]; an = f"aT{it['qi']}"; r_ = rd[pi]; rn = f"rd{pi}"
                P.act(r_[:, 0:128], pD[pi][:, 0:128], AF.Ln, reads=[f"pD{pi}", "es"], writes=[rn], bias=es[:, j:j + 1])
                P.act(r_[:, 128:256], pD[pi][:, 128:256], AF.Ln, reads=[f"pD{pi}", "es"], writes=[rn], bias=es[:, 4 + j:5 + j])
                P.act(r_[:], r_[:], AF.Exp, reads=[rn], writes=[rn], scale=-1.0)
                P.tt(a[0:64, j, :], pO[pi][0:64, 0:128], r_[0:64, 0:128], ALU.mult, reads=[f"pO{pi}", rn], writes=[an])
                P.tt(a[64:128, j, :], pO[pi][64:128, 128:256], r_[64:128, 128:256], ALU.mult, reads=[f"pO{pi}", rn], writes=[an])
                if j == 3:
                    tq = it["s0"] + it["qb"] * 128
                    P.dma("sp", C.aT_d[:, tq:tq + 128].rearrange("(j p) t -> p j t", p=128), a[:], key=f"ast{it['qi']}", reads=[an])

        yield
        for n in range(len(items) + 1):
            flush_first = n < len(items) and items[n]["newseq"]
            if n >= 1 and flush_first:
                back(n - 1, items[n - 1])
                while deferred:
                    norm(deferred.pop(0)[1])
            if n < len(items):
                front(n, items[n])
            if n >= 1 and not flush_first:
                back(n - 1, items[n - 1])
            for dd in deferred:
                dd[0] -= 1
            while deferred and (deferred[0][0] <= 0 or n == len(items)):
                norm(deferred.pop(0)[1])
            yield


def _const_tables():
    perm = np.concatenate([np.arange(16, 32), np.arange(0, 16), np.arange(48, 64), np.arange(32, 48)])
    t = np.arange(LS); row = (t // 64).astype(np.float32); col = (t % 64).astype(np.float32)
    inv = (10000.0 ** (-np.arange(16, dtype=np.float32) / 16)).astype(np.float32)
    ar = row[None, :] * inv[:, None]; ac = col[None, :] * inv[:, None]
    cos64 = np.concatenate([np.cos(ar), np.cos(ar), np.cos(ac), np.cos(ac)], 0)
    sin64 = np.concatenate([-np.sin(ar), np.sin(ar), -np.sin(ac), np.sin(ac)], 0)
    cos = np.concatenate([cos64, cos64], 0).astype(np.float32); sin = np.concatenate([sin64, sin64], 0).astype(np.float32)
    j = np.arange(128)[:, None]; i = np.arange(128)[None, :]
    m0 = np.where(j >= i, 0.0, -30000.0); m1 = np.where(j <= i, 0.0, -30000.0)
    masks = np.stack([np.concatenate([m0, m0], 1), np.concatenate([m1, m1], 1)], 1).astype(np.float32)
    return perm, cos, sin, masks


def host_inputs(inp, core):
    b = core % 4
    perm, cos, sin, masks = _const_tables()
    d = {}
    d["x_all"] = np.concatenate([inp["x_sample"][b], inp["x_prompt"][4 * b:4 * b + 4].reshape(LP, D)], 0)
    d["cpair"] = np.stack([inp["c"][b], inp["c_ctx"]], 0)
    d["w_ada"] = inp["w_ada"][0]; d["b_ada"] = inp["b_ada"][0]; d["g_norm1"] = inp["g_norm1"][0]
    d["ident"] = np.eye(128, dtype=np.float32)
    w = inp["w_in"][0]
    q = w[:, 0:512].reshape(D, 8, 64); k = w[:, 512:640].reshape(D, 2, 64)
    qt = np.concatenate([np.concatenate([q[:, j], q[:, 4 + j]], 1) for j in range(4)], 1)
    qp = np.concatenate([np.concatenate([q[:, j][:, perm], q[:, 4 + j][:, perm]], 1) for j in range(4)], 1)
    kt = k.reshape(D, 128); kp = k[:, :, perm].reshape(D, 128)
    d["w_inx"] = np.concatenate([qt, qp, kp, kt, w[:, 640:768], w[:, 768:1280]], 1)
    d["rope_cos"] = cos; d["rope_sin"] = sin; d["masks"] = masks
    d["attn_sink"] = inp["attn_sink"][0]
    d["cache_k"] = inp["cache_k"][b, 0].reshape(512, 128); d["cache_v"] = inp["cache_v"][b, 0].reshape(512, 128)
    tA = lambda a: a.transpose(0, 2, 1).reshape(128, 32)
    tB = lambda a: a.transpose(1, 0, 2).reshape(4096)
    d["lamA_re"] = tA(inp["ssm_lam_re"][0]); d["lamA_im"] = tA(inp["ssm_lam_im"][0])
    ldt = np.repeat(inp["ssm_log_dt"][0][:, :, None], 64, 2)
    d["ldtA"] = tA(ldt); d["lamB_re"] = tB(inp["ssm_lam_re"][0]); d["lamB_im"] = tB(inp["ssm_lam_im"][0]); d["ldtB"] = tB(ldt)
    d["h0A_re"] = tA(inp["state_ssm_re"][b, 0]); d["h0A_im"] = tA(inp["state_ssm_im"][b, 0])
    for nm, src in (("BpadT_re", "ssm_b_re"), ("BpadT_im", "ssm_b_im")):
        B = inp[src][0]
        o = np.zeros((128, 32, 128), np.float32)
        for g in range(32):
            gl = g % 8
            o[gl * 16:(gl + 1) * 16, g, :] = B[:, g].transpose(2, 0, 1).reshape(16, 128)
        d[nm] = o
    for nm, src in (("CTpad_re", "ssm_c_re"), ("CTpad_im", "ssm_c_im")):
        Cm = inp[src][0]
        o = np.zeros((128, 32, 128), np.float32)
        for g in range(32):
            gl = g % 8
            o[:, g, gl * 16:(gl + 1) * 16] = Cm[:, g].transpose(0, 2, 1).reshape(128, 16)
        d[nm] = o
    d["Dcol"] = inp["ssm_d"][0].reshape(4, 128).T
    for nm, src in (("B_A_re", "ssm_b_re"), ("B_A_im", "ssm_b_im")):
        d[nm] = inp[src][0].transpose(0, 2, 1, 3).reshape(128, 512)
    for nm, src in (("CT_A_re", "ssm_c_re"), ("CT_A_im", "ssm_c_im")):
        d[nm] = inp[src][0].transpose(0, 3, 1, 2).reshape(128, 512)
    wk = np.zeros((128, 32, 128), np.float32)
    for s_ in range(8):
        wk[s_ * 16:(s_ + 1) * 16, :, s_ * 16:(s_ + 1) * 16] = inp["ssm_w_glu"][0].transpose(1, 0, 2)
    d["wglu_k"] = wk
    sp = np.arange(128) // 16
    d["maskFB"] = np.stack([(sp[:, None] <= sp[None, :]), (sp[:, None] >= sp[None, :])], 1).astype(np.float32)
    d["Dsp"] = np.tile(inp["ssm_d"][0].T, (8, 1))
    wg = np.zeros((128, 4, 128), np.float32)
    for g in range(32):
        gl = g % 8
        wg[gl * 16:(gl + 1) * 16, g // 8, gl * 16:(gl + 1) * 16] = inp["ssm_w_glu"][0][g]
    d["wglu_bd"] = wg
    d["g_norm2"] = inp["g_norm2"][0]; d["g_final"] = inp["g_final"]
    wo = inp["w_out"][0]
    tp = lambda a: np.concatenate([np.concatenate([a[j * 64:(j + 1) * 64], a[(4 + j) * 64:(5 + j) * 64]], 0) for j in range(4)], 0)
    d["w_outx"] = np.concatenate([tp(wo[0:512]), wo[512:1024]], 0)
    gcat = np.concatenate([tp(inp["g_out_attn"][0]), inp["g_out_ssm"][0]], 0)
    d["gout_col"] = gcat.reshape(8, 128).T
    d["w_up"] = inp["w_up"][0]; d["w_down"] = inp["w_down"][0]
    d["convw_col"] = inp["conv_w"][0].reshape(3, 44, 128).transpose(2, 0, 1)
    d["convb_col"] = inp["conv_b"][0].reshape(44, 128).T
    return {k_: np.ascontiguousarray(v, dtype=np.float32) for k_, v in d.items()}


TB = 64
PI = float(np.pi)


def zoh(P, sb, lre, lim, ldt, n, tag, rd):
    cache = zoh.__dict__.setdefault("cache", {})
    def T(nm):
        k = (id(P.nc), tag, nm)
        if k not in cache:
            cache[k] = sb(f"z{tag}_{nm}", [128, n])
        return cache[k]
    dt = T("dt"); mag = T("mag"); ang = T("ang"); s = T("s"); c = T("c"); are = T("are"); aim = T("aim")
    den = T("den"); t1 = T("t1"); t2 = T("t2"); cre = T("cre"); cim = T("cim"); nre = T("nre")
    R = lambda *x: [f"z{tag}_{i}" for i in x]
    P.act(dt[:], ldt, AF.Exp, reads=rd, writes=R("dt"))
    P.tt(mag[:], lre, dt[:], ALU.mult, reads=rd + R("dt"), writes=R("mag"))
    P.act(mag[:], mag[:], AF.Exp, reads=R("mag"), writes=R("mag"))
    P.tt(ang[:], lim, dt[:], ALU.mult, reads=rd + R("dt"), writes=R("ang"))
    P.act(c[:], ang[:], AF.Sin, reads=R("ang"), writes=R("c"), scale=1.0 / 16)
    P.act(s[:], ang[:], AF.Sin, reads=R("ang"), writes=R("s"), scale=1.0 / 8)
    P.tt(c[:], c[:], c[:], ALU.mult, reads=R("c"), writes=R("c"))
    P.ts(c[:], c[:], -2.0, 1.0, ALU.mult, ALU.add, reads=R("c"), writes=R("c"))
    for _ in range(3):
        P.tt(t1[:], s[:], s[:], ALU.mult, reads=R("s"), writes=R("t1"))
        P.stt(s[:], s[:], 2.0, c[:], ALU.mult, ALU.mult, reads=R("s", "c"), writes=R("s"))
        P.ts(c[:], t1[:], -2.0, 1.0, ALU.mult, ALU.add, reads=R("t1"), writes=R("c"))
    P.tt(are[:], mag[:], c[:], ALU.mult, reads=R("mag", "c"), writes=R("are"))
    P.tt(aim[:], mag[:], s[:], ALU.mult, reads=R("mag", "s"), writes=R("aim"))
    P.tt(den[:], lre, lre, ALU.mult, reads=rd, writes=R("den"))
    P.tt(t1[:], lim, lim, ALU.mult, reads=rd, writes=R("t1"))
    P.tt(den[:], den[:], t1[:], ALU.add, reads=R("den", "t1"), writes=R("den"))
    P.recip(den[:], den[:], reads=R("den"), writes=R("den"))
    P.ts(nre[:], are[:], -1.0, None, ALU.add, reads=R("are"), writes=R("nre"))
    P.tt(t1[:], nre[:], lre, ALU.mult, reads=R("nre") + rd, writes=R("t1"))
    P.tt(t2[:], aim[:], lim, ALU.mult, reads=R("aim") + rd, writes=R("t2"))
    P.tt(t1[:], t1[:], t2[:], ALU.add, reads=R("t1", "t2"), writes=R("t1"))
    P.tt(cre[:], t1[:], den[:], ALU.mult, reads=R("t1", "den"), writes=R("cre"))
    P.tt(t1[:], aim[:], lre, ALU.mult, reads=R("aim") + rd, writes=R("t1"))
    P.tt(t2[:], nre[:], lim, ALU.mult, reads=R("nre") + rd, writes=R("t2"))
    P.tt(t1[:], t1[:], t2[:], ALU.subtract, reads=R("t1", "t2"), writes=R("t1"))
    P.tt(cim[:], t1[:], den[:], ALU.mult, reads=R("t1", "den"), writes=R("cim"))
    return are, aim, cre, cim, R("are", "aim", "cre", "cim")


NCH = NT // 8
CS = 64


def mkap(base, dims):
    return bass.AP(base.tensor, base.offset, [list(base.ap[0])] + [list(d) for d in dims])


def phase4(C):
    nc = C.nc
    with ExitStack() as outer:
        sbo, pso = mk(nc, outer, "q4")
        W = sbo("W", [128, 32, 128]); YS = sbo("YS", [128, 32, 2, 128]); WG = sbo("WG", [128, 32, 128])
        AB = sbo("AB", [128, 128]); h0x = sbo("h0x", [128, 128]); zer = sbo("zer", [128, 128])
        ident = sbo("ident", [128, 128])
        with ExitStack() as mid:
            sbm, _ = mk(nc, mid, "q4m")
            XS = sbm("XS", [128, 32, 2, 128])
            with ExitStack() as ph:
                P = Prog(nc, C.st, "q4a", C.dsems)
                sb, ps = mk(nc, ph, "q4a")
                P.dma("sp", ident[:], C.ident, key="q4ident", writes=["ident"])
                P.dma("sp", WG[:], C.wglu_k, key="WG", writes=["WG"])
                lA = sb("lA", [128, 3, 32])
                for i, nm in enumerate(("lamA_re", "lamA_im", "ldtA")):
                    P.dma("sp", lA[:, i, :], getattr(C, nm), key="lA2", writes=["lA"])
                are, aim, cre, cim, rr = zoh(P, sb, lA[:, 0, :], lA[:, 1, :], lA[:, 2, :], 32, "A2", ["lA"])
                tA = sb("tA", [128, 32]); tB = sb("tB", [128, 32]); rm2 = sb("rm2", [128, 32])

                def cmul(ore, oim, xr, xi, yr, yi, reads, writes, neg_im=False):
                    P.tt(tA[:], xr, yr, ALU.mult, reads=reads, writes=["tA"])
                    P.tt(tB[:], xi, yi, ALU.mult, reads=reads, writes=["tB"])
                    P.tt(ore, tA[:], tB[:], ALU.subtract, reads=["tA", "tB"], writes=writes)
                    P.tt(tA[:], xr, yi, ALU.mult, reads=reads, writes=["tA"])
                    P.tt(tB[:], xi, yr, ALU.mult, reads=reads, writes=["tB"])
                    P.tt(oim, tA[:], tB[:], ALU.add, reads=["tA", "tB"], writes=writes)
                    if neg_im:
                        P.ts(oim, oim, -1.0, None, ALU.mult, reads=writes, writes=writes)
                PW = sb("PW", [128, 9, 2, 32]); PIv = sb("PIv", [128, 8, 2, 32])
                P.memset(PW[:, 0, 0, :], 1.0, writes=["PW"]); P.memset(PW[:, 0, 1, :], 0.0, writes=["PW"])
                P.memset(PIv[:, 0, 0, :], 1.0, writes=["PIv"]); P.memset(PIv[:, 0, 1, :], 0.0, writes=["PIv"])
                P.copy(PW[:, 1, 0, :], are[:], reads=rr, writes=["PW"]); P.copy(PW[:, 1, 1, :], aim[:], reads=rr, writes=["PW"])
                P.tt(rm2[:], are[:], are[:], ALU.mult, reads=rr, writes=["rm2"])
                P.tt(tA[:], aim[:], aim[:], ALU.mult, reads=rr, writes=["tA"])
                P.tt(rm2[:], rm2[:], tA[:], ALU.add, reads=["rm2", "tA"], writes=["rm2"])
                P.recip(rm2[:], rm2[:], reads=["rm2"], writes=["rm2"])
                P.tt(PIv[:, 1, 0, :], are[:], rm2[:], ALU.mult, reads=rr + ["rm2"], writes=["PIv"])
                P.tt(PIv[:, 1, 1, :], aim[:], rm2[:], ALU.mult, reads=rr + ["rm2"], writes=["PIv"])
                P.ts(PIv[:, 1, 1, :], PIv[:, 1, 1, :], -1.0, None, ALU.mult, reads=["PIv"], writes=["PIv"])
                for k in range(2, 9):
                    cmul(PW[:, k, 0, :], PW[:, k, 1, :], PW[:, k - 1, 0, :], PW[:, k - 1, 1, :], PW[:, 1, 0, :], PW[:, 1, 1, :],
                         ["PW"], ["PW"])
                for k in range(2, 8):
                    cmul(PIv[:, k, 0, :], PIv[:, k, 1, :], PIv[:, k - 1, 0, :], PIv[:, k - 1, 1, :], PIv[:, 1, 0, :],
                         PIv[:, 1, 1, :], ["PIv"], ["PIv"])
                P.copy(AB[:, 0:32], PW[:, 8, 0, :], reads=["PW"], writes=["AB"]); P.copy(AB[:, 32:64], PW[:, 8, 0, :], reads=["PW"], writes=["AB"])
                P.ts(AB[:, 64:96], PW[:, 8, 1, :], -1.0, None, ALU.mult, reads=["PW"], writes=["AB"])
                P.copy(AB[:, 96:128], PW[:, 8, 1, :], reads=["PW"], writes=["AB"])
                for blk, nm in ((0, "h0A_re"), (1, "h0A_im"), (2, "h0A_re"), (3, "h0A_im")):
                    P.dma("sp", h0x[:, blk * 32:(blk + 1) * 32], getattr(C, nm), key="h0x", writes=["h0x"])
                P.memset(zer[:], 0.0, writes=["zer"])
                tabs = {}
                for nm, ff, fb in (("PX", lambda s_: (PW, 7 - s_), lambda s_: (PW, s_)),
                                   ("PY", lambda t: (PW, t + 1), lambda t: (PW, 8 - t)),
                                   ("PE", lambda s_: (PIv, s_), lambda s_: (PW, s_)),
                                   ("PF", lambda t: (PW, t), lambda t: (PIv, t))):
                    tb = sb(nm, [128, 8, 2, 32]); tabs[nm] = tb
                    for i in range(8):
                        src, k = ff(i)
                        P.copy(tb[0:64, i], src[0:64, k], reads=["PW", "PIv"], writes=[nm])
                        src, k = fb(i)
                        P.act(tb[64:128, i], src[64:128, k], AF.Copy, reads=["PW", "PIv"], writes=[nm])
                Braw = sb("Braw", [128, 2, 512]); Bb = sb("Bb", [128, 2, 512]); Craw = sb("Craw", [128, 2, 512])
                P.dma("sp", Braw[:, 0, :], C.B_A_re, key="Braw", writes=["Braw"]); P.dma("sp", Braw[:, 1, :], C.B_A_im, key="Braw", writes=["Braw"])
                P.dma("sp", Craw[:, 0, :], C.CT_A_re, key="Craw", writes=["Craw"]); P.dma("sp", Craw[:, 1, :], C.CT_A_im, key="Craw", writes=["Craw"])
                bc = lambda t2: mkap(t2, [[1, 32], [0, 16]])
                g3 = lambda t3: t3.rearrange("p (g x) -> p g x", g=32)
                u1 = sb("u1", [128, 512]); u2 = sb("u2", [128, 512])

                def cmul_b(ore, oim, tr_, ti_, xr, xi, reads, writes, neg_im=False):
                    P.tt(g3(u1[:]), bc(tr_), g3(xr), ALU.mult, reads=reads, writes=["u1"])
                    P.tt(g3(u2[:]), bc(ti_), g3(xi), ALU.mult, reads=reads, writes=["u2"])
                    P.tt(ore, g3(u1[:]), g3(u2[:]), ALU.subtract, reads=["u1", "u2"], writes=writes)
                    P.tt(g3(u1[:]), bc(tr_), g3(xi), ALU.mult, reads=reads, writes=["u1"])
                    P.tt(g3(u2[:]), bc(ti_), g3(xr), ALU.mult, reads=reads, writes=["u2"])
                    P.tt(oim, g3(u1[:]), g3(u2[:]), ALU.add if not neg_im else ALU.add, reads=["u1", "u2"], writes=writes)
                    if neg_im:
                        P.ts(oim, oim, -1.0, None, ALU.mult, reads=writes, writes=writes)
                cmul_b(g3(Bb[:, 0, :]), g3(Bb[:, 1, :]), cre[:], cim[:], Braw[:, 0, :], Braw[:, 1, :], rr + ["Braw"], ["Bb"])
                tmpA = sb("tmpA", [128, 2, 32, 8, 16]); tmpB = sb("tmpB", [128, 2, 32, 8, 16])
                pW = [ps(f"pW{i}", [128, 512]) for i in range(4)]
                for i in range(8):
                    cmul_b(tmpA[:, 0, :, i, :], tmpA[:, 1, :, i, :], tabs["PX"][:, i, 0, :], tabs["PX"][:, i, 1, :], Bb[:, 0, :], Bb[:, 1, :],
                           ["PX", "Bb"], ["tmpA"])
                n = 0
                for g in range(32):
                    for ri in range(2):
                        pw = pW[(n // 4) % 2]; pwn = f"pW{(n // 4) % 2}"
                        P.tr(pw[:, (n % 4) * 128:(n % 4 + 1) * 128], tmpA[:, ri, g].rearrange("p s x -> p (s x)"), ident[:],
                             reads=["tmpA", "ident"], writes=[pwn])
                        n += 1
                        if n % 4 == 0:
                            g0 = g - 1
                            P.act(XS[:, g0:g0 + 2].rearrange("p g r x -> p (g r x)"), pw[:, :], AF.Copy, reads=[pwn], writes=["XS"])
                ysv = YS[:].rearrange("p g r (t q) -> p g r t q", t=8)
                for i in range(8):
                    cmul_b(ysv[:, :, 0, i, :], ysv[:, :, 1, i, :], tabs["PY"][:, i, 0, :], tabs["PY"][:, i, 1, :], Craw[:, 0, :], Craw[:, 1, :],
                           ["PY", "Craw"], ["YS"], neg_im=True)
                for i in range(8):
                    cmul_b(tmpA[:, 0, :, i, :], tmpA[:, 1, :, i, :], tabs["PE"][:, i, 0, :], tabs["PE"][:, i, 1, :], Bb[:, 0, :], Bb[:, 1, :],
                           ["PE", "Bb"], ["tmpA"])
                    cmul_b(tmpB[:, 0, :, i, :], tmpB[:, 1, :, i, :], tabs["PF"][:, i, 0, :], tabs["PF"][:, i, 1, :], Craw[:, 0, :], Craw[:, 1, :],
                           ["PF", "Craw"], ["tmpB"], neg_im=True)
                mk2 = sb("mk2", [128, 2, 128]); Dsp = sb("Dsp", [128, 32]); w1 = sb("w1", [128, 128]); w2 = sb("w2", [128, 128])
                P.dma("sp", mk2[:], C.maskFB, key="mk2", writes=["mk2"]); P.dma("sp", Dsp[:], C.Dsp, key="Dsp", writes=["Dsp"])
                fl = lambda t5, ri, g, r0: t5[r0:r0 + 64, ri, g].rearrange("p s x -> p (s x)")
                for g in range(32):
                    for d, pw, pwn in ((0, pW[2], "pW2"), (1, pW[3], "pW3")):
                        r0 = d * 64
                        P.mm(pw[:, 0:128], fl(tmpA, 0, g, r0), fl(tmpB, 0, g, r0), start=True, stop=False,
                             reads=["tmpA", "tmpB"], writes=[pwn])
                        P.mm(pw[:, 0:128], fl(tmpA, 1, g, r0), fl(tmpB, 1, g, r0), start=False, stop=True,
                             reads=["tmpA", "tmpB"], writes=[pwn])
                    P.tt(w1[:], pW[2][:, 0:128], mk2[:, 0, :], ALU.mult, reads=["pW2", "mk2"], writes=["w1"])
                    P.tt(w2[:], pW[3][:, 0:128], mk2[:, 1, :], ALU.mult, reads=["pW3", "mk2"], writes=["w2"])
                    P.tt(w1[:], w1[:], w2[:], ALU.add, reads=["w1", "w2"], writes=["w1"])
                    P.stt(W[:, g, :], ident[:], Dsp[:, g:g + 1], w1[:], ALU.mult, ALU.add, reads=["ident", "Dsp", "w1"], writes=["W"])
                P.emit()
            import os as _os
            _p4 = int(_os.environ.get("P4UPTO", "9"))
            if _p4 < 1:
                return
            with ExitStack() as ph:
                P = Prog(nc, C.st, "q4b", C.dsems)
                sb, ps = mk(nc, ph, "q4b")
                ucs = [sb(f"ucs{i}", [128, 32, 128]) for i in range(2)]
                Ust = [sb(f"Ust{i}", [128, 32, 128]) for i in range(2)]
                Sst = [sb("Sst0", [128, 128, 2, 32])] * 2
                pU = [ps(f"pU{i}", [128, 512]) for i in range(2)]; pS_ = [ps(f"pS{i}", [128, 512]) for i in range(2)]
                for st_ in range(NCH // 128):
                    b = st_ % 2; c0 = st_ * 128
                    P.dma("sp", ucs[b][:], C.uc_d[c0:c0 + 128, :].rearrange("c (g x) -> c g x", g=32), key=f"ucs{b}", writes=[f"ucs{b}"])
                    for g in range(32):
                        pu = pU[(g // 4) % 2]; pun = f"pU{(g // 4) % 2}"
                        P.tr(pu[:, (g % 4) * 128:(g % 4 + 1) * 128], ucs[b][:, g, :], ident[:],
                             reads=[f"ucs{b}", "ident"], writes=[pun])
                        if g % 4 == 3:
                            P.act(Ust[b][:, g - 3:g + 1, :].rearrange("p g c -> p (g c)"), pu[:, :], AF.Copy, reads=[pun],
                                  writes=[f"Ust{b}"])
                    P.dma("sp", C.U_d[:, :, c0:c0 + 128], Ust[b][:], key=f"Ust{b}", reads=[f"Ust{b}"])
                    for g in range(32):
                        p_ = pS_[g % 2]; pn = f"pS{g % 2}"
                        for ri in range(2):
                            P.mm(p_[:, ri * 128:(ri + 1) * 128], XS[:, g, ri, :], Ust[b][:, g, :], start=True, stop=True,
                                 reads=["XS", f"Ust{b}"], writes=[pn])
                        P.act(Sst[b][:, :, :, g], p_[:, 0:256].rearrange("p (r c) -> p c r", r=2), AF.Copy, reads=[pn],
                              writes=["Sst"])
                    P.dma("sp", C.S_d[:, c0:c0 + 128, :], Sst[b][:].rearrange("p c r g -> p c (r g)"), key="Sst",
                          reads=["Sst"])
                P.emit()
        if _p4 < 2:
            return
        with ExitStack() as ph:
            sb, ps = mk(nc, ph, "q4c")
            Hs = [sb(f"Hs{i}", [128, 66, 128]) for i in range(2)]
            Sb = [sb(f"Sb{i}", [128, CS, 64]) for i in range(2)]
            mt = [sb(f"mt{d}", [128, 2, 64]) for d in range(2)]; sm = [sb(f"sm{d}", [128, 64]) for d in range(2)]
            fin = sb("fin", [128, 64])
            nS = (LS // 8) // CS
            stages = [("s", i * CS, (nS - 1 - i) * CS, i) for i in range(nS)]
            stages += [("p", LS // 8 + i * CS, LS // 8 + i * CS, i) for i in range(LP // 8 // CS)]
            P = Prog(nc, C.st, "q4c", C.dsems)
            sb3, ps3 = mk(nc, ph, "p3")
            gen_attn = attn_ops(C, P, sb3, ps3)

            def chain_ops():
              for si, (kind, cf, cb_, idx) in enumerate(stages):
                yield from chain_stage(si, kind, cf, cb_, idx)

            def chain_stage(si, kind, cf, cb_, idx):
                hb = si % 2; H = Hs[hb]; Hp = Hs[1 - hb]; S_ = Sb[hb]
                P.dma("sp", S_[0:64], C.S_d[0:64, cf:cf + CS, :], key=f"Sb{hb}f", writes=[f"Sb{hb}_0"])
                P.dma("sp", S_[64:128], C.S_d[64:128, cb_:cb_ + CS, :], key=f"Sb{hb}b", writes=[f"Sb{hb}_1"])
                engs = ("dve", "pool")
                nseq = 1 if kind == "s" else 2
                L_ = CS // nseq
                for k in range(nseq):
                    base = 33 * k if kind == "p" else 0
                    for d in range(2):
                        rs = slice(d * 64, (d + 1) * 64); hn = f"Hs{hb}_{d}"
                        slot0 = base if d == 0 else base + L_
                        if kind == "s" and idx > 0:
                            src = Hp[rs, CS if d == 0 else 0]; srn = f"Hs{1 - hb}_{d}"
                        elif kind == "s":
                            src = h0x[rs]; srn = "h0x"
                        else:
                            src = zer[rs]; srn = "zer"
                        P.copy(H[rs, slot0], src, reads=[srn], writes=[hn], eng=engs[d])
                    import os as _os2
                    P.relaxed = bool(_os2.environ.get("CHAIN_RELAXED"))
                    for j_ in range(L_):
                        for opi in range(3):
                            for d in range(2):
                                rs = slice(d * 64, (d + 1) * 64); hn = f"Hs{hb}_{d}"
                                eng = engs[d]
                                m = j_ if d == 0 else L_ - 1 - j_
                                sl_in = base + m if d == 0 else base + m + 1
                                sl_out = base + m + 1 if d == 0 else base + m
                                cl = k * L_ + m
                                if opi == 0:
                                    prev = H[rs, sl_in, 0:64]
                                    P.tt(mt[d][rs], mkap(AB[rs, 0:64], [[64, 2], [1, 64]]), mkap(prev, [[32, 2], [1, 64]]), ALU.mult,
                                         reads=["AB", hn], writes=[f"mt{d}"], eng=eng)
                                elif opi == 1:
                                    P.tt(sm[d][rs], mt[d][rs, 0, :], mt[d][rs, 1, :], ALU.add, reads=[f"mt{d}"], writes=[f"sm{d}"], eng=eng)
                                else:
                                    P.tt(H[rs, sl_out].rearrange("p (a b) -> p a b", a=2), mkap(sm[d][rs, :], [[0, 2], [1, 64]]),
                                         mkap(S_[rs, cl, :], [[0, 2], [1, 64]]), ALU.add, reads=[f"sm{d}", f"Sb{hb}_{d}"], writes=[hn], eng=eng)
                        yield
                    P.relaxed = False
                    for d in range(2):
                        rs = slice(d * 64, (d + 1) * 64); hn = f"Hs{hb}_{d}"
                        if kind == "p":
                            seq = idx * 2 + k
                            slf = base + L_ if d == 0 else base
                            P.copy(fin[rs], H[rs, slf, 0:64], reads=[hn], writes=[f"fin{d}"], eng=engs[d])
                            for ri, dst in ((0, C.new_sre), (1, C.new_sim)):
                                P.dma("sp", dst[seq, d].rearrange("g n -> n g"), fin[rs, ri * 32:(ri + 1) * 32],
                                      key=f"finst{d}", reads=[f"fin{d}"], allow_slow_non_contiguous=True)
                        cg0 = (cf if d == 0 else cb_) + k * L_
                        sl0 = base if d == 0 else base + 1
                        P.dma("sp", C.Hin_d[rs, cg0:cg0 + L_, :], H[rs, sl0:sl0 + L_, 0:64], key=f"Hin{hb}{d}", reads=[hn])

            gen_chain = chain_ops()
            import os as _os3
            alive = [not _os3.environ.get("NOATTN"), not _os3.environ.get("NOCHAIN")]
            n_attn = 0
            while alive[0] or alive[1]:
                for _ in range(3):
                    if alive[0]:
                        try:
                            next(gen_attn)
                        except StopIteration:
                            alive[0] = False
                for _ in range(2):
                    if alive[1]:
                        try:
                            next(gen_chain)
                        except StopIteration:
                            alive[1] = False
            P.emit()
        if _p4 < 3:
            return
        with ExitStack() as ph:
            P = Prog(nc, C.st, "q4d", C.dsems)
            sb, ps = mk(nc, ph, "q4d")
            Ust = [sb(f"Ust{i}", [128, 32, 128]) for i in range(2)]
            Hin = [sb(f"Hin{i}", [128, 128, 64]) for i in range(2)]
            xg = [sb(f"xg{i}", [128, 512]) for i in range(3)]; sg = [sb(f"sg{i}", [128, 512]) for i in range(2)]
            og = [sb(f"og{i}", [128, 512]) for i in range(2)]
            tm = sb("tm", [128, 8, 512]); sTs = [sb("sTs0", [128, 4, 1024], BF16)] * 2
            pYy = [ps(f"pY{i}", [128, 512]) for i in range(2)]; pZ = [ps(f"pZ{i}", [128, 512]) for i in range(2)]
            pR = [ps(f"pR{i}", [128, 512]) for i in range(2)]; pQ = [ps(f"pQ{i}", [128, 512]) for i in range(2)]
            NST = NCH // 128

            def ld(it):
                st_, gq = divmod(it, 8)
                if gq == 0:
                    b = st_ % 2; c0 = st_ * 128
                    P.dma("sp", Ust[b][:], C.U_d[:, :, c0:c0 + 128], key=f"dUst{b}", writes=[f"Ust{b}"])
                    P.dma("sp", Hin[b][:], C.Hin_d[:, c0:c0 + 128, :], key=f"dHin{b}", writes=[f"Hin{b}"])

            def sA(it):
                st_, gq = divmod(it, 8); b = st_ % 2; x = it % 2; py = pYy[x]
                for gi in range(4):
                    g = gq * 4 + gi; cs_ = slice(gi * 128, (gi + 1) * 128)
                    P.mm(py[:, cs_], W[:, g, :], Ust[b][:, g, :], start=True, stop=False, reads=["W", f"Ust{b}"], writes=[f"pY{x}"])
                    P.mm(py[:, cs_], YS[:, g, 0, :], Hin[b][:, :, g], start=False, stop=False, reads=["YS", f"Hin{b}"], writes=[f"pY{x}"])
                    P.mm(py[:, cs_], YS[:, g, 1, :], Hin[b][:, :, 32 + g], start=False, stop=True, reads=["YS", f"Hin{b}"], writes=[f"pY{x}"])

            def sB(it):
                x = it % 2
                P.act(xg[it % 3][:], pYy[x][:, :], AF.Gelu, reads=[f"pY{x}"], writes=[f"xg{it % 3}"])

            def sC(it):
                st_, gq = divmod(it, 8); x = it % 2
                for gi in range(4):
                    g = gq * 4 + gi; cs_ = slice(gi * 128, (gi + 1) * 128)
                    P.mm(pZ[x][:, cs_], WG[:, g, :], xg[it % 3][:, cs_], start=True, stop=True, reads=["WG", f"xg{it % 3}"], writes=[f"pZ{x}"])

            def sD(it):
                x = it % 2
                P.act(sg[x][:], pZ[x][:, :], AF.Sigmoid, reads=[f"pZ{x}"], writes=[f"sg{x}"])

            def sE(it):
                x = it % 2
                P.tt(og[x][:], xg[it % 3][:], sg[x][:], ALU.mult, reads=[f"xg{it % 3}", f"sg{x}"], writes=[f"og{x}"])

            def sF(it):
                x = it % 2
                for gi in range(4):
                    cs_ = slice(gi * 128, (gi + 1) * 128)
                    P.tr(pR[x][:, cs_], og[x][:, cs_], ident[:], reads=[f"og{x}", "ident"], writes=[f"pR{x}"])

            def sG(it):
                st_, gq = divmod(it, 8); x = it % 2; b = st_ % 2; c0 = st_ * 128
                for gi in range(4):
                    g = gq * 4 + gi
                    P.ts(tm[:, :, g * 16:(g + 1) * 16], pR[x][:, gi * 128:(gi + 1) * 128].rearrange("c (t q) -> c t q", t=8), 1.0, None,
                         ALU.mult, reads=[f"pR{x}"], writes=["tm"])
                if gq == 7:
                    n = 0
                    for t in range(8):
                        for ct in range(4):
                            pq = pQ[n % 2]; n += 1
                            P.tr(pq[:, 0:128], tm[:, t, ct * 128:(ct + 1) * 128], ident[:], reads=["tm", "ident"], writes=[f"pQ{(n - 1) % 2}"])
                            P.act(sTs[b][:, ct, t:1024:8], pq[:, 0:128], AF.Copy, reads=[f"pQ{(n - 1) % 2}"], writes=["sTs"])
                    P.dma("sp", C.sT_d[:, c0 * 8:c0 * 8 + 1024].rearrange("(c p) t -> p c t", p=128), sTs[b][:], key="sTs",
                          reads=["sTs"])
            pipeline(NST * 8, [ld, sA, sB, sC, sD, sE, sF, sG])
            P.emit()


def phase4_old(C):
    nc = C.nc
    with ExitStack() as outer:
        sbo, pso = mk(nc, outer, "p4")
        BTm = sbo("BTm", [128, 32, 2, 128]); CTm = sbo("CTm", [128, 32, 2, 128])
        AA = sbo("AA", [128, 2, 32]); BB = sbo("BB", [128, 2, 32]); h0 = sbo("h0", [128, 3, 32])
        zero3 = sbo("zero3", [128, 3, 32]); Dcol = sbo("Dcol", [128, 4])
        with ExitStack() as ph:
            P = Prog(nc, C.st, "p4a", C.dsems)
            sb, ps = mk(nc, ph, "p4a")
            lA = sb("lA", [128, 3, 32])
            for i, nm in enumerate(("lamA_re", "lamA_im", "ldtA")):
                P.dma("sp", lA[:, i, :], getattr(C, nm), key="lA", writes=["lA"])
            are, aim, _, _, rr = zoh(P, sb, lA[:, 0, :], lA[:, 1, :], lA[:, 2, :], 32, "A", ["lA"])
            P.copy(AA[:, 0, :], are[:], reads=rr, writes=["AA"]); P.copy(AA[:, 1, :], are[:], reads=rr, writes=["AA"])
            P.ts(BB[:, 0, :], aim[:], -1.0, None, ALU.mult, reads=rr, writes=["BB"])
            P.copy(BB[:, 1, :], aim[:], reads=rr, writes=["BB"])
            P.dma("sp", h0[:, 0, :], C.h0A_re, key="h0", writes=["h0"])
            P.dma("sp", h0[:, 1, :], C.h0A_im, key="h0", writes=["h0"])
            P.dma("sp", h0[:, 2, :], C.h0A_re, key="h0", writes=["h0"])
            P.memset(zero3[:], 0.0, writes=["zero3"])
            P.dma("sp", Dcol[:], C.Dcol, key="Dcol", writes=["Dcol"])
            P.dma("sp", CTm[:, :, 0, :], C.CTpad_re, key="CTm", writes=["CTm"])
            P.dma("sp", CTm[:, :, 1, :], C.CTpad_im, key="CTm", writes=["CTm"])
            P.ts(CTm[:, :, 1, :], CTm[:, :, 1, :], -1.0, None, ALU.mult, reads=["CTm"], writes=["CTm"])
            lB = sb("lB", [128, 3, 1024]); Bp = sb("Bp", [128, 2, 8, 128]); u1 = sb("u1", [128, 1024]); u2 = sb("u2", [128, 1024])
            for gt in range(4):
                for i, nm in enumerate(("lamB_re", "lamB_im", "ldtB")):
                    P.dma("sp", lB[:, i, :], getattr(C, nm)[gt * 1024:(gt + 1) * 1024].partition_broadcast(128), key="lB",
                          writes=["lB"])
                P.dma("sp", Bp[:, 0], C.BpadT_re[:, gt * 8:(gt + 1) * 8, :], key="Bp", writes=["Bp"])
                P.dma("sp", Bp[:, 1], C.BpadT_im[:, gt * 8:(gt + 1) * 8, :], key="Bp", writes=["Bp"])
                _, _, cre, cim, rr = zoh(P, sb, lB[:, 0, :], lB[:, 1, :], lB[:, 2, :], 1024, "B", ["lB"])
                bre = Bp[:, 0].rearrange("p g n -> p (g n)"); bim = Bp[:, 1].rearrange("p g n -> p (g n)")
                o_re = BTm[:, gt * 8:(gt + 1) * 8, 0, :]; o_im = BTm[:, gt * 8:(gt + 1) * 8, 1, :]
                c3 = lambda t: t[:].rearrange("p (g n) -> p g n", g=8)
                P.tt(u1[:], cre[:], bre, ALU.mult, reads=rr + ["Bp"], writes=["u1"])
                P.tt(u2[:], cim[:], bim, ALU.mult, reads=rr + ["Bp"], writes=["u2"])
                P.tt(o_re, c3(u1), c3(u2), ALU.subtract, reads=["u1", "u2"], writes=["BTm"])
                P.tt(u1[:], cre[:], bim, ALU.mult, reads=rr + ["Bp"], writes=["u1"])
                P.tt(u2[:], cim[:], bre, ALU.mult, reads=rr + ["Bp"], writes=["u2"])
                P.tt(o_im, c3(u1), c3(u2), ALU.add, reads=["u1", "u2"], writes=["BTm"])
            P.emit()
        H = [sbo(f"H{i}", [128, TB, 3, 32]) for i in range(2)]
        Bu = [sbo(f"Bu{i}", [128, TB, 2, 32]) for i in range(2)]
        uT = [[sbo(f"uT{d}{i}", [128, 4, TB]) for i in range(2)] for d in range(2)]
        yo = [[sbo(f"yo{d}{i}", [128, 4, TB]) for i in range(2)] for d in range(2)]
        m1 = [sbo(f"m1{d}", [128, 2, 32]) for d in range(2)]; m2 = [sbo(f"m2{d}", [128, 2, 32]) for d in range(2)]
        fin = sbo("fin", [128, 2, 32])
        pB = [pso(f"pB{i}", [128, 512]) for i in range(2)]
        pYd = [[pso(f"pY{d}{i}", [128, 512]) for i in range(3)] for d in range(2)]
        seqs = [(0, LS, True, -1)] + [(LS + 256 * i, 256, False, i) for i in range(4)]
        stage = 0
        for (s0, L, sample, pi) in seqs:
            nb = L // TB
            BPP = 16
            for b0 in range(0, nb, BPP):
                P = Prog(nc, C.st, f"p4s{s0}_{b0}", C.dsems)
                for bi in range(b0, min(nb, b0 + BPP)):
                    hb = stage % 2; stage += 1
                    Hc = H[hb]; Hp = H[1 - hb]
                    blk = (bi, nb - 1 - bi)
                    for d in range(2):
                        t0 = s0 + blk[d] * TB
                        P.dma("sp", uT[d][hb][:], C.uT_d[:, t0:t0 + TB].rearrange("(c p) t -> p c t", p=128),
                              key=f"uT{d}{hb}", writes=[f"uT{d}{hb}"])
                    for g in range(32):
                        pb = pB[g % 2]
                        for ri in range(2):
                            for d in range(2):
                                P.mm(pb[d * 64:(d + 1) * 64, ri * TB:(ri + 1) * TB], BTm[:, g, ri, d * 64:(d + 1) * 64],
                                     uT[d][hb][:, g // 8, :], start=True, stop=True, reads=["BTm", f"uT{d}{hb}"],
                                     writes=[f"pB{g % 2}"])
                        P.act(Bu[hb][:, :, :, g], pb[:, 0:2 * TB].rearrange("p (r t) -> p t r", r=2), AF.Copy,
                              reads=[f"pB{g % 2}"], writes=[f"Bu{hb}"])
                    for d, eng in ((0, "dve"), (1, "pool")):
                        rs = slice(d * 64, (d + 1) * 64)
                        for st_ in range(TB):
                            t = st_ if d == 0 else TB - 1 - st_
                            if st_ == 0:
                                if bi == 0:
                                    prev = (h0 if sample else zero3)[rs]; pn = "h0"
                                else:
                                    prev = Hp[rs, TB - 1 if d == 0 else 0]; pn = f"H{1 - hb}_{d}"
                            else:
                                prev = Hc[rs, t - 1 if d == 0 else t + 1]; pn = f"H{hb}_{d}"
                            hn = f"H{hb}_{d}"
                            P.tt(m1[d][rs], AA[rs], prev[:, 0:2, :], ALU.mult, reads=["AA", pn, "zero3"], writes=[f"m1{d}"], eng=eng)
                            P.tt(m2[d][rs], BB[rs], prev[:, 1:3, :], ALU.mult, reads=["BB", pn, "zero3"], writes=[f"m2{d}"], eng=eng)
                            P.tt(m1[d][rs], m1[d][rs], m2[d][rs], ALU.add, reads=[f"m1{d}", f"m2{d}"], writes=[f"m1{d}"], eng=eng)
                            P.tt(Hc[rs, t, 0:2, :], m1[d][rs], Bu[hb][rs, t], ALU.add, reads=[f"m1{d}", f"Bu{hb}"], writes=[hn], eng=eng)
                            P.copy(Hc[rs, t, 2, :], Hc[rs, t, 0, :], reads=[hn], writes=[hn], eng=eng)
                    for d in range(2):
                        rs = slice(d * 64, (d + 1) * 64)
                        t0 = s0 + blk[d] * TB
                        for ct in range(4):
                            py = pYd[d][ct % 3]; pyn = f"pY{d}{ct % 3}"
                            n = 0
                            for gl in range(8):
                                for ri in range(2):
                                    g = ct * 8 + gl
                                    P.mm(py[:, d * TB:(d + 1) * TB], CTm[rs, g, ri, :], Hc[rs, :, ri, g], start=(n == 0),
                                         stop=(n == 15), reads=["CTm", f"H{hb}_{d}"], writes=[pyn])
                                    n += 1
                            if d == 0:
                                P.stt(yo[d][hb][:, ct, :], uT[d][hb][:, ct, :], Dcol[:, ct:ct + 1], py[:, 0:TB], ALU.mult, ALU.add,
                                      reads=[f"uT{d}{hb}", "Dcol", pyn], writes=[f"yo{d}{hb}"])
                            else:
                                P.ts(yo[d][hb][:, ct, :], py[:, TB:2 * TB], 1.0, None, ALU.mult, reads=[pyn], writes=[f"yo{d}{hb}"])
                        dst = (C.yf_d if d == 0 else C.yb_d)[:, t0:t0 + TB].rearrange("(c p) t -> p c t", p=128)
                        P.dma("sp", dst, yo[d][hb][:], key=f"yst{d}{hb}", reads=[f"yo{d}{hb}"], writes=["y_d"])
                    if (not sample) and bi == nb - 1:
                        P.copy(fin[0:64], Hc[0:64, TB - 1, 0:2, :], reads=[f"H{hb}_0"], writes=["fin"])
                        P.copy(fin[64:128], Hc[64:128, 0, 0:2, :], reads=[f"H{hb}_1"], writes=["fin"], eng="pool")
                        for d in range(2):
                            for ri, dst in ((0, C.new_sre), (1, C.new_sim)):
                                P.dma("sp", dst[pi, d].rearrange("g n -> n g"), fin[d * 64:(d + 1) * 64, ri, :],
                                      key="finst", reads=["fin"], writes=["new_s"], allow_slow_non_contiguous=True)
                P.emit()
        with ExitStack() as ph:
            P = Prog(nc, C.st, "p4c", C.dsems)
            sb, ps = mk(nc, ph, "p4c")
            wg = sb("wg", [128, 4, 128])
            P.dma("sp", wg[:], C.wglu_bd, key="wg", writes=["wg"])
            yf = [sb(f"yf{i}", [128, 4, 512]) for i in range(2)]; yb = [sb(f"yb{i}", [128, 4, 512]) for i in range(2)]
            sg = sb("sg", [128, 512]); so = [sb(f"so{i}", [128, 4, 512], BF16) for i in range(2)]
            pz = pB[0:2]
            for b in range(NT // 512):
                s = b % 2; t0 = b * 512
                P.dma("sp", yf[s][:], C.yf_d[:, t0:t0 + 512].rearrange("(c p) t -> p c t", p=128), key=f"yf{s}", writes=[f"yf{s}"])
                P.dma("sp", yb[s][:], C.yb_d[:, t0:t0 + 512].rearrange("(c p) t -> p c t", p=128), key=f"yb{s}", writes=[f"yb{s}"])
                P.tt(yf[s][:], yf[s][:], yb[s][:], ALU.add, reads=[f"yf{s}", f"yb{s}"], writes=[f"yf{s}"])
                P.act(yf[s][:], yf[s][:], AF.Gelu, reads=[f"yf{s}"], writes=[f"yf{s}"])
                for ct in range(4):
                    P.mm(pz[ct % 2][:, :], wg[:, ct, :], yf[s][:, ct, :], start=True, stop=True, reads=["wg", f"yf{s}"],
                         writes=[f"pz{ct % 2}"])
                    P.act(sg[:], pz[ct % 2][:, :], AF.Sigmoid, reads=[f"pz{ct % 2}"], writes=["sg"])
                    P.tt(so[s][:, ct, :], yf[s][:, ct, :], sg[:], ALU.mult, reads=[f"yf{s}", "sg"], writes=[f"so{s}"])
                P.dma("sp", C.sT_d[:, t0:t0 + 512].rearrange("(c p) t -> p c t", p=128), so[s][:], key=f"sost{s}",
                      reads=[f"so{s}"], writes=["sT_d"])
            P.emit()


H2COLS = LS + 2 + 4 * 258


def h2col(tok):
    if tok < LS:
        return 1 + tok
    i, r = divmod(tok - LS, 256)
    return LS + 2 + i * 258 + 1 + r


def phase5(C):
    nc = C.nc
    with ExitStack() as ph:
        P = Prog(nc, C.st, "p5", C.dsems)
        sb, ps = mk(nc, ph, "p5")
        ident = sb("ident", [128, 128]); onesb = sb("onesb", [128, 2], BF16)
        P.dma("sp", ident[:], C.ident, key="ident5", writes=["ident"])
        P.memset(onesb[:], 1.0, writes=["onesb"])
        msc, modcol = load_modcols(C, P, sb, C.g_norm2, 24, 32, "b")
        wo = sb("wo", [128, 8, D], BF16); stg = [sb(f"stg{i}", [128, D]) for i in range(2)]
        gcol = sb("gocol", [128, 8])
        P.dma("sp", gcol[:], C.gout_col, key="gocol", writes=["gocol"])
        for kt in range(8):
            s = kt % 2
            P.dma("sp", stg[s][:], C.w_outx[kt * 128:(kt + 1) * 128, :], key=f"stg{s}", writes=[f"stg{s}"])
            P.ts(wo[:, kt, :], stg[s][:], gcol[:, kt:kt + 1], None, ALU.mult, reads=[f"stg{s}", "gocol"], writes=["wo"])
        g1 = sb("g1", [128, 2, D])
        for c in range(2):
            P.dma("sp", g1[:, c, :], C.mod_d[c, 2048:3072].partition_broadcast(128), key="g1", writes=["g1"])
        zc = sb("zc", [128, 8, 1], BF16)
        P.memset(zc[:], 0.0, writes=["zc"])
        pads = [0, LS + 1] + [LS + 2 + i * 258 for i in range(4)] + [LS + 2 + i * 258 + 257 for i in range(4)]
        for pc in pads:
            P.dma("sp", C.h2T_d[:, pc:pc + 1].rearrange("(k p) t -> p k t", p=128), zc[:], key="zc", reads=["zc"],
                  allow_slow_non_contiguous=True)
        NA = 3
        am = [sb(f"am{i}", [128, 8, 512], BF16) for i in range(NA)]
        sq = [sb(f"sq{i}", [128, 8, 128], BF16) for i in range(2)]
        xs = [sb(f"x{i}", [128, D]) for i in range(3)]
        t1 = [sb(f"t1{i}", [128, D]) for i in range(2)]; t2 = [sb(f"t2{i}", [128, D]) for i in range(2)]
        NM = 6
        xm = [sb(f"xm{i}", [128, D]) for i in range(NM)]
        junk = sb("junk", [128, D]); xn = [sb(f"xn{i}", [128, D]) for i in range(2)]
        r2 = [sb(f"r2{i}", [128, 4]) for i in range(8)]; ss = [sb(f"ss{i}", [128, 1]) for i in range(8)]
        rstd = [sb(f"rstd{i}", [128, 1]) for i in range(8)]
        hT = [sb(f"hT{i}", [128, 8, 512], BF16) for i in range(2)]
        pA = ps("pA", [128, 1024]); pS = ps("pS", [128, 1024]); pq = ps("pq", [128, 512]); pT = ps("pT", [128, 1024])

        def grp(i):
            if i < LS // 128:
                return i // 4, i % 4, 4
            return LS // 512 + (i - LS // 128) // 2, (i - LS // 128) % 2, 2
        cond_of = lambda ti: 0 if ti < LS // 128 else 1

        def s0(i):
            tk = i * 128; a_ = am[(i // 4) % NA]; an = f"am{(i // 4) % NA}"; pos = i % 4
            if pos == 0:
                P.dma("sp", a_[:, 0:4, :], C.aT_d[:, tk:tk + 512].rearrange("(j p) t -> p j t", p=128), key=an, writes=[an])
                P.dma("sp", a_[:, 4:8, :], C.sT_d[:, tk:tk + 512].rearrange("(j p) t -> p j t", p=128), key=an, writes=[an])
            P.act(sq[i % 2][:], a_[:, :, pos * 128:(pos + 1) * 128], AF.Square, reads=[an], writes=[f"sq{i % 2}"])

        def s1(i):
            for part in range(2):
                for kt in range(4):
                    P.mm(pq[:, part * 2:part * 2 + 2], sq[i % 2][:, part * 4 + kt, :], onesb[:], start=(kt == 0), stop=(kt == 3),
                         reads=[f"sq{i % 2}", "onesb"], writes=["pq"])

        def s2(i):
            P.ts(r2[i % 8][:], pq[:, 0:4], 1.0 / 512, EPS, ALU.mult, ALU.add, reads=["pq"], writes=[f"r2{i % 8}"])

        def s3(i):
            P.act(r2[i % 8][:], r2[i % 8][:], AF.Sqrt, reads=[f"r2{i % 8}"], writes=[f"r2{i % 8}"])

        def s4(i):
            P.recip(r2[i % 8][:], r2[i % 8][:], reads=[f"r2{i % 8}"], writes=[f"r2{i % 8}"])

        def s5(i):
            tk = i * 128; a_ = am[(i // 4) % NA]; an = f"am{(i // 4) % NA}"; pos = i % 4
            P.dma("sp", xs[i % 3][:], C.x_all[tk:tk + 128, :], key=f"x5{i % 3}", writes=[f"x{i % 3}"])
            for part, pp, pn in ((0, pA, "pA"), (1, pS, "pS")):
                for hf in range(2):
                    for kt in range(4):
                        P.mm(pp[:, hf * 512:(hf + 1) * 512], a_[:, part * 4 + kt, pos * 128:(pos + 1) * 128], wo[:, part * 4 + kt, hf * 512:(hf + 1) * 512],
                             start=(kt == 0), stop=(kt == 3), reads=[an, "wo"], writes=[f"{pn}{hf}"])

        def s6(i):
            r = r2[i % 8]; rn = f"r2{i % 8}"
            P.act(t1[i % 2][:], pA[:, :], AF.Copy, reads=["pA0", "pA1", rn], writes=[f"t1{i % 2}"], scale=r[:, 0:1])
            P.ts(t2[i % 2][:], pS[:, :], r[:, 2:3], None, ALU.mult, reads=["pS0", "pS1", rn], writes=[f"t2{i % 2}"])

        def s7(i):
            cond = cond_of(i); tk = i * 128; x_ = xm[i % NM]; xn_ = f"xm{i % NM}"
            P.tt(t2[i % 2][:], t2[i % 2][:], t1[i % 2][:], ALU.add, reads=[f"t1{i % 2}", f"t2{i % 2}"], writes=[f"t2{i % 2}"])
            P.tt(t2[i % 2][:], t2[i % 2][:], g1[:, cond, :], ALU.mult, reads=[f"t2{i % 2}", "g1"], writes=[f"t2{i % 2}"])
            P.tt(x_[:], xs[i % 3][:], t2[i % 2][:], ALU.add, reads=[f"x{i % 3}", f"t2{i % 2}"], writes=[xn_])
            P.dma("sp", C.xmid_d[tk:tk + 128, :], x_[:], key=f"xmst{i % NM}", reads=[xn_])

        def s8(i):
            P.act(junk[:], xm[i % NM][:], AF.Square, reads=[f"xm{i % NM}"], writes=["junk", f"ss{i % 8}"], accum_out=ss[i % 8][:])

        def s9(i):
            P.ts(ss[i % 8][:], ss[i % 8][:], 1.0 / D, EPS, ALU.mult, ALU.add, reads=[f"ss{i % 8}"], writes=[f"ss{i % 8}"])

        def s10(i):
            P.act(ss[i % 8][:], ss[i % 8][:], AF.Sqrt, reads=[f"ss{i % 8}"], writes=[f"ss{i % 8}"])

        def s11(i):
            P.recip(rstd[i % 8][:], ss[i % 8][:], reads=[f"ss{i % 8}"], writes=[f"rstd{i % 8}"])
            P.ts(xn[i % 2][:], xm[i % NM][:], rstd[i % 8][:, 0:1], None, ALU.mult, reads=[f"xm{i % NM}", f"rstd{i % 8}"],
                 writes=[f"xn{i % 2}"])

        def s12(i):
            for k in range(8):
                P.tr(pT[:, k * 128:(k + 1) * 128], xn[i % 2][:, k * 128:(k + 1) * 128], ident[:], reads=[f"xn{i % 2}", "ident"],
                     writes=[f"pT_{k // 4}"])

        def s13(i):
            cond = cond_of(i); gid, pos, gsz = grp(i); h = hT[gid % 2]; hn_ = f"hT{gid % 2}"
            for k in range(8):
                P.act(h[:, k, pos * 128:(pos + 1) * 128], pT[:, k * 128:(k + 1) * 128], AF.Identity,
                      reads=[f"pT_{k // 4}", "msc0", "msc1", "modcol0", "modcol1"], writes=[hn_], scale=msc[:, cond, k:k + 1],
                      bias=modcol[:, cond, 24 + k:25 + k])
            if pos == gsz - 1:
                c0 = h2col((i - pos) * 128)
                P.dma("sp", C.h2T_d[:, c0:c0 + gsz * 128].rearrange("(k p) t -> p k t", p=128), h[:, :, 0:gsz * 128], key=f"h2st{gid % 2}",
                      reads=[hn_])
        pipeline(NQT, [s0, s1, s2, s3, s4, s5, s6, s7, s8, s9, s10, s11, s12, s13])
        P.emit()


def phase6(C):
    nc = C.nc
    with ExitStack() as ph:
        P = Prog(nc, C.st, "p6", C.dsems)
        sb, ps = mk(nc, ph, "p6")
        wup = sb("wup", [128, 8, 5632], BF16); wdn = sb("wdn", [128, 22, D], BF16)
        t1 = sb("t1", [128, D]); junk = sb("junk", [128, D])
        stg = [t1, junk]; stn = ["t1", "junk"]
        cw = sb("cw", [128, 3, 44]); cb = sb("cb", [128, 44])
        P.dma("sp", cw[:], C.convw_col, key="cw", writes=["cw"]); P.dma("sp", cb[:], C.convb_col, key="cb", writes=["cb"])
        g2 = sb("g2", [128, 2, D]); gf = sb("gf", [128, D])
        for c in range(2):
            P.dma("sp", g2[:, c, :], C.mod_d[c, 5120:6144].partition_broadcast(128), key="g2", writes=["g2"])
        P.dma("sp", gf[:], C.g_final.partition_broadcast(128), key="gf", writes=["gf"])
        NBM = 512
        h2 = [sb(f"h2{i}", [128, 8, NBM + 2], BF16) for i in range(1)]
        up = [sb(f"up{i}", [128, NBM + 2]) for i in range(3)]
        acc = [[sb(f"acc{hv}{i}", [128, NBM]) for i in range(2)] for hv in range(2)]
        actT = sb("actT", [128, 22, NBM], BF16)
        xm = [sb(f"xm{i}", [128, D]) for i in range(2)]
        ss = sb("ss", [128, 1]); rstd = sb("rstd", [128, 1])
        pu = [ps(f"pu{i}", [128, 512]) for i in range(3)]; phh = [ps(f"phh{i}", [128, 512]) for i in range(2)]
        pd = ps("pd", [128, 1024])
        def load_weights():
            n = 0
            for q in (0, 4, 1, 5, 2, 6, 3, 7):
                for k in range(8):
                    s = n % 2; n += 1
                    P.dma("sp", stg[s][:, 0:704], C.w_up[k * 128:(k + 1) * 128, q * 704:(q + 1) * 704], key=f"stg6{s}", writes=[stn[s]])
                    P.copy(wup[:, k, q * 704:(q + 1) * 704], stg[s][:, 0:704], reads=[stn[s]], writes=[f"wup{q}"], eng="pool")
                yield
            for j in range(22):
                s = n % 2; n += 1
                P.dma("sp", stg[s][:, 0:D], C.w_down[j * 128:(j + 1) * 128, :], key=f"stg6{s}", writes=[stn[s]])
                P.copy(wdn[:, j, :], stg[s][:, 0:D], reads=[stn[s]], writes=["wdn"], eng="pool")
                yield

        blocks = [(t0, 0, 512) for t0 in range(0, LS, 512)] + [(LS + 256 * i, 1, 256) for i in range(4)]
        import os as _os
        if _os.environ.get("P6NOBLK"):
            blocks = blocks[:int(_os.environ["P6NOBLK"]) - 1]
        iu = 0; ix = 0
        for bi, (t0, cond, NB) in enumerate(blocks):
            s = 0
            c0 = h2col(t0) - 1
            P.dma("sp", h2[s][:, :, 0:NB + 2], C.h2T_d[:, c0:c0 + NB + 2].rearrange("(k p) t -> p k t", p=128), key=f"h2l{s}", writes=[f"h2{s}"])
            if bi == 0:
                wgen = load_weights()
                next(wgen); next(wgen)
            for j in range(22):
                if bi == 0:
                    next(wgen, None)
                a2 = j % 2
                for hv in range(2):
                    m = j + 22 * hv
                    u = iu % 3; hb_ = iu % 2; hcol = 0; iu += 1
                    wr = sorted({f"wup{(m * 128) // 704}", f"wup{((m + 1) * 128 - 1) // 704}"})
                    for k in range(8):
                        P.mm(pu[u][:, 0:NB], wup[:, k, m * 128:(m + 1) * 128], h2[s][:, k, 1:NB + 1], start=(k == 0), stop=(k == 7),
                             reads=wr + [f"h2{s}"], writes=[f"pu{u}"])
                    for k in range(8):
                        P.mm(phh[hb_][:, hcol:hcol + 2], wup[:, k, m * 128:(m + 1) * 128], h2[s][:, k, 0:NB + 2:NB + 1], start=(k == 0),
                             stop=(k == 7), reads=wr + [f"h2{s}"], writes=[f"phh{hb_}"])
                    ac = acc[hv][a2]; acn = f"acc{hv}{a2}"
                    P.act(ac[:, 0:NB], pu[u][:, 0:NB], AF.Identity, reads=[f"pu{u}", "cw", "cb"], writes=[acn],
                          scale=cw[:, 1, m:m + 1], bias=cb[:, m:m + 1])
                    P.act(up[u][:, 1:NB + 1], pu[u][:, 0:NB], AF.Copy, reads=[f"pu{u}"], writes=[f"up{u}"])
                    P.act(up[u][:, 0:NB + 2:NB + 1], phh[hb_][:, hcol:hcol + 2], AF.Copy, reads=[f"phh{hb_}"], writes=[f"up{u}"])
                    P.stt(ac[:, 0:NB], up[u][:, 0:NB], cw[:, 0, m:m + 1], ac[:, 0:NB], ALU.mult, ALU.add,
                          reads=[f"up{u}", "cw", acn], writes=[acn])
                    P.stt(ac[:, 0:NB], up[u][:, 2:NB + 2], cw[:, 2, m:m + 1], ac[:, 0:NB], ALU.mult, ALU.add,
                          reads=[f"up{u}", "cw", acn], writes=[acn])
                P.act(acc[0][a2][:, 0:NB], acc[0][a2][:, 0:NB], AF.Silu, reads=[f"acc0{a2}"], writes=[f"acc0{a2}"])
                P.tt(actT[:, j, 0:NB], acc[0][a2][:, 0:NB], acc[1][a2][:, 0:NB], ALU.mult, reads=[f"acc0{a2}", f"acc1{a2}"],
                     writes=["actT"], eng="pool")
            if bi == 0:
                for _ in wgen:
                    pass
            for i in range(NB // 128):
                x = ix % 2; ix += 1
                tk = t0 + i * 128
                P.dma("sp", xm[x][:], C.xmid_d[tk:tk + 128, :], key=f"xml{x}", writes=[f"xm{x}"])
                for hf in range(2):
                    for j in range(22):
                        P.mm(pd[:, hf * 512:(hf + 1) * 512], actT[:, j, i * 128:(i + 1) * 128], wdn[:, j, hf * 512:(hf + 1) * 512],
                             start=(j == 0), stop=(j == 21), reads=["actT", "wdn"], writes=[f"pd{hf}"])
                P.tt(t1[:], pd[:, :], g2[:, cond, :], ALU.mult, reads=["pd0", "pd1", "g2"], writes=["t1"])
                P.tt(t1[:], t1[:], xm[x][:], ALU.add, reads=["t1", f"xm{x}"], writes=["t1"])
                P.act(junk[:], t1[:], AF.Square, reads=["t1"], writes=["junk", "ss"], accum_out=ss[:])
                P.ts(ss[:], ss[:], 1.0 / D, EPS, ALU.mult, ALU.add, reads=["ss"], writes=["ss"])
                P.act(ss[:], ss[:], AF.Sqrt, reads=["ss"], writes=["ss"])
                P.recip(rstd[:], ss[:], reads=["ss"], writes=["rstd"])
                P.stt(xm[x][:], t1[:], rstd[:, 0:1], gf[:], ALU.mult, ALU.mult, reads=["t1", "rstd", "gf"], writes=[f"xm{x}"])
                P.dma("sp", C.y_out[tk:tk + 128, :], xm[x][:], key=f"yost{x}", reads=[f"xm{x}"])
        P.emit()


_NC_CACHE = {}


def kernel(**inp):
    inp = {k: np.asarray(v) for k, v in inp.items()}
    if "nc" not in _NC_CACHE:
        _NC_CACHE["nc"] = build()
    nc = _NC_CACHE["nc"]
    in_maps = [host_inputs(inp, c) for c in range(8)]
    res = run_bass_kernel_spmd(nc, in_maps, core_ids=list(range(8)))
    yp = np.zeros((16, 256, D), np.float32); ysm = np.zeros((4, LS, D), np.float32)
    nk = np.zeros((16, 1, 256, 2, 64), np.float32); nv = np.zeros_like(nk)
    sre = np.zeros((16, 1, 2, 32, 64), np.float32); sim = np.zeros_like(sre)
    for b in range(4):
        r = res.results[b]
        ysm[b] = r["y_out"][:LS]; yp[4 * b:4 * b + 4] = r["y_out"][LS:].reshape(4, 256, D)
        nk[4 * b:4 * b + 4, 0] = r["new_k"].reshape(4, 256, 2, 64); nv[4 * b:4 * b + 4, 0] = r["new_v"].reshape(4, 256, 2, 64)
        sre[4 * b:4 * b + 4, 0] = r["new_sre"]; sim[4 * b:4 * b + 4, 0] = r["new_sim"]
    return (yp, ysm, nk, nv, sre, sim)
```

```python
import numpy as np
from contextlib import ExitStack
import concourse.bass as bass
import concourse.mybir as mybir
from concourse.bass_utils import run_bass_kernel_spmd

F32 = mybir.dt.float32
BF16 = mybir.dt.bfloat16
AF = mybir.ActivationFunctionType
ALU = mybir.AluOpType


class Prog:
    ENGS = ("pe", "act", "dve", "pool", "sp")

    def __init__(self, nc, stack, tag, dma_sems):
        self.nc = nc
        self.tag = tag
        self.ops = {e: [] for e in self.ENGS}
        if "__esem" not in dma_sems:
            dma_sems["__esem"] = {e: stack.enter_context(nc.semaphore(f"eng_{e}")) for e in self.ENGS}
            dma_sems["__ecnt"] = {e: 0 for e in self.ENGS}
        self.esem = dma_sems["__esem"]
        self.ecnt = dma_sems["__ecnt"]
        self.dma_sems = dma_sems
        self.stack = stack
        self.lastw = {}
        self.readers = {}
        self.waited = {}
        self.used_keys = set()
        self.keymap = {}

    def _deps(self, eng, reads, writes):
        toks = []
        relaxed = getattr(self, "relaxed", False)
        for r in reads:
            t = self.lastw.get(r)
            if t is not None:
                toks.append(t)
        for w in writes:
            t = self.lastw.get(w)
            if t is not None:
                toks.append(t)
            toks.extend(self.readers.get(w, ()))
        waits = {}
        for (sem, val, teng) in toks:
            if teng == "pe" and eng == "pe":
                continue
            if relaxed and teng == eng:
                continue
            k = (eng, id(sem))
            if self.waited.get(k, 0) >= val:
                continue
            if waits.get(id(sem), (None, 0))[1] < val:
                waits[id(sem)] = (sem, val)
        for (sem, val) in waits.values():
            self.waited[(eng, id(sem))] = val
        return list(waits.values())

    def _commit(self, tok, reads, writes):
        for r in reads:
            self.readers.setdefault(r, []).append(tok)
        for w in writes:
            self.lastw[w] = tok
            self.readers[w] = []

    def op(self, eng, fn, reads=(), writes=()):
        reads = tuple(reads); writes = tuple(writes)
        waits = self._deps(eng, reads, writes)
        self.ecnt[eng] += 1
        tok = (self.esem[eng], self.ecnt[eng], eng)
        self.ops[eng].append((waits, fn, (self.esem[eng], 1)))
        self._commit(tok, reads, writes)

    def dma(self, eng, out, in_, key, reads=(), writes=(), **kw):
        reads = tuple(reads); writes = tuple(w for w in writes if not (w.endswith("_d") or w.startswith("new_") or w.startswith("y_")))
        waits = self._deps(eng, reads, writes)
        pool = self.dma_sems.setdefault("__pool", [])
        if key not in self.keymap:
            idx = len(self.keymap)
            if idx >= len(pool):
                pool.append([self.stack.enter_context(self.nc.semaphore(f"dq_{idx}")), 0])
            self.keymap[key] = pool[idx]
        ent = self.keymap[key]
        ent[1] += 16
        self.used_keys.add(key)
        tok = (ent[0], ent[1], "dma")
        fn = (lambda e, o=out, i=in_, kw=kw: e.dma_start(out=o, in_=i, **kw))
        self.ops[eng].append((waits, fn, (ent[0], 16)))
        self._commit(tok, reads, writes)

    def mm(self, out, lhsT, rhs, start=True, stop=True, reads=(), writes=(), **kw):
        self.op("pe", lambda e: e.matmul(out, lhsT, rhs, start=start, stop=stop, **kw), reads, writes)

    def tr(self, out, in_, ident, reads=(), writes=()):
        self.op("pe", lambda e: e.transpose(out, in_, ident), reads, writes)

    def act(self, out, in_, func, reads=(), writes=(), eng="act", **kw):
        self.op(eng, lambda e: e.activation(out=out, in_=in_, func=func, **kw), reads, writes)

    def tt(self, out, a, b, op, reads=(), writes=(), eng="dve"):
        self.op(eng, lambda e: e.tensor_tensor(out=out, in0=a, in1=b, op=op), reads, writes)

    def ts(self, out, a, s1, s2, op0, op1=None, reads=(), writes=(), eng="dve"):
        if op1 is None:
            self.op(eng, lambda e: e.tensor_scalar(out=out, in0=a, scalar1=s1, scalar2=None, op0=op0), reads, writes)
        else:
            self.op(eng, lambda e: e.tensor_scalar(out=out, in0=a, scalar1=s1, scalar2=s2, op0=op0, op1=op1), reads, writes)

    def stt(self, out, in0, scalar, in1, op0, op1, reads=(), writes=()):
        self.op("dve", lambda e: e.scalar_tensor_tensor(out=out, in0=in0, scalar=scalar, in1=in1, op0=op0, op1=op1), reads, writes)

    def copy(self, out, in_, reads=(), writes=(), eng="dve"):
        self.op(eng, lambda e: e.tensor_copy(out=out, in_=in_), reads, writes)

    def memset(self, ap, val, writes=(), eng="dve"):
        self.op(eng, lambda e: e.memset(ap, val), (), writes)

    def recip(self, out, in_, reads=(), writes=()):
        self.op("dve", lambda e: e.reciprocal(out=out, in_=in_), reads, writes)

    def emit(self, final_keys=None):
        nc = self.nc
        flush = [(self.keymap[k][0], self.keymap[k][1]) for k in sorted(self.used_keys)]
        with nc.Block() as block:
            def body(engname):
                def f(e):
                    for (waits, fn, inc) in self.ops[engname]:
                        for (sem, val) in waits:
                            e.wait_ge(sem, val)
                        fn(e).then_inc(inc[0], inc[1])
                    if engname == "sp":
                        for (sem, val) in flush:
                            e.wait_ge(sem, val)
                return f
            block.tensor(body("pe"))
            block.scalar(body("act"))
            block.vector(body("dve"))
            block.gpsimd(body("pool"))
            block.sync(body("sp"))

LS = 4096; LP = 1024; NT = LS + LP; D = 1024; EPS = 1e-6
NQT = NT // 128


class Ctx:
    pass


def build(upto=99, debug=()):
    nc = bass.Bass("TRN2", target_bir_lowering=False)
    st = ExitStack()
    dsems = {}
    C = Ctx(); C.nc = nc; C.st = st; C.dsems = dsems

    def din(name, shape, dt=F32):
        return nc.dram_tensor(name, list(shape), dt, kind="ExternalInput").ap()

    def dout(name, shape, dt=F32):
        return nc.dram_tensor(name, list(shape), dt, kind="ExternalOutput").ap()

    def dscr(name, shape, dt=F32):
        kind = "ExternalOutput" if name in debug else "Internal"
        return nc.dram_tensor(name, list(shape), dt, kind=kind).ap()
    C.din, C.dout, C.dscr = din, dout, dscr

    C.x_all = din("x_all", [NT, D])
    C.cpair = din("cpair", [2, D])
    C.w_ada = din("w_ada", [D, 6 * D])
    C.b_ada = din("b_ada", [6 * D])
    C.g_norm1 = din("g_norm1", [D])
    C.ident = din("ident", [128, 128])
    C.mod_d = dscr("mod_d", [2, 6 * D])
    C.h1T_d = dscr("h1T_d", [D, NT], BF16)

    C.w_inx = din("w_inx", [D, NWC])
    C.rope_cos = din("rope_cos", [128, LS]); C.rope_sin = din("rope_sin", [128, LS])
    C.masks = din("masks", [128, 2, 256])
    C.attn_sink = din("attn_sink", [8])
    C.cache_k = din("cache_k", [512, 128]); C.cache_v = din("cache_v", [512, 128])
    C.qT_d = dscr("qT_d", [512, NT], BF16); C.kT_d = dscr("kT_d", [128, NT], BF16)
    C.V_d = dscr("V_d", [NT, 128], BF16); C.uc_d = dscr("uc_d", [NT // 8, 4096])
    C.aT_d = dscr("aT_d", [512, NT], BF16)
    C.new_k = dout("new_k", [LP, 128]); C.new_v = dout("new_v", [LP, 128])
    for nm in ("lamA_re", "lamA_im", "ldtA", "h0A_re", "h0A_im"):
        setattr(C, nm, din(nm, [128, 32]))
    for nm in ("lamB_re", "lamB_im", "ldtB"):
        setattr(C, nm, din(nm, [4096]))
    for nm in ("BpadT_re", "BpadT_im", "CTpad_re", "CTpad_im"):
        setattr(C, nm, din(nm, [128, 32, 128]))
    C.Dcol = din("Dcol", [128, 4]); C.wglu_bd = din("wglu_bd", [128, 4, 128])
    C.sT_d = dscr("sT_d", [512, NT], BF16)
    C.U_d = dscr("U_d", [128, 32, NCH]); C.S_d = dscr("S_d", [128, NCH, 64]); C.Hin_d = dscr("Hin_d", [128, NCH, 64])
    for nm in ("B_A_re", "B_A_im", "CT_A_re", "CT_A_im"):
        setattr(C, nm, din(nm, [128, 512]))
    C.wglu_k = din("wglu_k", [128, 32, 128]); C.maskFB = din("maskFB", [128, 2, 128]); C.Dsp = din("Dsp", [128, 32])
    C.new_sre = dout("new_sre", [4, 2, 32, 64]); C.new_sim = dout("new_sim", [4, 2, 32, 64])
    C.g_norm2 = din("g_norm2", [D]); C.gout_col = din("gout_col", [128, 8]); C.w_outx = din("w_outx", [D, D])
    C.w_up = din("w_up", [D, 5632]); C.w_down = din("w_down", [2816, D])
    C.convw_col = din("convw_col", [128, 3, 44]); C.convb_col = din("convb_col", [128, 44]); C.g_final = din("g_final", [D])
    C.xmid_d = dscr("xmid_d", [NT, D]); C.h2T_d = dscr("h2T_d", [D, H2COLS], BF16)
    C.y_out = dout("y_out", [NT, D]); C.halfsel = din("halfsel", [128, 2])
    phases = [phase0, phase1, phase2, phase4, phase5, phase6]
    for i, ph in enumerate(phases):
        if i > upto:
            break
        ph(C)
    st.close()
    return nc


def mk(nc, ph, tag):
    sb = lambda name, shape, dt=F32: ph.enter_context(nc.sbuf_tensor(f"{tag}_{name}", shape, dt))
    ps = lambda name, shape, dt=F32: ph.enter_context(nc.psum_tensor(f"{tag}_{name}", shape, dt))
    return sb, ps


def phase0(C):
    nc = C.nc
    with ExitStack() as ph:
        P = Prog(nc, C.st, "p0", C.dsems)
        sb, ps = mk(nc, ph, "p0")
        cT = sb("cT", [128, 2, 8]); sT = sb("sT", [128, 2, 8])
        bada = sb("bada", [2, 6 * D]); modrow = sb("modrow", [2, 6 * D])
        wa = [sb(f"wa{i}", [128, 8, 512]) for i in range(2)]
        pm = [ps(f"pm{i}", [128, 512]) for i in range(2)]
        for c in range(2):
            P.dma("sp", cT[:, c, :], C.cpair[c].rearrange("(k p) -> p k", p=128), key="cT", writes=["cT"],
                  allow_slow_non_contiguous=True)
        P.dma("sp", bada[:], C.b_ada.partition_broadcast(2), key="bada", writes=["bada"])
        P.act(sT[:], cT[:], AF.Silu, reads=["cT"], writes=["sT"])
        for n in range(12):
            s = n % 2
            P.dma("sp", wa[s][:], C.w_ada[:, n * 512:(n + 1) * 512].rearrange("(k p) n -> p k n", p=128),
                  key=f"wa{s}", writes=[f"wa{s}"])
            for k in range(8):
                P.mm(pm[s][0:2, :], sT[:, :, k], wa[s][:, k, :], start=(k == 0), stop=(k == 7),
                     reads=["sT", f"wa{s}"], writes=[f"pm{s}"])
            P.tt(modrow[0:2, n * 512:(n + 1) * 512], pm[s][0:2, :], bada[0:2, n * 512:(n + 1) * 512], ALU.add,
                 reads=[f"pm{s}", "bada"], writes=[f"modrow{n}"])
        P.dma("sp", C.mod_d[:, :], modrow[0:2, :], key="modst", reads=[f"modrow{n}" for n in range(12)],
              writes=["mod_d"])
        P.emit()


def load_modcols(C, P, sb, gname_ap, j_shift, j_scale, tag):
    modcol = sb(f"modcol{tag}", [128, 2, 48]); gcol = sb(f"gcol{tag}", [128, 8])
    msc = sb(f"msc{tag}", [128, 2, 8])
    for c in range(2):
        P.dma("sp", modcol[:, c, :], C.mod_d[c].rearrange("(j p) -> p j", p=128), key=f"modcol{tag}{c}",
              writes=[f"modcol{c}"], allow_slow_non_contiguous=True)
    P.dma("sp", gcol[:], gname_ap.rearrange("(k p) -> p k", p=128), key=f"gcol{tag}", writes=["gcol"],
          allow_slow_non_contiguous=True)
    for c in range(2):
        P.ts(msc[:, c, :], modcol[:, c, j_scale:j_scale + 8], 1.0, None, ALU.add, reads=[f"modcol{c}"],
             writes=[f"msc{c}"])
        P.tt(msc[:, c, :], msc[:, c, :], gcol[:], ALU.mult, reads=[f"msc{c}", "gcol"], writes=[f"msc{c}"])
    return msc, modcol


def norm_to_T(C, P, xt, xres, ss, rstd, xn, junk, pT, hT, msc, msh_ap_fn, cond, ident, sfx):
    P.act(junk[:], xt, AF.Square, reads=[xres], writes=["junk" + sfx, "ss" + sfx], accum_out=ss[:])
    P.ts(ss[:], ss[:], 1.0 / D, EPS, ALU.mult, ALU.add, reads=["ss" + sfx], writes=["ss" + sfx])
    P.act(ss[:], ss[:], AF.Sqrt, reads=["ss" + sfx], writes=["ss" + sfx])
    P.recip(rstd[:], ss[:], reads=["ss" + sfx], writes=["rstd" + sfx])
    P.ts(xn[:], xt, rstd[:, 0:1], None, ALU.mult, reads=[xres, "rstd" + sfx], writes=["xn" + sfx])
    for k in range(8):
        P.tr(pT[:, k * 128:(k + 1) * 128], xn[:, k * 128:(k + 1) * 128], ident[:], reads=["xn" + sfx, "ident"],
             writes=[f"pT{sfx}{k // 4}"])
    for k in range(8):
        P.act(hT[:, k, :], pT[:, k * 128:(k + 1) * 128], AF.Identity, reads=[f"pT{sfx}{k // 4}", "msc0", "msc1", "modcol0", "modcol1"],
              writes=[f"hT{sfx}{k}"], scale=msc[:, cond, k:k + 1], bias=msh_ap_fn(cond, k))


def pipeline(n, stages):
    ns = len(stages)
    for step in range(n + ns - 1):
        for k in reversed(range(ns)):
            i = step - k
            if 0 <= i < n:
                stages[k](i)


def phase1(C):
    nc = C.nc
    with ExitStack() as ph:
        P = Prog(nc, C.st, "p1", C.dsems)
        sb, ps = mk(nc, ph, "p1")
        ident = sb("ident", [128, 128])
        P.dma("sp", ident[:], C.ident, key="ident", writes=["ident"])
        msc, modcol = load_modcols(C, P, sb, C.g_norm1, 0, 8, "a")
        NX = 6
        xs = [sb(f"x{i}", [128, D]) for i in range(NX)]
        junk = sb("junk", [128, D]); xn = [sb(f"xn{i}", [128, D]) for i in range(2)]
        ss = [sb(f"ss{i}", [128, 1]) for i in range(8)]; rstd = [sb(f"rstd{i}", [128, 1]) for i in range(8)]
        hT = [sb(f"hT{i}", [128, 8, 512], BF16) for i in range(2)]
        pT = [ps(f"pT{i}", [128, 1024]) for i in range(2)]
        cond_of = lambda ti: 0 if ti < LS // 128 else 1

        def s0(i):
            P.dma("sp", xs[i % NX][:], C.x_all[i * 128:(i + 1) * 128, :], key=f"x{i % NX}", writes=[f"x{i % NX}"])

        def s1(i):
            P.act(junk[:], xs[i % NX][:], AF.Square, reads=[f"x{i % NX}"], writes=["junk", f"ss{i % 8}"], accum_out=ss[i % 8][:])

        def s2(i):
            P.ts(ss[i % 8][:], ss[i % 8][:], 1.0 / D, EPS, ALU.mult, ALU.add, reads=[f"ss{i % 8}"], writes=[f"ss{i % 8}"])

        def s3(i):
            P.act(ss[i % 8][:], ss[i % 8][:], AF.Sqrt, reads=[f"ss{i % 8}"], writes=[f"ss{i % 8}"])

        def s4(i):
            P.recip(rstd[i % 8][:], ss[i % 8][:], reads=[f"ss{i % 8}"], writes=[f"rstd{i % 8}"])
            P.ts(xn[i % 2][:], xs[i % NX][:], rstd[i % 8][:, 0:1], None, ALU.mult, reads=[f"x{i % NX}", f"rstd{i % 8}"],
                 writes=[f"xn{i % 2}"])

        def s5(i):
            for k in range(8):
                P.tr(pT[i % 2][:, k * 128:(k + 1) * 128], xn[i % 2][:, k * 128:(k + 1) * 128], ident[:],
                     reads=[f"xn{i % 2}", "ident"], writes=[f"pT{i % 2}_{k // 4}"])

        def s6(i):
            cond = cond_of(i); hb_ = (i // 4) % 2; h = hT[hb_]; pos = i % 4
            for k in range(8):
                P.act(h[:, k, pos * 128:(pos + 1) * 128], pT[i % 2][:, k * 128:(k + 1) * 128], AF.Identity,
                      reads=[f"pT{i % 2}_{k // 4}", "msc0", "msc1", "modcol0", "modcol1"], writes=[f"hT{hb_}"],
                      scale=msc[:, cond, k:k + 1], bias=modcol[:, cond, k:k + 1])
            if pos == 3:
                P.dma("sp", C.h1T_d[:, (i - 3) * 128:(i + 1) * 128].rearrange("(k p) t -> p k t", p=128), h[:], key=f"hTst{hb_}",
                      reads=[f"hT{hb_}"])
        pipeline(NQT, [s0, s1, s2, s3, s4, s5, s6])
        P.emit()


NWC = 1920


def phase2(C):
    nc = C.nc
    with ExitStack() as ph:
        P = Prog(nc, C.st, "p2", C.dsems)
        sb, ps = mk(nc, ph, "p2")
        w = sb("w", [128, 8, NWC], BF16)
        wst = [sb(f"wst{i}", [128, 960]) for i in range(2)]
        n = 0
        for k in range(8):
            for hh in range(2):
                s = n % 2; n += 1
                P.dma("sp", wst[s][:], C.w_inx[k * 128:(k + 1) * 128, hh * 960:(hh + 1) * 960], key=f"wst{s}",
                      writes=[f"wst{s}"])
                P.copy(w[:, k, hh * 960:(hh + 1) * 960], wst[s][:], reads=[f"wst{s}"], writes=["w"])
        hT = [sb(f"hT{i}", [128, 8, 512], BF16) for i in range(2)]
        cs = [sb(f"cs{i}", [128, 2, 512]) for i in range(2)]
        t1 = sb("t1", [128, 512]); t2 = sb("t2", [128, 512])
        qo = [sb(f"qo{i}", [128, 512], BF16) for i in range(2)]
        uall = [sb(f"uall{i}", [64, 32, 8, 16]) for i in range(2)]
        vo = [sb(f"vo{i}", [128, 128], BF16) for i in range(2)]
        kvo = [sb(f"kvo{i}", [128, 256]) for i in range(2)]
        pa = [ps(f"pa{i}", [128, 512]) for i in range(4)]
        pb = [ps(f"pb{i}", [128, 512]) for i in range(2)]
        ia = 0; ib = 0; iq = 0; iu = 0; iv = 0; ikv = 0
        for b in range(NT // 512):
            s = b % 2; t0 = b * 512; sample = t0 < LS
            P.dma("sp", hT[s][:], C.h1T_d[:, t0:t0 + 512].rearrange("(k p) t -> p k t", p=128), key=f"hTl{s}",
                  writes=[f"hT{s}"])
            if sample:
                P.dma("sp", cs[s][:, 0, :], C.rope_cos[:, t0:t0 + 512], key=f"cs{s}", writes=[f"cs{s}"])
                P.dma("sp", cs[s][:, 1, :], C.rope_sin[:, t0:t0 + 512], key=f"cs{s}", writes=[f"cs{s}"])

            def proj(col0, pt):
                for k in range(8):
                    P.mm(pt[:, :], w[:, k, col0:col0 + 128], hT[s][:, k, :], start=(k == 0), stop=(k == 7),
                         reads=["w", f"hT{s}"], writes=[pt.name])
            import os as _os
            _parts = _os.environ.get("P2PARTS", "quv")
            for j in range(5 if "q" in _parts else 0):
                c0 = j * 128 if j < 4 else 1152; c1 = 512 + j * 128 if j < 4 else 1024
                p0 = pa[ia % 4]; ia += 1
                proj(c0, p0)
                o = qo[iq % 2]; on = f"qo{iq % 2}"; iq += 1
                if sample:
                    p1 = pa[ia % 4]; ia += 1
                    proj(c1, p1)
                    P.tt(t1[:], p0[:, :], cs[s][:, 0, :], ALU.mult, reads=[p0.name, f"cs{s}"], writes=["t1"])
                    P.tt(t2[:], p1[:, :], cs[s][:, 1, :], ALU.mult, reads=[p1.name, f"cs{s}"], writes=["t2"])
                    P.tt(o[:], t1[:], t2[:], ALU.add, reads=["t1", "t2"], writes=[on])
                else:
                    P.act(o[:], p0[:, :], AF.Copy, reads=[p0.name], writes=[on])
                dst = C.qT_d[j * 128:(j + 1) * 128, t0:t0 + 512] if j < 4 else C.kT_d[:, t0:t0 + 512]
                P.dma("sp", dst, o[:], key=f"qst{(iq - 1) % 2}", reads=[on], writes=["qkT_d"])
            ua = uall[b % 2]; uan = f"uall{b % 2}"
            for sx in range(8):
                p0 = pa[ia % 4]; ia += 1
                for k in range(8):
                    P.mm(p0[0:64, :], hT[s][:, k, sx:512:8], w[:, k, 1408:1920], start=(k == 0), stop=(k == 7),
                         reads=["w", f"hT{s}"], writes=[p0.name])
                P.act(ua[0:64, :, sx, :], p0[0:64, :].rearrange("c (g x) -> c g x", g=32), AF.Copy, reads=[p0.name], writes=[uan])
            P.dma("sp", C.uc_d[t0 // 8:t0 // 8 + 64, :], ua[0:64].rearrange("c g s x -> c (g s x)"), key=f"ust{b % 2}",
                  reads=[uan], writes=["uc_d"])
            for i in range(4 if ("v" in _parts and not ("S" in _parts and not sample) and not ("P" in _parts and sample)) else 0):
                tk = t0 + i * 128
                p0 = pb[ib % 2]; ib += 1
                vc = 0 if sample else 128
                for k in range(8):
                    if sample:
                        P.mm(p0[:, 0:128], hT[s][:, k, i * 128:(i + 1) * 128], w[:, k, 1280:1408], start=(k == 0),
                             stop=(k == 7), reads=["w", f"hT{s}"], writes=[p0.name])
                    else:
                        P.mm(p0[:, 0:256], hT[s][:, k, i * 128:(i + 1) * 128], w[:, k, 1152:1408], start=(k == 0),
                             stop=(k == 7), reads=["w", f"hT{s}"], writes=[p0.name])
                o = vo[iv % 2]; on = f"vo{iv % 2}"; iv += 1
                P.act(o[:], p0[:, vc:vc + 128], AF.Copy, reads=[p0.name], writes=[on])
                P.dma("sp", C.V_d[tk:tk + 128, :], o[:], key=f"vst{(iv - 1) % 2}", reads=[on], writes=["V_d"])
                if not sample:
                    o2 = kvo[ikv % 2]; on2 = f"kvo{ikv % 2}"; ikv += 1
                    if "C" not in _parts:
                        P.act(o2[:], p0[:, 0:256], AF.Copy, reads=[p0.name], writes=[on2])
                if (not sample) and "N" not in _parts:
                    P.dma("sp", C.new_k[tk - LS:tk - LS + 128, :], o2[:, 0:128], key=f"kvst{(ikv - 1) % 2}",
                          reads=[on2], writes=["new_k"])
                    P.dma("sp", C.new_v[tk - LS:tk - LS + 128, :], o2[:, 128:256], key=f"kvst{(ikv - 1) % 2}",
                          reads=[on2], writes=["new_v"])
        P.emit()


def attn_ops(C, P, sb, ps):
    if True:
        nc = C.nc
        identf = sb("identf", [128, 128]); identb = sb("identb", [128, 128], BF16); onesb = sb("onesb", [128, 128], BF16)
        P.dma("sp", identf[:], C.ident, key="identf", writes=["identf"])
        P.copy(identb[:], identf[:], reads=["identf"], writes=["identb"])
        P.memset(onesb[:], 1.0, writes=["onesb"])
        maskf = sb("maskf", [128, 2, 256]); maskb = sb("maskb", [128, 2, 256], BF16)
        P.dma("sp", maskf[:], C.masks, key="maskf", writes=["maskf"])
        P.copy(maskb[:], maskf[:], reads=["maskf"], writes=["maskb"])
        es = sb("es", [128, 8])
        P.dma("sp", es[:], C.attn_sink.partition_broadcast(128), key="es", writes=["es"])
        P.act(es[:], es[:], AF.Exp, reads=["es"], writes=["es"])
        ckf = sb("ckf", [128, 4, 128]); cvf = sb("cvf", [128, 4, 128])
        ckT = sb("ckT", [128, 512], BF16); cv = sb("cv", [128, 4, 128], BF16)
        P.dma("sp", ckf[:], C.cache_k.rearrange("(b p) d -> p b d", p=128), key="ckf", writes=["ckf"])
        P.dma("sp", cvf[:], C.cache_v.rearrange("(b p) d -> p b d", p=128), key="cvf", writes=["cvf"])
        P.copy(cv[:], cvf[:], reads=["cvf"], writes=["cv"])
        pSS = [ps(f"pS{i}", [128, 1024]) for i in range(2)]
        pO = [ps(f"pO{i}", [128, 512]) for i in range(2)]; pD = [ps(f"pD{i}", [128, 512]) for i in range(2)]
        for b in range(4):
            P.tr(pSS[0][:, b * 128:(b + 1) * 128], ckf[:, b, :], identf[:], reads=["ckf", "identf"], writes=["pS0"])
        P.act(ckT[:], pSS[0][:, 0:512], AF.Copy, reads=["pS0"], writes=["ckT"])
        kT = sb("kT", [128, LS], BF16); V = sb("V", [128, LS // 128, 128], BF16)
        qT = [sb(f"qT{i}", [128, 4, 128], BF16) for i in range(2)]
        PT = [sb(f"PT{i}", [128, 256], BF16) for i in range(3)]
        rd = [sb(f"rd{i}", [128, 256]) for i in range(2)]; aT = [sb(f"aT{i}", [128, 4, 128], BF16) for i in range(2)]
        seqs = [(0, LS, True)] + [(LS + 256 * i, 256, False) for i in range(4)]
        iq = 0
        items = []
        for (s0, L, sample) in seqs:
            nb = L // 128
            for qb in range(nb):
                if sample:
                    tiles = [("b", kb, (0 if kb < qb else (1 if kb > qb else None))) for kb in (qb - 1, qb, qb + 1)
                             if 0 <= kb < nb] + [("c", kb, None) for kb in range(4)]
                else:
                    tiles = [("b", kb, None) for kb in range(nb)]
                for j in range(4):
                    for ti, tl in enumerate(tiles):
                        items.append(dict(s0=s0, L=L, nb=nb, qb=qb, j=j, ti=ti, nt=len(tiles), tile=tl,
                                          newseq=(qb == 0 and j == 0 and ti == 0), newq=(j == 0 and ti == 0)))
        state = dict(iq=0, ipair=-1)

        def front(n, it):
            if it["newseq"]:
                P.dma("sp", kT[:, 0:it["L"]], C.kT_d[:, it["s0"]:it["s0"] + it["L"]], key="kT", writes=["kT"])
                P.dma("sp", V[:, 0:it["nb"], :], C.V_d[it["s0"]:it["s0"] + it["L"], :].rearrange("(b p) d -> p b d", p=128), key="V",
                      writes=["V"])
            if it["newq"]:
                state["iq"] += 1
                qi = state["iq"] % 2
                tq = it["s0"] + it["qb"] * 128
                P.dma("sp", qT[qi][:], C.qT_d[:, tq:tq + 128].rearrange("(j p) t -> p j t", p=128), key=f"qT{qi}", writes=[f"qT{qi}"])
            if it["ti"] == 0:
                state["ipair"] += 1
            it["qi"] = state["iq"] % 2; it["pi"] = state["ipair"] % 2
            q = qT[it["qi"]]; qn = f"qT{it['qi']}"
            kind, kb, mk_ = it["tile"]; j = it["j"]
            SS = pSS[n % 2]; SSn = f"pS{n % 2}"
            ksrc = kT if kind == "b" else ckT; ksn = "kT" if kind == "b" else "ckT"
            for (c0, r0) in ((0, 0), (512, 64)):
                first = True
                if mk_ is not None:
                    P.mm(SS[:, c0:c0 + 128], identb[:], maskb[:, mk_, 0:128], start=True, stop=False, reads=["identb", "maskb"],
                         writes=[SSn]); first = False
                P.mm(SS[:, c0:c0 + 128], ksrc[r0:r0 + 64, kb * 128:(kb + 1) * 128], q[r0:r0 + 64, j, :], start=first, stop=True,
                     reads=[ksn, qn], writes=[SSn])
            pt = PT[n % 3]
            P.act(pt[:].rearrange("p (a b) -> p a b", a=2), mkap(SS[:, 0:128], [[512, 2], [1, 128]]), AF.Exp, reads=[SSn],
                  writes=[f"PT{n % 3}"], scale=0.125)

        def back(n, it):
            kind, kb, mk_ = it["tile"]; j = it["j"]; pi = it["pi"]
            pt = PT[n % 3]; ptn = f"PT{n % 3}"
            vsrc = V[:, kb, :] if kind == "b" else cv[:, kb, :]; vsn = "V" if kind == "b" else "cv"
            if it["ti"] == 0:
                for dd in [d_ for d_ in deferred if d_[1]["pi"] == pi]:
                    deferred.remove(dd); norm(dd[1])
            P.mm(pO[pi][:, 0:256], vsrc, pt[:], start=(it["ti"] == 0), stop=(it["ti"] == it["nt"] - 1), reads=[vsn, ptn], writes=[f"pO{pi}"])
            P.mm(pD[pi][:, 0:256], onesb[:], pt[:], start=(it["ti"] == 0), stop=(it["ti"] == it["nt"] - 1), reads=["onesb", ptn],
                 writes=[f"pD{pi}"])
            if it["ti"] == it["nt"] - 1:
                deferred.append([4, it])

        deferred = []

        def norm(it):
            if True:
                j = it["j"]; pi = it["pi"]
                a = aT[it["qi"]]; an = f"aT{it['qi']}"; r_ = rd[pi]; rn = f"rd{pi}"
                P.act(r_[:, 0:128], pD[pi][:, 0:128], AF.Ln, reads=[f"pD{pi}", "es"], writes=[rn], bias=es[:, j:j + 1])
                P.act(r_[:, 128:256], pD[pi][:, 128:256], AF.Ln, reads=[f"pD{pi}", "es"], writes=[rn], bias=es[:, 4 + j:5 + j])
                P.act(r_[:], r_[:], AF.Exp, reads=[rn], writes=[rn], scale=-1.0)
                P.tt(a[0:64, j, :], pO[pi][0:64, 0:128], r_[0:64, 0:128], ALU.mult, reads=[f"pO{pi}", rn], writes=[an])
                P.tt(a[64:128, j, :], pO[pi][64:128, 128:256], r_[64:128, 128:256], ALU.mult, reads=[f"pO{pi}", rn], writes=[an])
                if j == 3:
                    tq = it["s0"] + it["qb"] * 128
                    P.dma("sp", C.aT_d[:, tq:tq + 128].rearrange("(j p) t -> p j t", p=128), a[:], key=f"ast{it['qi']}", reads=[an])

        yield
        for n in range(len(items) + 1):
            flush_first = n < len(items) and items[n]["newseq"]
            if n >= 1 and flush_first:
                back(n - 1, items[n - 1])
                while deferred:
                    norm(deferred.pop(0)[1])
            if n < len(items):
                front(n, items[n])
            if n >= 1 and not flush_first:
                back(n - 1, items[n - 1])
            for dd in deferred:
                dd[0] -= 1
            while deferred and (deferred[0][0] <= 0 or n == len(items)):
                norm(deferred.pop(0)[1])
            yield


def _const_tables():
    perm = np.concatenate([np.arange(16, 32), np.arange(0, 16), np.arange(48, 64), np.arange(32, 48)])
    t = np.arange(LS); row = (t // 64).astype(np.float32); col = (t % 64).astype(np.float32)
    inv = (10000.0 ** (-np.arange(16, dtype=np.float32) / 16)).astype(np.float32)
    ar = row[None, :] * inv[:, None]; ac = col[None, :] * inv[:, None]
    cos64 = np.concatenate([np.cos(ar), np.cos(ar), np.cos(ac), np.cos(ac)], 0)
    sin64 = np.concatenate([-np.sin(ar), np.sin(ar), -np.sin(ac), np.sin(ac)], 0)
    cos = np.concatenate([cos64, cos64], 0).astype(np.float32); sin = np.concatenate([sin64, sin64], 0).astype(np.float32)
    j = np.arange(128)[:, None]; i = np.arange(128)[None, :]
    m0 = np.where(j >= i, 0.0, -30000.0); m1 = np.where(j <= i, 0.0, -30000.0)
    masks = np.stack([np.concatenate([m0, m0], 1), np.concatenate([m1, m1], 1)], 1).astype(np.float32)
    return perm, cos, sin, masks


def host_inputs(inp, core):
    b = core % 4
    perm, cos, sin, masks = _const_tables()
    d = {}
    d["x_all"] = np.concatenate([inp["x_sample"][b], inp["x_prompt"][4 * b:4 * b + 4].reshape(LP, D)], 0)
    d["cpair"] = np.stack([inp["c"][b], inp["c_ctx"]], 0)
    d["w_ada"] = inp["w_ada"][0]; d["b_ada"] = inp["b_ada"][0]; d["g_norm1"] = inp["g_norm1"][0]
    d["ident"] = np.eye(128, dtype=np.float32)
    w = inp["w_in"][0]
    q = w[:, 0:512].reshape(D, 8, 64); k = w[:, 512:640].reshape(D, 2, 64)
    qt = np.concatenate([np.concatenate([q[:, j], q[:, 4 + j]], 1) for j in range(4)], 1)
    qp = np.concatenate([np.concatenate([q[:, j][:, perm], q[:, 4 + j][:, perm]], 1) for j in range(4)], 1)
    kt = k.reshape(D, 128); kp = k[:, :, perm].reshape(D, 128)
    d["w_inx"] = np.concatenate([qt, qp, kp, kt, w[:, 640:768], w[:, 768:1280]], 1)
    d["rope_cos"] = cos; d["rope_sin"] = sin; d["masks"] = masks
    d["attn_sink"] = inp["attn_sink"][0]
    d["cache_k"] = inp["cache_k"][b, 0].reshape(512, 128); d["cache_v"] = inp["cache_v"][b, 0].reshape(512, 128)
    tA = lambda a: a.transpose(0, 2, 1).reshape(128, 32)
    tB = lambda a: a.transpose(1, 0, 2).reshape(4096)
    d["lamA_re"] = tA(inp["ssm_lam_re"][0]); d["lamA_im"] = tA(inp["ssm_lam_im"][0])
    ldt = np.repeat(inp["ssm_log_dt"][0][:, :, None], 64, 2)
    d["ldtA"] = tA(ldt); d["lamB_re"] = tB(inp["ssm_lam_re"][0]); d["lamB_im"] = tB(inp["ssm_lam_im"][0]); d["ldtB"] = tB(ldt)
    d["h0A_re"] = tA(inp["state_ssm_re"][b, 0]); d["h0A_im"] = tA(inp["state_ssm_im"][b, 0])
    for nm, src in (("BpadT_re", "ssm_b_re"), ("BpadT_im", "ssm_b_im")):
        B = inp[src][0]
        o = np.zeros((128, 32, 128), np.float32)
        for g in range(32):
            gl = g % 8
            o[gl * 16:(gl + 1) * 16, g, :] = B[:, g].transpose(2, 0, 1).reshape(16, 128)
        d[nm] = o
    for nm, src in (("CTpad_re", "ssm_c_re"), ("CTpad_im", "ssm_c_im")):
        Cm = inp[src][0]
        o = np.zeros((128, 32, 128), np.float32)
        for g in range(32):
            gl = g % 8
            o[:, g, gl * 16:(gl + 1) * 16] = Cm[:, g].transpose(0, 2, 1).reshape(128, 16)
        d[nm] = o
    d["Dcol"] = inp["ssm_d"][0].reshape(4, 128).T
    for nm, src in (("B_A_re", "ssm_b_re"), ("B_A_im", "ssm_b_im")):
        d[nm] = inp[src][0].transpose(0, 2, 1, 3).reshape(128, 512)
    for nm, src in (("CT_A_re", "ssm_c_re"), ("CT_A_im", "ssm_c_im")):
        d[nm] = inp[src][0].transpose(0, 3, 1, 2).reshape(128, 512)
    wk = np.zeros((128, 32, 128), np.float32)
    for s_ in range(8):
        wk[s_ * 16:(s_ + 1) * 16, :, s_ * 16:(s_ + 1) * 16] = inp["ssm_w_glu"][0].transpose(1, 0, 2)
    d["wglu_k"] = wk
    sp = np.arange(128) // 16
    d["maskFB"] = np.stack([(sp[:, None] <= sp[None, :]), (sp[:, None] >= sp[None, :])], 1).astype(np.float32)
    d["Dsp"] = np.tile(inp["ssm_d"][0].T, (8, 1))
    wg = np.zeros((128, 4, 128), np.float32)
    for g in range(32):
        gl = g % 8
        wg[gl * 16:(gl + 1) * 16, g // 8, gl * 16:(gl + 1) * 16] = inp["ssm_w_glu"][0][g]
    d["wglu_bd"] = wg
    d["g_norm2"] = inp["g_norm2"][0]; d["g_final"] = inp["g_final"]
    d["halfsel"] = np.tile(np.array([[1.0, 0.0]] if core < 4 else [[0.0, 1.0]], np.float32), (128, 1))
    wo = inp["w_out"][0]
    tp = lambda a: np.concatenate([np.concatenate([a[j * 64:(j + 1) * 64], a[(4 + j) * 64:(5 + j) * 64]], 0) for j in range(4)], 0)
    d["w_outx"] = np.concatenate([tp(wo[0:512]), wo[512:1024]], 0)
    gcat = np.concatenate([tp(inp["g_out_attn"][0]), inp["g_out_ssm"][0]], 0)
    d["gout_col"] = gcat.reshape(8, 128).T
    d["w_up"] = inp["w_up"][0]; d["w_down"] = inp["w_down"][0]
    d["convw_col"] = inp["conv_w"][0].reshape(3, 44, 128).transpose(2, 0, 1)
    d["convb_col"] = inp["conv_b"][0].reshape(44, 128).T
    return {k_: np.ascontiguousarray(v, dtype=np.float32) for k_, v in d.items()}


TB = 64
PI = float(np.pi)


def zoh(P, sb, lre, lim, ldt, n, tag, rd):
    cache = zoh.__dict__.setdefault("cache", {})
    def T(nm):
        k = (id(P.nc), tag, nm)
        if k not in cache:
            cache[k] = sb(f"z{tag}_{nm}", [128, n])
        return cache[k]
    dt = T("dt"); mag = T("mag"); ang = T("ang"); s = T("s"); c = T("c"); are = T("are"); aim = T("aim")
    den = T("den"); t1 = T("t1"); t2 = T("t2"); cre = T("cre"); cim = T("cim"); nre = T("nre")
    R = lambda *x: [f"z{tag}_{i}" for i in x]
    P.act(dt[:], ldt, AF.Exp, reads=rd, writes=R("dt"))
    P.tt(mag[:], lre, dt[:], ALU.mult, reads=rd + R("dt"), writes=R("mag"))
    P.act(mag[:], mag[:], AF.Exp, reads=R("mag"), writes=R("mag"))
    P.tt(ang[:], lim, dt[:], ALU.mult, reads=rd + R("dt"), writes=R("ang"))
    P.act(c[:], ang[:], AF.Sin, reads=R("ang"), writes=R("c"), scale=1.0 / 16)
    P.act(s[:], ang[:], AF.Sin, reads=R("ang"), writes=R("s"), scale=1.0 / 8)
    P.tt(c[:], c[:], c[:], ALU.mult, reads=R("c"), writes=R("c"))
    P.ts(c[:], c[:], -2.0, 1.0, ALU.mult, ALU.add, reads=R("c"), writes=R("c"))
    for _ in range(3):
        P.tt(t1[:], s[:], s[:], ALU.mult, reads=R("s"), writes=R("t1"))
        P.stt(s[:], s[:], 2.0, c[:], ALU.mult, ALU.mult, reads=R("s", "c"), writes=R("s"))
        P.ts(c[:], t1[:], -2.0, 1.0, ALU.mult, ALU.add, reads=R("t1"), writes=R("c"))
    P.tt(are[:], mag[:], c[:], ALU.mult, reads=R("mag", "c"), writes=R("are"))
    P.tt(aim[:], mag[:], s[:], ALU.mult, reads=R("mag", "s"), writes=R("aim"))
    P.tt(den[:], lre, lre, ALU.mult, reads=rd, writes=R("den"))
    P.tt(t1[:], lim, lim, ALU.mult, reads=rd, writes=R("t1"))
    P.tt(den[:], den[:], t1[:], ALU.add, reads=R("den", "t1"), writes=R("den"))
    P.recip(den[:], den[:], reads=R("den"), writes=R("den"))
    P.ts(nre[:], are[:], -1.0, None, ALU.add, reads=R("are"), writes=R("nre"))
    P.tt(t1[:], nre[:], lre, ALU.mult, reads=R("nre") + rd, writes=R("t1"))
    P.tt(t2[:], aim[:], lim, ALU.mult, reads=R("aim") + rd, writes=R("t2"))
    P.tt(t1[:], t1[:], t2[:], ALU.add, reads=R("t1", "t2"), writes=R("t1"))
    P.tt(cre[:], t1[:], den[:], ALU.mult, reads=R("t1", "den"), writes=R("cre"))
    P.tt(t1[:], aim[:], lre, ALU.mult, reads=R("aim") + rd, writes=R("t1"))
    P.tt(t2[:], nre[:], lim, ALU.mult, reads=R("nre") + rd, writes=R("t2"))
    P.tt(t1[:], t1[:], t2[:], ALU.subtract, reads=R("t1", "t2"), writes=R("t1"))
    P.tt(cim[:], t1[:], den[:], ALU.mult, reads=R("t1", "den"), writes=R("cim"))
    return are, aim, cre, cim, R("are", "aim", "cre", "cim")


NCH = NT // 8
CS = 64


def mkap(base, dims):
    return bass.AP(base.tensor, base.offset, [list(base.ap[0])] + [list(d) for d in dims])


def phase4(C):
    nc = C.nc
    with ExitStack() as outer:
        sbo, pso = mk(nc, outer, "q4")
        W = sbo("W", [128, 32, 128]); YS = sbo("YS", [128, 32, 2, 128]); WG = sbo("WG", [128, 32, 128])
        AB = sbo("AB", [128, 128]); h0x = sbo("h0x", [128, 128]); zer = sbo("zer", [128, 128])
        ident = sbo("ident", [128, 128])
        with ExitStack() as mid:
            sbm, _ = mk(nc, mid, "q4m")
            XS = sbm("XS", [128, 32, 2, 128])
            with ExitStack() as ph:
                P = Prog(nc, C.st, "q4a", C.dsems)
                sb, ps = mk(nc, ph, "q4a")
                P.dma("sp", ident[:], C.ident, key="q4ident", writes=["ident"])
                P.dma("sp", WG[:], C.wglu_k, key="WG", writes=["WG"])
                lA = sb("lA", [128, 3, 32])
                for i, nm in enumerate(("lamA_re", "lamA_im", "ldtA")):
                    P.dma("sp", lA[:, i, :], getattr(C, nm), key="lA2", writes=["lA"])
                are, aim, cre, cim, rr = zoh(P, sb, lA[:, 0, :], lA[:, 1, :], lA[:, 2, :], 32, "A2", ["lA"])
                tA = sb("tA", [128, 32]); tB = sb("tB", [128, 32]); rm2 = sb("rm2", [128, 32])

                def cmul(ore, oim, xr, xi, yr, yi, reads, writes, neg_im=False):
                    P.tt(tA[:], xr, yr, ALU.mult, reads=reads, writes=["tA"])
                    P.tt(tB[:], xi, yi, ALU.mult, reads=reads, writes=["tB"])
                    P.tt(ore, tA[:], tB[:], ALU.subtract, reads=["tA", "tB"], writes=writes)
                    P.tt(tA[:], xr, yi, ALU.mult, reads=reads, writes=["tA"])
                    P.tt(tB[:], xi, yr, ALU.mult, reads=reads, writes=["tB"])
                    P.tt(oim, tA[:], tB[:], ALU.add, reads=["tA", "tB"], writes=writes)
                    if neg_im:
                        P.ts(oim, oim, -1.0, None, ALU.mult, reads=writes, writes=writes)
                PW = sb("PW", [128, 9, 2, 32]); PIv = sb("PIv", [128, 8, 2, 32])
                P.memset(PW[:, 0, 0, :], 1.0, writes=["PW"]); P.memset(PW[:, 0, 1, :], 0.0, writes=["PW"])
                P.memset(PIv[:, 0, 0, :], 1.0, writes=["PIv"]); P.memset(PIv[:, 0, 1, :], 0.0, writes=["PIv"])
                P.copy(PW[:, 1, 0, :], are[:], reads=rr, writes=["PW"]); P.copy(PW[:, 1, 1, :], aim[:], reads=rr, writes=["PW"])
                P.tt(rm2[:], are[:], are[:], ALU.mult, reads=rr, writes=["rm2"])
                P.tt(tA[:], aim[:], aim[:], ALU.mult, reads=rr, writes=["tA"])
                P.tt(rm2[:], rm2[:], tA[:], ALU.add, reads=["rm2", "tA"], writes=["rm2"])
                P.recip(rm2[:], rm2[:], reads=["rm2"], writes=["rm2"])
                P.tt(PIv[:, 1, 0, :], are[:], rm2[:], ALU.mult, reads=rr + ["rm2"], writes=["PIv"])
                P.tt(PIv[:, 1, 1, :], aim[:], rm2[:], ALU.mult, reads=rr + ["rm2"], writes=["PIv"])
                P.ts(PIv[:, 1, 1, :], PIv[:, 1, 1, :], -1.0, None, ALU.mult, reads=["PIv"], writes=["PIv"])
                for k in range(2, 9):
                    cmul(PW[:, k, 0, :], PW[:, k, 1, :], PW[:, k - 1, 0, :], PW[:, k - 1, 1, :], PW[:, 1, 0, :], PW[:, 1, 1, :],
                         ["PW"], ["PW"])
                for k in range(2, 8):
                    cmul(PIv[:, k, 0, :], PIv[:, k, 1, :], PIv[:, k - 1, 0, :], PIv[:, k - 1, 1, :], PIv[:, 1, 0, :],
                         PIv[:, 1, 1, :], ["PIv"], ["PIv"])
                P.copy(AB[:, 0:32], PW[:, 8, 0, :], reads=["PW"], writes=["AB"]); P.copy(AB[:, 32:64], PW[:, 8, 0, :], reads=["PW"], writes=["AB"])
                P.ts(AB[:, 64:96], PW[:, 8, 1, :], -1.0, None, ALU.mult, reads=["PW"], writes=["AB"])
                P.copy(AB[:, 96:128], PW[:, 8, 1, :], reads=["PW"], writes=["AB"])
                for blk, nm in ((0, "h0A_re"), (1, "h0A_im"), (2, "h0A_re"), (3, "h0A_im")):
                    P.dma("sp", h0x[:, blk * 32:(blk + 1) * 32], getattr(C, nm), key="h0x", writes=["h0x"])
                P.memset(zer[:], 0.0, writes=["zer"])
                tabs = {}
                for nm, ff, fb in (("PX", lambda s_: (PW, 7 - s_), lambda s_: (PW, s_)),
                                   ("PY", lambda t: (PW, t + 1), lambda t: (PW, 8 - t)),
                                   ("PE", lambda s_: (PIv, s_), lambda s_: (PW, s_)),
                                   ("PF", lambda t: (PW, t), lambda t: (PIv, t))):
                    tb = sb(nm, [128, 8, 2, 32]); tabs[nm] = tb
                    for i in range(8):
                        src, k = ff(i)
                        P.copy(tb[0:64, i], src[0:64, k], reads=["PW", "PIv"], writes=[nm])
                        src, k = fb(i)
                        P.act(tb[64:128, i], src[64:128, k], AF.Copy, reads=["PW", "PIv"], writes=[nm])
                Braw = sb("Braw", [128, 2, 512]); Bb = sb("Bb", [128, 2, 512]); Craw = sb("Craw", [128, 2, 512])
                P.dma("sp", Braw[:, 0, :], C.B_A_re, key="Braw", writes=["Braw"]); P.dma("sp", Braw[:, 1, :], C.B_A_im, key="Braw", writes=["Braw"])
                P.dma("sp", Craw[:, 0, :], C.CT_A_re, key="Craw", writes=["Craw"]); P.dma("sp", Craw[:, 1, :], C.CT_A_im, key="Craw", writes=["Craw"])
                bc = lambda t2: mkap(t2, [[1, 32], [0, 16]])
                g3 = lambda t3: t3.rearrange("p (g x) -> p g x", g=32)
                u1 = sb("u1", [128, 512]); u2 = sb("u2", [128, 512])

                def cmul_b(ore, oim, tr_, ti_, xr, xi, reads, writes, neg_im=False):
                    P.tt(g3(u1[:]), bc(tr_), g3(xr), ALU.mult, reads=reads, writes=["u1"])
                    P.tt(g3(u2[:]), bc(ti_), g3(xi), ALU.mult, reads=reads, writes=["u2"])
                    P.tt(ore, g3(u1[:]), g3(u2[:]), ALU.subtract, reads=["u1", "u2"], writes=writes)
                    P.tt(g3(u1[:]), bc(tr_), g3(xi), ALU.mult, reads=reads, writes=["u1"])
                    P.tt(g3(u2[:]), bc(ti_), g3(xr), ALU.mult, reads=reads, writes=["u2"])
                    P.tt(oim, g3(u1[:]), g3(u2[:]), ALU.add if not neg_im else ALU.add, reads=["u1", "u2"], writes=writes)
                    if neg_im:
                        P.ts(oim, oim, -1.0, None, ALU.mult, reads=writes, writes=writes)
                cmul_b(g3(Bb[:, 0, :]), g3(Bb[:, 1, :]), cre[:], cim[:], Braw[:, 0, :], Braw[:, 1, :], rr + ["Braw"], ["Bb"])
                tmpA = sb("tmpA", [128, 2, 32, 8, 16]); tmpB = sb("tmpB", [128, 2, 32, 8, 16])
                pW = [ps(f"pW{i}", [128, 512]) for i in range(4)]
                for i in range(8):
                    cmul_b(tmpA[:, 0, :, i, :], tmpA[:, 1, :, i, :], tabs["PX"][:, i, 0, :], tabs["PX"][:, i, 1, :], Bb[:, 0, :], Bb[:, 1, :],
                           ["PX", "Bb"], ["tmpA"])
                n = 0
                for g in range(32):
                    for ri in range(2):
                        pw = pW[(n // 4) % 2]; pwn = f"pW{(n // 4) % 2}"
                        P.tr(pw[:, (n % 4) * 128:(n % 4 + 1) * 128], tmpA[:, ri, g].rearrange("p s x -> p (s x)"), ident[:],
                             reads=["tmpA", "ident"], writes=[pwn])
                        n += 1
                        if n % 4 == 0:
                            g0 = g - 1
                            P.act(XS[:, g0:g0 + 2].rearrange("p g r x -> p (g r x)"), pw[:, :], AF.Copy, reads=[pwn], writes=["XS"])
                ysv = YS[:].rearrange("p g r (t q) -> p g r t q", t=8)
                for i in range(8):
                    cmul_b(ysv[:, :, 0, i, :], ysv[:, :, 1, i, :], tabs["PY"][:, i, 0, :], tabs["PY"][:, i, 1, :], Craw[:, 0, :], Craw[:, 1, :],
                           ["PY", "Craw"], ["YS"], neg_im=True)
                for i in range(8):
                    cmul_b(tmpA[:, 0, :, i, :], tmpA[:, 1, :, i, :], tabs["PE"][:, i, 0, :], tabs["PE"][:, i, 1, :], Bb[:, 0, :], Bb[:, 1, :],
                           ["PE", "Bb"], ["tmpA"])
                    cmul_b(tmpB[:, 0, :, i, :], tmpB[:, 1, :, i, :], tabs["PF"][:, i, 0, :], tabs["PF"][:, i, 1, :], Craw[:, 0, :], Craw[:, 1, :],
                           ["PF", "Craw"], ["tmpB"], neg_im=True)
                mk2 = sb("mk2", [128, 2, 128]); Dsp = sb("Dsp", [128, 32]); w1 = sb("w1", [128, 128]); w2 = sb("w2", [128, 128])
                P.dma("sp", mk2[:], C.maskFB, key="mk2", writes=["mk2"]); P.dma("sp", Dsp[:], C.Dsp, key="Dsp", writes=["Dsp"])
                fl = lambda t5, ri, g, r0: t5[r0:r0 + 64, ri, g].rearrange("p s x -> p (s x)")
                for g in range(32):
                    for d, pw, pwn in ((0, pW[2], "pW2"), (1, pW[3], "pW3")):
                        r0 = d * 64
                        P.mm(pw[:, 0:128], fl(tmpA, 0, g, r0), fl(tmpB, 0, g, r0), start=True, stop=False,
                             reads=["tmpA", "tmpB"], writes=[pwn])
                        P.mm(pw[:, 0:128], fl(tmpA, 1, g, r0), fl(tmpB, 1, g, r0), start=False, stop=True,
                             reads=["tmpA", "tmpB"], writes=[pwn])
                    P.tt(w1[:], pW[2][:, 0:128], mk2[:, 0, :], ALU.mult, reads=["pW2", "mk2"], writes=["w1"])
                    P.tt(w2[:], pW[3][:, 0:128], mk2[:, 1, :], ALU.mult, reads=["pW3", "mk2"], writes=["w2"])
                    P.tt(w1[:], w1[:], w2[:], ALU.add, reads=["w1", "w2"], writes=["w1"])
                    P.stt(W[:, g, :], ident[:], Dsp[:, g:g + 1], w1[:], ALU.mult, ALU.add, reads=["ident", "Dsp", "w1"], writes=["W"])
                P.emit()
            import os as _os
            _p4 = int(_os.environ.get("P4UPTO", "9"))
            if _p4 < 1:
                return
            with ExitStack() as ph:
                P = Prog(nc, C.st, "q4b", C.dsems)
                sb, ps = mk(nc, ph, "q4b")
                ucs = [sb(f"ucs{i}", [128, 32, 128]) for i in range(2)]
                Ust = [sb(f"Ust{i}", [128, 32, 128]) for i in range(2)]
                Sst = [sb("Sst0", [128, 128, 2, 32])] * 2
                pU = [ps(f"pU{i}", [128, 512]) for i in range(2)]; pS_ = [ps(f"pS{i}", [128, 512]) for i in range(2)]
                for st_ in range(NCH // 128):
                    b = st_ % 2; c0 = st_ * 128
                    P.dma("sp", ucs[b][:], C.uc_d[c0:c0 + 128, :].rearrange("c (g x) -> c g x", g=32), key=f"ucs{b}", writes=[f"ucs{b}"])
                    for g in range(32):
                        pu = pU[(g // 4) % 2]; pun = f"pU{(g // 4) % 2}"
                        P.tr(pu[:, (g % 4) * 128:(g % 4 + 1) * 128], ucs[b][:, g, :], ident[:],
                             reads=[f"ucs{b}", "ident"], writes=[pun])
                        if g % 4 == 3:
                            P.act(Ust[b][:, g - 3:g + 1, :].rearrange("p g c -> p (g c)"), pu[:, :], AF.Copy, reads=[pun],
                                  writes=[f"Ust{b}"])
                    P.dma("sp", C.U_d[:, :, c0:c0 + 128], Ust[b][:], key=f"Ust{b}", reads=[f"Ust{b}"])
                    for g in range(32):
                        p_ = pS_[g % 2]; pn = f"pS{g % 2}"
                        for ri in range(2):
                            P.mm(p_[:, ri * 128:(ri + 1) * 128], XS[:, g, ri, :], Ust[b][:, g, :], start=True, stop=True,
                                 reads=["XS", f"Ust{b}"], writes=[pn])
                        P.act(Sst[b][:, :, :, g], p_[:, 0:256].rearrange("p (r c) -> p c r", r=2), AF.Copy, reads=[pn],
                              writes=["Sst"])
                    P.dma("sp", C.S_d[:, c0:c0 + 128, :], Sst[b][:].rearrange("p c r g -> p c (r g)"), key="Sst",
                          reads=["Sst"])
                P.emit()
        if _p4 < 2:
            return
        with ExitStack() as ph:
            sb, ps = mk(nc, ph, "q4c")
            Hs = [sb(f"Hs{i}", [128, 66, 128]) for i in range(2)]
            Sb = [sb(f"Sb{i}", [128, CS, 64]) for i in range(2)]
            mt = [sb(f"mt{d}", [128, 2, 64]) for d in range(2)]; sm = [sb(f"sm{d}", [128, 64]) for d in range(2)]
            fin = sb("fin", [128, 64])
            nS = (LS // 8) // CS
            stages = [("s", i * CS, (nS - 1 - i) * CS, i) for i in range(nS)]
            stages += [("p", LS // 8 + i * CS, LS // 8 + i * CS, i) for i in range(LP // 8 // CS)]
            P = Prog(nc, C.st, "q4c", C.dsems)
            sb3, ps3 = mk(nc, ph, "p3")
            gen_attn = attn_ops(C, P, sb3, ps3)

            def chain_ops():
              for si, (kind, cf, cb_, idx) in enumerate(stages):
                yield from chain_stage(si, kind, cf, cb_, idx)

            def chain_stage(si, kind, cf, cb_, idx):
                hb = si % 2; H = Hs[hb]; Hp = Hs[1 - hb]; S_ = Sb[hb]
                P.dma("sp", S_[0:64], C.S_d[0:64, cf:cf + CS, :], key=f"Sb{hb}f", writes=[f"Sb{hb}_0"])
                P.dma("sp", S_[64:128], C.S_d[64:128, cb_:cb_ + CS, :], key=f"Sb{hb}b", writes=[f"Sb{hb}_1"])
                engs = ("dve", "pool")
                nseq = 1 if kind == "s" else 2
                L_ = CS // nseq
                for k in range(nseq):
                    base = 33 * k if kind == "p" else 0
                    for d in range(2):
                        rs = slice(d * 64, (d + 1) * 64); hn = f"Hs{hb}_{d}"
                        slot0 = base if d == 0 else base + L_
                        if kind == "s" and idx > 0:
                            src = Hp[rs, CS if d == 0 else 0]; srn = f"Hs{1 - hb}_{d}"
                        elif kind == "s":
                            src = h0x[rs]; srn = "h0x"
                        else:
                            src = zer[rs]; srn = "zer"
                        P.copy(H[rs, slot0], src, reads=[srn], writes=[hn], eng=engs[d])
                    import os as _os2
                    P.relaxed = bool(_os2.environ.get("CHAIN_RELAXED"))
                    for j_ in range(L_):
                        for opi in range(3):
                            for d in range(2):
                                rs = slice(d * 64, (d + 1) * 64); hn = f"Hs{hb}_{d}"
                                eng = engs[d]
                                m = j_ if d == 0 else L_ - 1 - j_
                                sl_in = base + m if d == 0 else base + m + 1
                                sl_out = base + m + 1 if d == 0 else base + m
                                cl = k * L_ + m
                                if opi == 0:
                                    prev = H[rs, sl_in, 0:64]
                                    P.tt(mt[d][rs], mkap(AB[rs, 0:64], [[64, 2], [1, 64]]), mkap(prev, [[32, 2], [1, 64]]), ALU.mult,
                                         reads=["AB", hn], writes=[f"mt{d}"], eng=eng)
                                elif opi == 1:
                                    P.tt(sm[d][rs], mt[d][rs, 0, :], mt[d][rs, 1, :], ALU.add, reads=[f"mt{d}"], writes=[f"sm{d}"], eng=eng)
                                else:
                                    P.tt(H[rs, sl_out].rearrange("p (a b) -> p a b", a=2), mkap(sm[d][rs, :], [[0, 2], [1, 64]]),
                                         mkap(S_[rs, cl, :], [[0, 2], [1, 64]]), ALU.add, reads=[f"sm{d}", f"Sb{hb}_{d}"], writes=[hn], eng=eng)
                        yield
                    P.relaxed = False
                    for d in range(2):
                        rs = slice(d * 64, (d + 1) * 64); hn = f"Hs{hb}_{d}"
                        if kind == "p":
                            seq = idx * 2 + k
                            slf = base + L_ if d == 0 else base
                            P.copy(fin[rs], H[rs, slf, 0:64], reads=[hn], writes=[f"fin{d}"], eng=engs[d])
                            for ri, dst in ((0, C.new_sre), (1, C.new_sim)):
                                P.dma("sp", dst[seq, d].rearrange("g n -> n g"), fin[rs, ri * 32:(ri + 1) * 32],
                                      key=f"finst{d}", reads=[f"fin{d}"], allow_slow_non_contiguous=True)
                        cg0 = (cf if d == 0 else cb_) + k * L_
                        sl0 = base if d == 0 else base + 1
                        P.dma("sp", C.Hin_d[rs, cg0:cg0 + L_, :], H[rs, sl0:sl0 + L_, 0:64], key=f"Hin{hb}{d}", reads=[hn])

            gen_chain = chain_ops()
            import os as _os3
            alive = [not _os3.environ.get("NOATTN"), not _os3.environ.get("NOCHAIN")]
            n_attn = 0
            while alive[0] or alive[1]:
                for _ in range(3):
                    if alive[0]:
                        try:
                            next(gen_attn)
                        except StopIteration:
                            alive[0] = False
                for _ in range(2):
                    if alive[1]:
                        try:
                            next(gen_chain)
                        except StopIteration:
                            alive[1] = False
            P.emit()
        if _p4 < 3:
            return
        with ExitStack() as ph:
            P = Prog(nc, C.st, "q4d", C.dsems)
            sb, ps = mk(nc, ph, "q4d")
            Ust = [sb(f"Ust{i}", [128, 32, 128]) for i in range(2)]
            Hin = [sb(f"Hin{i}", [128, 128, 64]) for i in range(2)]
            xg = [sb(f"xg{i}", [128, 512]) for i in range(3)]; sg = [sb(f"sg{i}", [128, 512]) for i in range(2)]
            og = [sb(f"og{i}", [128, 512]) for i in range(2)]
            tm = sb("tm", [128, 8, 512]); sTs = [sb("sTs0", [128, 4, 1024], BF16)] * 2
            pYy = [ps(f"pY{i}", [128, 512]) for i in range(2)]; pZ = [ps(f"pZ{i}", [128, 512]) for i in range(2)]
            pR = [ps(f"pR{i}", [128, 512]) for i in range(2)]; pQ = [ps(f"pQ{i}", [128, 512]) for i in range(2)]
            NST = NCH // 128

            def ld(it):
                st_, gq = divmod(it, 8)
                if gq == 0:
                    b = st_ % 2; c0 = st_ * 128
                    P.dma("sp", Ust[b][:], C.U_d[:, :, c0:c0 + 128], key=f"dUst{b}", writes=[f"Ust{b}"])
                    P.dma("sp", Hin[b][:], C.Hin_d[:, c0:c0 + 128, :], key=f"dHin{b}", writes=[f"Hin{b}"])

            def sA(it):
                st_, gq = divmod(it, 8); b = st_ % 2; x = it % 2; py = pYy[x]
                for gi in range(4):
                    g = gq * 4 + gi; cs_ = slice(gi * 128, (gi + 1) * 128)
                    P.mm(py[:, cs_], W[:, g, :], Ust[b][:, g, :], start=True, stop=False, reads=["W", f"Ust{b}"], writes=[f"pY{x}"])
                    P.mm(py[:, cs_], YS[:, g, 0, :], Hin[b][:, :, g], start=False, stop=False, reads=["YS", f"Hin{b}"], writes=[f"pY{x}"])
                    P.mm(py[:, cs_], YS[:, g, 1, :], Hin[b][:, :, 32 + g], start=False, stop=True, reads=["YS", f"Hin{b}"], writes=[f"pY{x}"])

            def sB(it):
                x = it % 2
                P.act(xg[it % 3][:], pYy[x][:, :], AF.Gelu, reads=[f"pY{x}"], writes=[f"xg{it % 3}"])

            def sC(it):
                st_, gq = divmod(it, 8); x = it % 2
                for gi in range(4):
                    g = gq * 4 + gi; cs_ = slice(gi * 128, (gi + 1) * 128)
                    P.mm(pZ[x][:, cs_], WG[:, g, :], xg[it % 3][:, cs_], start=True, stop=True, reads=["WG", f"xg{it % 3}"], writes=[f"pZ{x}"])

            def sD(it):
                x = it % 2
                P.act(sg[x][:], pZ[x][:, :], AF.Sigmoid, reads=[f"pZ{x}"], writes=[f"sg{x}"])

            def sE(it):
                x = it % 2
                P.tt(og[x][:], xg[it % 3][:], sg[x][:], ALU.mult, reads=[f"xg{it % 3}", f"sg{x}"], writes=[f"og{x}"])

            def sF(it):
                x = it % 2
                for gi in range(4):
                    cs_ = slice(gi * 128, (gi + 1) * 128)
                    P.tr(pR[x][:, cs_], og[x][:, cs_], ident[:], reads=[f"og{x}", "ident"], writes=[f"pR{x}"])

            def sG(it):
                st_, gq = divmod(it, 8); x = it % 2; b = st_ % 2; c0 = st_ * 128
                for gi in range(4):
                    g = gq * 4 + gi
                    P.ts(tm[:, :, g * 16:(g + 1) * 16], pR[x][:, gi * 128:(gi + 1) * 128].rearrange("c (t q) -> c t q", t=8), 1.0, None,
                         ALU.mult, reads=[f"pR{x}"], writes=["tm"])
                if gq == 7:
                    n = 0
                    for t in range(8):
                        for ct in range(4):
                            pq = pQ[n % 2]; n += 1
                            P.tr(pq[:, 0:128], tm[:, t, ct * 128:(ct + 1) * 128], ident[:], reads=["tm", "ident"], writes=[f"pQ{(n - 1) % 2}"])
                            P.act(sTs[b][:, ct, t:1024:8], pq[:, 0:128], AF.Copy, reads=[f"pQ{(n - 1) % 2}"], writes=["sTs"])
                    P.dma("sp", C.sT_d[:, c0 * 8:c0 * 8 + 1024].rearrange("(c p) t -> p c t", p=128), sTs[b][:], key="sTs",
                          reads=["sTs"])
            pipeline(NST * 8, [ld, sA, sB, sC, sD, sE, sF, sG])
            P.emit()


def phase4_old(C):
    nc = C.nc
    with ExitStack() as outer:
        sbo, pso = mk(nc, outer, "p4")
        BTm = sbo("BTm", [128, 32, 2, 128]); CTm = sbo("CTm", [128, 32, 2, 128])
        AA = sbo("AA", [128, 2, 32]); BB = sbo("BB", [128, 2, 32]); h0 = sbo("h0", [128, 3, 32])
        zero3 = sbo("zero3", [128, 3, 32]); Dcol = sbo("Dcol", [128, 4])
        with ExitStack() as ph:
            P = Prog(nc, C.st, "p4a", C.dsems)
            sb, ps = mk(nc, ph, "p4a")
            lA = sb("lA", [128, 3, 32])
            for i, nm in enumerate(("lamA_re", "lamA_im", "ldtA")):
                P.dma("sp", lA[:, i, :], getattr(C, nm), key="lA", writes=["lA"])
            are, aim, _, _, rr = zoh(P, sb, lA[:, 0, :], lA[:, 1, :], lA[:, 2, :], 32, "A", ["lA"])
            P.copy(AA[:, 0, :], are[:], reads=rr, writes=["AA"]); P.copy(AA[:, 1, :], are[:], reads=rr, writes=["AA"])
            P.ts(BB[:, 0, :], aim[:], -1.0, None, ALU.mult, reads=rr, writes=["BB"])
            P.copy(BB[:, 1, :], aim[:], reads=rr, writes=["BB"])
            P.dma("sp", h0[:, 0, :], C.h0A_re, key="h0", writes=["h0"])
            P.dma("sp", h0[:, 1, :], C.h0A_im, key="h0", writes=["h0"])
            P.dma("sp", h0[:, 2, :], C.h0A_re, key="h0", writes=["h0"])
            P.memset(zero3[:], 0.0, writes=["zero3"])
            P.dma("sp", Dcol[:], C.Dcol, key="Dcol", writes=["Dcol"])
            P.dma("sp", CTm[:, :, 0, :], C.CTpad_re, key="CTm", writes=["CTm"])
            P.dma("sp", CTm[:, :, 1, :], C.CTpad_im, key="CTm", writes=["CTm"])
            P.ts(CTm[:, :, 1, :], CTm[:, :, 1, :], -1.0, None, ALU.mult, reads=["CTm"], writes=["CTm"])
            lB = sb("lB", [128, 3, 1024]); Bp = sb("Bp", [128, 2, 8, 128]); u1 = sb("u1", [128, 1024]); u2 = sb("u2", [128, 1024])
            for gt in range(4):
                for i, nm in enumerate(("lamB_re", "lamB_im", "ldtB")):
                    P.dma("sp", lB[:, i, :], getattr(C, nm)[gt * 1024:(gt + 1) * 1024].partition_broadcast(128), key="lB",
                          writes=["lB"])
                P.dma("sp", Bp[:, 0], C.BpadT_re[:, gt * 8:(gt + 1) * 8, :], key="Bp", writes=["Bp"])
                P.dma("sp", Bp[:, 1], C.BpadT_im[:, gt * 8:(gt + 1) * 8, :], key="Bp", writes=["Bp"])
                _, _, cre, cim, rr = zoh(P, sb, lB[:, 0, :], lB[:, 1, :], lB[:, 2, :], 1024, "B", ["lB"])
                bre = Bp[:, 0].rearrange("p g n -> p (g n)"); bim = Bp[:, 1].rearrange("p g n -> p (g n)")
                o_re = BTm[:, gt * 8:(gt + 1) * 8, 0, :]; o_im = BTm[:, gt * 8:(gt + 1) * 8, 1, :]
                c3 = lambda t: t[:].rearrange("p (g n) -> p g n", g=8)
                P.tt(u1[:], cre[:], bre, ALU.mult, reads=rr + ["Bp"], writes=["u1"])
                P.tt(u2[:], cim[:], bim, ALU.mult, reads=rr + ["Bp"], writes=["u2"])
                P.tt(o_re, c3(u1), c3(u2), ALU.subtract, reads=["u1", "u2"], writes=["BTm"])
                P.tt(u1[:], cre[:], bim, ALU.mult, reads=rr + ["Bp"], writes=["u1"])
                P.tt(u2[:], cim[:], bre, ALU.mult, reads=rr + ["Bp"], writes=["u2"])
                P.tt(o_im, c3(u1), c3(u2), ALU.add, reads=["u1", "u2"], writes=["BTm"])
            P.emit()
        H = [sbo(f"H{i}", [128, TB, 3, 32]) for i in range(2)]
        Bu = [sbo(f"Bu{i}", [128, TB, 2, 32]) for i in range(2)]
        uT = [[sbo(f"uT{d}{i}", [128, 4, TB]) for i in range(2)] for d in range(2)]
        yo = [[sbo(f"yo{d}{i}", [128, 4, TB]) for i in range(2)] for d in range(2)]
        m1 = [sbo(f"m1{d}", [128, 2, 32]) for d in range(2)]; m2 = [sbo(f"m2{d}", [128, 2, 32]) for d in range(2)]
        fin = sbo("fin", [128, 2, 32])
        pB = [pso(f"pB{i}", [128, 512]) for i in range(2)]
        pYd = [[pso(f"pY{d}{i}", [128, 512]) for i in range(3)] for d in range(2)]
        seqs = [(0, LS, True, -1)] + [(LS + 256 * i, 256, False, i) for i in range(4)]
        stage = 0
        for (s0, L, sample, pi) in seqs:
            nb = L // TB
            BPP = 16
            for b0 in range(0, nb, BPP):
                P = Prog(nc, C.st, f"p4s{s0}_{b0}", C.dsems)
                for bi in range(b0, min(nb, b0 + BPP)):
                    hb = stage % 2; stage += 1
                    Hc = H[hb]; Hp = H[1 - hb]
                    blk = (bi, nb - 1 - bi)
                    for d in range(2):
                        t0 = s0 + blk[d] * TB
                        P.dma("sp", uT[d][hb][:], C.uT_d[:, t0:t0 + TB].rearrange("(c p) t -> p c t", p=128),
                              key=f"uT{d}{hb}", writes=[f"uT{d}{hb}"])
                    for g in range(32):
                        pb = pB[g % 2]
                        for ri in range(2):
                            for d in range(2):
                                P.mm(pb[d * 64:(d + 1) * 64, ri * TB:(ri + 1) * TB], BTm[:, g, ri, d * 64:(d + 1) * 64],
                                     uT[d][hb][:, g // 8, :], start=True, stop=True, reads=["BTm", f"uT{d}{hb}"],
                                     writes=[f"pB{g % 2}"])
                        P.act(Bu[hb][:, :, :, g], pb[:, 0:2 * TB].rearrange("p (r t) -> p t r", r=2), AF.Copy,
                              reads=[f"pB{g % 2}"], writes=[f"Bu{hb}"])
                    for d, eng in ((0, "dve"), (1, "pool")):
                        rs = slice(d * 64, (d + 1) * 64)
                        for st_ in range(TB):
                            t = st_ if d == 0 else TB - 1 - st_
                            if st_ == 0:
                                if bi == 0:
                                    prev = (h0 if sample else zero3)[rs]; pn = "h0"
                                else:
                                    prev = Hp[rs, TB - 1 if d == 0 else 0]; pn = f"H{1 - hb}_{d}"
                            else:
                                prev = Hc[rs, t - 1 if d == 0 else t + 1]; pn = f"H{hb}_{d}"
                            hn = f"H{hb}_{d}"
                            P.tt(m1[d][rs], AA[rs], prev[:, 0:2, :], ALU.mult, reads=["AA", pn, "zero3"], writes=[f"m1{d}"], eng=eng)
                            P.tt(m2[d][rs], BB[rs], prev[:, 1:3, :], ALU.mult, reads=["BB", pn, "zero3"], writes=[f"m2{d}"], eng=eng)
                            P.tt(m1[d][rs], m1[d][rs], m2[d][rs], ALU.add, reads=[f"m1{d}", f"m2{d}"], writes=[f"m1{d}"], eng=eng)
                            P.tt(Hc[rs, t, 0:2, :], m1[d][rs], Bu[hb][rs, t], ALU.add, reads=[f"m1{d}", f"Bu{hb}"], writes=[hn], eng=eng)
                            P.copy(Hc[rs, t, 2, :], Hc[rs, t, 0, :], reads=[hn], writes=[hn], eng=eng)
                    for d in range(2):
                        rs = slice(d * 64, (d + 1) * 64)
                        t0 = s0 + blk[d] * TB
                        for ct in range(4):
                            py = pYd[d][ct % 3]; pyn = f"pY{d}{ct % 3}"
                            n = 0
                            for gl in range(8):
                                for ri in range(2):
                                    g = ct * 8 + gl
                                    P.mm(py[:, d * TB:(d + 1) * TB], CTm[rs, g, ri, :], Hc[rs, :, ri, g], start=(n == 0),
                                         stop=(n == 15), reads=["CTm", f"H{hb}_{d}"], writes=[pyn])
                                    n += 1
                            if d == 0:
                                P.stt(yo[d][hb][:, ct, :], uT[d][hb][:, ct, :], Dcol[:, ct:ct + 1], py[:, 0:TB], ALU.mult, ALU.add,
                                      reads=[f"uT{d}{hb}", "Dcol", pyn], writes=[f"yo{d}{hb}"])
                            else:
                                P.ts(yo[d][hb][:, ct, :], py[:, TB:2 * TB], 1.0, None, ALU.mult, reads=[pyn], writes=[f"yo{d}{hb}"])
                        dst = (C.yf_d if d == 0 else C.yb_d)[:, t0:t0 + TB].rearrange("(c p) t -> p c t", p=128)
                        P.dma("sp", dst, yo[d][hb][:], key=f"yst{d}{hb}", reads=[f"yo{d}{hb}"], writes=["y_d"])
                    if (not sample) and bi == nb - 1:
                        P.copy(fin[0:64], Hc[0:64, TB - 1, 0:2, :], reads=[f"H{hb}_0"], writes=["fin"])
                        P.copy(fin[64:128], Hc[64:128, 0, 0:2, :], reads=[f"H{hb}_1"], writes=["fin"], eng="pool")
                        for d in range(2):
                            for ri, dst in ((0, C.new_sre), (1, C.new_sim)):
                                P.dma("sp", dst[pi, d].rearrange("g n -> n g"), fin[d * 64:(d + 1) * 64, ri, :],
                                      key="finst", reads=["fin"], writes=["new_s"], allow_slow_non_contiguous=True)
                P.emit()
        with ExitStack() as ph:
            P = Prog(nc, C.st, "p4c", C.dsems)
            sb, ps = mk(nc, ph, "p4c")
            wg = sb("wg", [128, 4, 128])
            P.dma("sp", wg[:], C.wglu_bd, key="wg", writes=["wg"])
            yf = [sb(f"yf{i}", [128, 4, 512]) for i in range(2)]; yb = [sb(f"yb{i}", [128, 4, 512]) for i in range(2)]
            sg = sb("sg", [128, 512]); so = [sb(f"so{i}", [128, 4, 512], BF16) for i in range(2)]
            pz = pB[0:2]
            for b in range(NT // 512):
                s = b % 2; t0 = b * 512
                P.dma("sp", yf[s][:], C.yf_d[:, t0:t0 + 512].rearrange("(c p) t -> p c t", p=128), key=f"yf{s}", writes=[f"yf{s}"])
                P.dma("sp", yb[s][:], C.yb_d[:, t0:t0 + 512].rearrange("(c p) t -> p c t", p=128), key=f"yb{s}", writes=[f"yb{s}"])
                P.tt(yf[s][:], yf[s][:], yb[s][:], ALU.add, reads=[f"yf{s}", f"yb{s}"], writes=[f"yf{s}"])
                P.act(yf[s][:], yf[s][:], AF.Gelu, reads=[f"yf{s}"], writes=[f"yf{s}"])
                for ct in range(4):
                    P.mm(pz[ct % 2][:, :], wg[:, ct, :], yf[s][:, ct, :], start=True, stop=True, reads=["wg", f"yf{s}"],
                         writes=[f"pz{ct % 2}"])
                    P.act(sg[:], pz[ct % 2][:, :], AF.Sigmoid, reads=[f"pz{ct % 2}"], writes=["sg"])
                    P.tt(so[s][:, ct, :], yf[s][:, ct, :], sg[:], ALU.mult, reads=[f"yf{s}", "sg"], writes=[f"so{s}"])
                P.dma("sp", C.sT_d[:, t0:t0 + 512].rearrange("(c p) t -> p c t", p=128), so[s][:], key=f"sost{s}",
                      reads=[f"so{s}"], writes=["sT_d"])
            P.emit()


H2COLS = LS + 2 + 4 * 258


def h2col(tok):
    if tok < LS:
        return 1 + tok
    i, r = divmod(tok - LS, 256)
    return LS + 2 + i * 258 + 1 + r


def phase5(C):
    nc = C.nc
    with ExitStack() as ph:
        P = Prog(nc, C.st, "p5", C.dsems)
        sb, ps = mk(nc, ph, "p5")
        ident = sb("ident", [128, 128]); onesb = sb("onesb", [128, 2], BF16)
        P.dma("sp", ident[:], C.ident, key="ident5", writes=["ident"])
        P.memset(onesb[:], 1.0, writes=["onesb"])
        msc, modcol = load_modcols(C, P, sb, C.g_norm2, 24, 32, "b")
        wo = sb("wo", [128, 8, D], BF16); stg = [sb(f"stg{i}", [128, D]) for i in range(2)]
        gcol = sb("gocol", [128, 8])
        P.dma("sp", gcol[:], C.gout_col, key="gocol", writes=["gocol"])
        for kt in range(8):
            s = kt % 2
            P.dma("sp", stg[s][:], C.w_outx[kt * 128:(kt + 1) * 128, :], key=f"stg{s}", writes=[f"stg{s}"])
            P.ts(wo[:, kt, :], stg[s][:], gcol[:, kt:kt + 1], None, ALU.mult, reads=[f"stg{s}", "gocol"], writes=["wo"])
        g1 = sb("g1", [128, 2, D])
        for c in range(2):
            P.dma("sp", g1[:, c, :], C.mod_d[c, 2048:3072].partition_broadcast(128), key="g1", writes=["g1"])
        zc = sb("zc", [128, 8, 1], BF16)
        P.memset(zc[:], 0.0, writes=["zc"])
        pads = [0, LS + 1] + [LS + 2 + i * 258 for i in range(4)] + [LS + 2 + i * 258 + 257 for i in range(4)]
        for pc in pads:
            P.dma("sp", C.h2T_d[:, pc:pc + 1].rearrange("(k p) t -> p k t", p=128), zc[:], key="zc", reads=["zc"],
                  allow_slow_non_contiguous=True)
        NA = 3
        am = [sb(f"am{i}", [128, 8, 512], BF16) for i in range(NA)]
        sq = [sb(f"sq{i}", [128, 8, 128], BF16) for i in range(2)]
        xs = [sb(f"x{i}", [128, D]) for i in range(3)]
        t1 = [sb(f"t1{i}", [128, D]) for i in range(2)]; t2 = [sb(f"t2{i}", [128, D]) for i in range(2)]
        NM = 6
        xm = [sb(f"xm{i}", [128, D]) for i in range(NM)]
        junk = sb("junk", [128, D]); xn = [sb(f"xn{i}", [128, D]) for i in range(2)]
        r2 = [sb(f"r2{i}", [128, 4]) for i in range(8)]; ss = [sb(f"ss{i}", [128, 1]) for i in range(8)]
        rstd = [sb(f"rstd{i}", [128, 1]) for i in range(8)]
        hT = [sb(f"hT{i}", [128, 8, 512], BF16) for i in range(2)]
        pA = ps("pA", [128, 1024]); pS = ps("pS", [128, 1024]); pq = ps("pq", [128, 512]); pT = ps("pT", [128, 1024])

        def grp(i):
            if i < LS // 128:
                return i // 4, i % 4, 4
            return LS // 512 + (i - LS // 128) // 2, (i - LS // 128) % 2, 2
        cond_of = lambda ti: 0 if ti < LS // 128 else 1

        def s0(i):
            tk = i * 128; a_ = am[(i // 4) % NA]; an = f"am{(i // 4) % NA}"; pos = i % 4
            if pos == 0:
                P.dma("sp", a_[:, 0:4, :], C.aT_d[:, tk:tk + 512].rearrange("(j p) t -> p j t", p=128), key=an, writes=[an])
                P.dma("sp", a_[:, 4:8, :], C.sT_d[:, tk:tk + 512].rearrange("(j p) t -> p j t", p=128), key=an, writes=[an])
            P.act(sq[i % 2][:], a_[:, :, pos * 128:(pos + 1) * 128], AF.Square, reads=[an], writes=[f"sq{i % 2}"])

        def s1(i):
            for part in range(2):
                for kt in range(4):
                    P.mm(pq[:, part * 2:part * 2 + 2], sq[i % 2][:, part * 4 + kt, :], onesb[:], start=(kt == 0), stop=(kt == 3),
                         reads=[f"sq{i % 2}", "onesb"], writes=["pq"])

        def s2(i):
            P.ts(r2[i % 8][:], pq[:, 0:4], 1.0 / 512, EPS, ALU.mult, ALU.add, reads=["pq"], writes=[f"r2{i % 8}"])

        def s3(i):
            P.act(r2[i % 8][:], r2[i % 8][:], AF.Sqrt, reads=[f"r2{i % 8}"], writes=[f"r2{i % 8}"])

        def s4(i):
            P.recip(r2[i % 8][:], r2[i % 8][:], reads=[f"r2{i % 8}"], writes=[f"r2{i % 8}"])

        def s5(i):
            tk = i * 128; a_ = am[(i // 4) % NA]; an = f"am{(i // 4) % NA}"; pos = i % 4
            P.dma("sp", xs[i % 3][:], C.x_all[tk:tk + 128, :], key=f"x5{i % 3}", writes=[f"x{i % 3}"])
            for part, pp, pn in ((0, pA, "pA"), (1, pS, "pS")):
                for hf in range(2):
                    for kt in range(4):
                        P.mm(pp[:, hf * 512:(hf + 1) * 512], a_[:, part * 4 + kt, pos * 128:(pos + 1) * 128], wo[:, part * 4 + kt, hf * 512:(hf + 1) * 512],
                             start=(kt == 0), stop=(kt == 3), reads=[an, "wo"], writes=[f"{pn}{hf}"])

        def s6(i):
            r = r2[i % 8]; rn = f"r2{i % 8}"
            P.act(t1[i % 2][:], pA[:, :], AF.Copy, reads=["pA0", "pA1", rn], writes=[f"t1{i % 2}"], scale=r[:, 0:1])
            P.ts(t2[i % 2][:], pS[:, :], r[:, 2:3], None, ALU.mult, reads=["pS0", "pS1", rn], writes=[f"t2{i % 2}"])

        def s7(i):
            cond = cond_of(i); tk = i * 128; x_ = xm[i % NM]; xn_ = f"xm{i % NM}"
            P.tt(t2[i % 2][:], t2[i % 2][:], t1[i % 2][:], ALU.add, reads=[f"t1{i % 2}", f"t2{i % 2}"], writes=[f"t2{i % 2}"])
            P.tt(t2[i % 2][:], t2[i % 2][:], g1[:, cond, :], ALU.mult, reads=[f"t2{i % 2}", "g1"], writes=[f"t2{i % 2}"])
            P.tt(x_[:], xs[i % 3][:], t2[i % 2][:], ALU.add, reads=[f"x{i % 3}", f"t2{i % 2}"], writes=[xn_])
            P.dma("sp", C.xmid_d[tk:tk + 128, :], x_[:], key=f"xmst{i % NM}", reads=[xn_])

        def s8(i):
            P.act(junk[:], xm[i % NM][:], AF.Square, reads=[f"xm{i % NM}"], writes=["junk", f"ss{i % 8}"], accum_out=ss[i % 8][:])

        def s9(i):
            P.ts(ss[i % 8][:], ss[i % 8][:], 1.0 / D, EPS, ALU.mult, ALU.add, reads=[f"ss{i % 8}"], writes=[f"ss{i % 8}"])

        def s10(i):
            P.act(ss[i % 8][:], ss[i % 8][:], AF.Sqrt, reads=[f"ss{i % 8}"], writes=[f"ss{i % 8}"])

        def s11(i):
            P.recip(rstd[i % 8][:], ss[i % 8][:], reads=[f"ss{i % 8}"], writes=[f"rstd{i % 8}"])
            P.ts(xn[i % 2][:], xm[i % NM][:], rstd[i % 8][:, 0:1], None, ALU.mult, reads=[f"xm{i % NM}", f"rstd{i % 8}"],
                 writes=[f"xn{i % 2}"])

        def s12(i):
            for k in range(8):
                P.tr(pT[:, k * 128:(k + 1) * 128], xn[i % 2][:, k * 128:(k + 1) * 128], ident[:], reads=[f"xn{i % 2}", "ident"],
                     writes=[f"pT_{k // 4}"])

        def s13(i):
            cond = cond_of(i); gid, pos, gsz = grp(i); h = hT[gid % 2]; hn_ = f"hT{gid % 2}"
            for k in range(8):
                P.act(h[:, k, pos * 128:(pos + 1) * 128], pT[:, k * 128:(k + 1) * 128], AF.Identity,
                      reads=[f"pT_{k // 4}", "msc0", "msc1", "modcol0", "modcol1"], writes=[hn_], scale=msc[:, cond, k:k + 1],
                      bias=modcol[:, cond, 24 + k:25 + k])
            if pos == gsz - 1:
                c0 = h2col((i - pos) * 128)
                P.dma("sp", C.h2T_d[:, c0:c0 + gsz * 128].rearrange("(k p) t -> p k t", p=128), h[:, :, 0:gsz * 128], key=f"h2st{gid % 2}",
                      reads=[hn_])
        pipeline(NQT, [s0, s1, s2, s3, s4, s5, s6, s7, s8, s9, s10, s11, s12, s13])
        P.emit()


def phase6(C):
    nc = C.nc
    with ExitStack() as ph:
        P = Prog(nc, C.st, "p6", C.dsems)
        sb, ps = mk(nc, ph, "p6")
        wup = sb("wup", [128, 8, 5632], BF16); wdn = sb("wdn", [128, 22, D], BF16)
        t1 = sb("t1", [128, D]); junk = sb("junk", [128, D])
        stg = [t1, junk]; stn = ["t1", "junk"]
        cw = sb("cw", [128, 3, 44]); cb = sb("cb", [128, 44])
        P.dma("sp", cw[:], C.convw_col, key="cw", writes=["cw"]); P.dma("sp", cb[:], C.convb_col, key="cb", writes=["cb"])
        g2 = sb("g2", [128, 2, D]); gf = sb("gf", [128, D])
        for c in range(2):
            P.dma("sp", g2[:, c, :], C.mod_d[c, 5120:6144].partition_broadcast(128), key="g2", writes=["g2"])
        P.dma("sp", gf[:], C.g_final.partition_broadcast(128), key="gf", writes=["gf"])
        NBM = 256
        h2 = [sb(f"h2{i}", [128, 8, NBM + 2], BF16) for i in range(1)]
        up = [sb(f"up{i}", [128, NBM + 2]) for i in range(3)]
        acc = [[sb(f"acc{hv}{i}", [128, NBM]) for i in range(2)] for hv in range(2)]
        actT = sb("actT", [128, 22, NBM], BF16)
        xm = [sb(f"xm{i}", [128, D]) for i in range(2)]
        ss = sb("ss", [128, 1]); rstd = sb("rstd", [128, 1])
        pu = [ps(f"pu{i}", [128, 512]) for i in range(3)]; phh = [ps(f"phh{i}", [128, 512]) for i in range(2)]
        pd = ps("pd", [128, 1024])
        def load_weights():
            n = 0
            for q in (0, 4, 1, 5, 2, 6, 3, 7):
                for k in range(8):
                    s = n % 2; n += 1
                    P.dma("sp", stg[s][:, 0:704], C.w_up[k * 128:(k + 1) * 128, q * 704:(q + 1) * 704], key=f"stg6{s}", writes=[stn[s]])
                    P.copy(wup[:, k, q * 704:(q + 1) * 704], stg[s][:, 0:704], reads=[stn[s]], writes=[f"wup{q}"], eng="pool")
                yield
            for j in range(22):
                s = n % 2; n += 1
                P.dma("sp", stg[s][:, 0:D], C.w_down[j * 128:(j + 1) * 128, :], key=f"stg6{s}", writes=[stn[s]])
                P.copy(wdn[:, j, :], stg[s][:, 0:D], reads=[stn[s]], writes=["wdn"], eng="pool")
                yield

        hsel = sb("hsel", [128, 2])
        P.dma("sp", hsel[:], C.halfsel, key="hsel", writes=["hsel"])
        g2x = sb("g2x", [128, D])
        P.ts(g2x[:], g2[:, 0, :], hsel[:, 0:1], None, ALU.mult, reads=["g2", "hsel"], writes=["g2x"])
        P.stt(g2x[:], g2[:, 1, :], hsel[:, 1:2], g2x[:], ALU.mult, ALU.add, reads=["g2", "hsel", "g2x"], writes=["g2x"])
        h2c = [sb(f"h2c{i}", [128, 8, 258], BF16) for i in range(2)]
        xmc = [sb(f"xmc{i}", [128, D]) for i in range(2)]
        NBLK = NT // 256 // 2
        blocks = [(256 * v, 256 * (v + NBLK)) for v in range(NBLK)]
        iu = 0; ix = 0
        for bi, (tA, tB) in enumerate(blocks):
            s = 0; NB = 256
            condB = 0 if tB < LS else 1
            for ci, tX in enumerate((tA, tB)):
                c0 = h2col(tX) - 1
                P.dma("sp", h2c[ci][:], C.h2T_d[:, c0:c0 + NB + 2].rearrange("(k p) t -> p k t", p=128), key=f"h2c{ci}", writes=[f"h2c{ci}"])
            P.ts(h2[s][:, :, 0:NB + 2], h2c[0][:], hsel[:, 0:1], None, ALU.mult, reads=["h2c0", "hsel"], writes=[f"h2{s}"])
            P.stt(h2[s][:, :, 0:NB + 2], h2c[1][:], hsel[:, 1:2], h2[s][:, :, 0:NB + 2], ALU.mult, ALU.add,
                  reads=["h2c1", "hsel", f"h2{s}"], writes=[f"h2{s}"])
            if bi == 0:
                wgen = load_weights()
                next(wgen); next(wgen)
            for j in range(22):
                if bi == 0:
                    next(wgen, None)
                a2 = j % 2
                for hv in range(2):
                    m = j + 22 * hv
                    u = iu % 3; hb_ = iu % 2; hcol = 0; iu += 1
                    wr = sorted({f"wup{(m * 128) // 704}", f"wup{((m + 1) * 128 - 1) // 704}"})
                    for k in range(8):
                        P.mm(pu[u][:, 0:NB], wup[:, k, m * 128:(m + 1) * 128], h2[s][:, k, 1:NB + 1], start=(k == 0), stop=(k == 7),
                             reads=wr + [f"h2{s}"], writes=[f"pu{u}"])
                    for k in range(8):
                        P.mm(phh[hb_][:, hcol:hcol + 2], wup[:, k, m * 128:(m + 1) * 128], h2[s][:, k, 0:NB + 2:NB + 1], start=(k == 0),
                             stop=(k == 7), reads=wr + [f"h2{s}"], writes=[f"phh{hb_}"])
                    ac = acc[hv][a2]; acn = f"acc{hv}{a2}"
                    P.act(ac[:, 0:NB], pu[u][:, 0:NB], AF.Identity, reads=[f"pu{u}", "cw", "cb"], writes=[acn],
                          scale=cw[:, 1, m:m + 1], bias=cb[:, m:m + 1])
                    P.act(up[u][:, 1:NB + 1], pu[u][:, 0:NB], AF.Copy, reads=[f"pu{u}"], writes=[f"up{u}"])
                    P.act(up[u][:, 0:NB + 2:NB + 1], phh[hb_][:, hcol:hcol + 2], AF.Copy, reads=[f"phh{hb_}"], writes=[f"up{u}"])
                    P.stt(ac[:, 0:NB], up[u][:, 0:NB], cw[:, 0, m:m + 1], ac[:, 0:NB], ALU.mult, ALU.add,
                          reads=[f"up{u}", "cw", acn], writes=[acn])
                    P.stt(ac[:, 0:NB], up[u][:, 2:NB + 2], cw[:, 2, m:m + 1], ac[:, 0:NB], ALU.mult, ALU.add,
                          reads=[f"up{u}", "cw", acn], writes=[acn])
                P.act(acc[0][a2][:, 0:NB], acc[0][a2][:, 0:NB], AF.Silu, reads=[f"acc0{a2}"], writes=[f"acc0{a2}"])
                P.tt(actT[:, j, 0:NB], acc[0][a2][:, 0:NB], acc[1][a2][:, 0:NB], ALU.mult, reads=[f"acc0{a2}", f"acc1{a2}"],
                     writes=["actT"], eng="pool")
            if bi == 0:
                for _ in wgen:
                    pass
            for i in range(NB // 128):
                x = ix % 2; ix += 1
                tkA = tA + i * 128; tkB = tB + i * 128
                P.dma("sp", xmc[0][:], C.xmid_d[tkA:tkA + 128, :], key="xmc0", writes=["xmc0"])
                P.dma("sp", xmc[1][:], C.xmid_d[tkB:tkB + 128, :], key="xmc1", writes=["xmc1"])
                P.tt(xm[x][:], xmc[0][:], mkap(hsel[:, 0:1], [[0, D]]), ALU.mult, reads=["xmc0", "hsel"], writes=[f"xm{x}"], eng="pool")
                P.tt(xmc[1][:], xmc[1][:], mkap(hsel[:, 1:2], [[0, D]]), ALU.mult, reads=["xmc1", "hsel"], writes=["xmc1"], eng="pool")
                P.tt(xm[x][:], xm[x][:], xmc[1][:], ALU.add, reads=[f"xm{x}", "xmc1"], writes=[f"xm{x}"], eng="pool")
                for hf in range(2):
                    for j in range(22):
                        P.mm(pd[:, hf * 512:(hf + 1) * 512], actT[:, j, i * 128:(i + 1) * 128], wdn[:, j, hf * 512:(hf + 1) * 512],
                             start=(j == 0), stop=(j == 21), reads=["actT", "wdn"], writes=[f"pd{hf}"])
                gsel = g2[:, 0, :] if condB == 0 else g2x[:]
                P.tt(t1[:], pd[:, :], gsel, ALU.mult, reads=["pd0", "pd1", "g2", "g2x"], writes=["t1"])
                P.tt(t1[:], t1[:], xm[x][:], ALU.add, reads=["t1", f"xm{x}"], writes=["t1"])
                P.act(junk[:], t1[:], AF.Square, reads=["t1"], writes=["junk", "ss"], accum_out=ss[:])
                P.ts(ss[:], ss[:], 1.0 / D, EPS, ALU.mult, ALU.add, reads=["ss"], writes=["ss"])
                P.act(ss[:], ss[:], AF.Sqrt, reads=["ss"], writes=["ss"])
                P.recip(rstd[:], ss[:], reads=["ss"], writes=["rstd"])
                P.stt(xm[x][:], t1[:], rstd[:, 0:1], gf[:], ALU.mult, ALU.mult, reads=["t1", "rstd", "gf"], writes=[f"xm{x}"])
                P.dma("sp", C.y_out[tkA:tkA + 128, :], xm[x][:], key=f"yost{x}", reads=[f"xm{x}"])
                P.dma("sp", C.y_out[tkB:tkB + 128, :], xm[x][:], key=f"yost{x}", reads=[f"xm{x}"])
        P.emit()


_NC_CACHE = {}


def kernel(**inp):
    inp = {k: np.asarray(v) for k, v in inp.items()}
    if "nc" not in _NC_CACHE:
        _NC_CACHE["nc"] = build()
    nc = _NC_CACHE["nc"]
    in_maps = [host_inputs(inp, c) for c in range(8)]
    res = run_bass_kernel_spmd(nc, in_maps, core_ids=list(range(8)))
    yp = np.zeros((16, 256, D), np.float32); ysm = np.zeros((4, LS, D), np.float32)
    nk = np.zeros((16, 1, 256, 2, 64), np.float32); nv = np.zeros_like(nk)
    sre = np.zeros((16, 1, 2, 32, 64), np.float32); sim = np.zeros_like(sre)
    for b in range(4):
        r = res.results[b]
        yfull = np.concatenate([r["y_out"][:NT // 2], res.results[b + 4]["y_out"][NT // 2:]], 0)
        ysm[b] = yfull[:LS]; yp[4 * b:4 * b + 4] = yfull[LS:].reshape(4, 256, D)
        nk[4 * b:4 * b + 4, 0] = r["new_k"].reshape(4, 256, 2, 64); nv[4 * b:4 * b + 4, 0] = r["new_v"].reshape(4, 256, 2, 64)
        sre[4 * b:4 * b + 4, 0] = r["new_sre"]; sim[4 * b:4 * b + 4, 0] = r["new_sim"]
    return (yp, ysm, nk, nv, sre, sim)
```

```python
import numpy as np
from contextlib import ExitStack
import concourse.bass as bass
import concourse.mybir as mybir
from concourse.bass_utils import run_bass_kernel_spmd

F32 = mybir.dt.float32
BF16 = mybir.dt.bfloat16
AF = mybir.ActivationFunctionType
ALU = mybir.AluOpType


class Prog:
    ENGS = ("pe", "act", "dve", "pool", "sp")

    def __init__(self, nc, stack, tag, dma_sems):
        self.nc = nc
        self.tag = tag
        self.ops = {e: [] for e in self.ENGS}
        if "__esem" not in dma_sems:
            dma_sems["__esem"] = {e: stack.enter_context(nc.semaphore(f"eng_{e}")) for e in self.ENGS}
            dma_sems["__ecnt"] = {e: 0 for e in self.ENGS}
        self.esem = dma_sems["__esem"]
        self.ecnt = dma_sems["__ecnt"]
        self.dma_sems = dma_sems
        self.stack = stack
        self.lastw = {}
        self.readers = {}
        self.waited = {}
        self.used_keys = set()
        self.keymap = {}

    def _deps(self, eng, reads, writes):
        toks = []
        relaxed = getattr(self, "relaxed", False)
        for r in reads:
            t = self.lastw.get(r)
            if t is not None:
                toks.append(t)
        for w in writes:
            t = self.lastw.get(w)
            if t is not None:
                toks.append(t)
            toks.extend(self.readers.get(w, ()))
        waits = {}
        for (sem, val, teng) in toks:
            if teng == "pe" and eng == "pe":
                continue
            if relaxed and teng == eng:
                continue
            k = (eng, id(sem))
            if self.waited.get(k, 0) >= val:
                continue
            if waits.get(id(sem), (None, 0))[1] < val:
                waits[id(sem)] = (sem, val)
        for (sem, val) in waits.values():
            self.waited[(eng, id(sem))] = val
        return list(waits.values())

    def _commit(self, tok, reads, writes):
        for r in reads:
            self.readers.setdefault(r, []).append(tok)
        for w in writes:
            self.lastw[w] = tok
            self.readers[w] = []

    def op(self, eng, fn, reads=(), writes=()):
        reads = tuple(reads); writes = tuple(writes)
        waits = self._deps(eng, reads, writes)
        self.ecnt[eng] += 1
        tok = (self.esem[eng], self.ecnt[eng], eng)
        self.ops[eng].append((waits, fn, (self.esem[eng], 1)))
        self._commit(tok, reads, writes)

    def dma(self, eng, out, in_, key, reads=(), writes=(), **kw):
        reads = tuple(reads); writes = tuple(w for w in writes if not (w.endswith("_d") or w.startswith("new_") or w.startswith("y_")))
        waits = self._deps(eng, reads, writes)
        pool = self.dma_sems.setdefault("__pool", [])
        if key not in self.keymap:
            idx = len(self.keymap)
            if idx >= len(pool):
                pool.append([self.stack.enter_context(self.nc.semaphore(f"dq_{idx}")), 0])
            self.keymap[key] = pool[idx]
        ent = self.keymap[key]
        ent[1] += 16
        self.used_keys.add(key)
        tok = (ent[0], ent[1], "dma")
        fn = (lambda e, o=out, i=in_, kw=kw: e.dma_start(out=o, in_=i, **kw))
        self.ops[eng].append((waits, fn, (ent[0], 16)))
        self._commit(tok, reads, writes)

    def mm(self, out, lhsT, rhs, start=True, stop=True, reads=(), writes=(), **kw):
        self.op("pe", lambda e: e.matmul(out, lhsT, rhs, start=start, stop=stop, **kw), reads, writes)

    def tr(self, out, in_, ident, reads=(), writes=()):
        self.op("pe", lambda e: e.transpose(out, in_, ident), reads, writes)

    def act(self, out, in_, func, reads=(), writes=(), eng="act", **kw):
        self.op(eng, lambda e: e.activation(out=out, in_=in_, func=func, **kw), reads, writes)

    def tt(self, out, a, b, op, reads=(), writes=(), eng="dve"):
        self.op(eng, lambda e: e.tensor_tensor(out=out, in0=a, in1=b, op=op), reads, writes)

    def ts(self, out, a, s1, s2, op0, op1=None, reads=(), writes=(), eng="dve"):
        if op1 is None:
            self.op(eng, lambda e: e.tensor_scalar(out=out, in0=a, scalar1=s1, scalar2=None, op0=op0), reads, writes)
        else:
            self.op(eng, lambda e: e.tensor_scalar(out=out, in0=a, scalar1=s1, scalar2=s2, op0=op0, op1=op1), reads, writes)

    def stt(self, out, in0, scalar, in1, op0, op1, reads=(), writes=()):
        self.op("dve", lambda e: e.scalar_tensor_tensor(out=out, in0=in0, scalar=scalar, in1=in1, op0=op0, op1=op1), reads, writes)

    def copy(self, out, in_, reads=(), writes=(), eng="dve"):
        self.op(eng, lambda e: e.tensor_copy(out=out, in_=in_), reads, writes)

    def memset(self, ap, val, writes=(), eng="dve"):
        self.op(eng, lambda e: e.memset(ap, val), (), writes)

    def recip(self, out, in_, reads=(), writes=()):
        self.op("dve", lambda e: e.reciprocal(out=out, in_=in_), reads, writes)

    def emit(self, final_keys=None):
        nc = self.nc
        flush = [(self.keymap[k][0], self.keymap[k][1]) for k in sorted(self.used_keys)]
        with nc.Block() as block:
            def body(engname):
                def f(e):
                    for (waits, fn, inc) in self.ops[engname]:
                        for (sem, val) in waits:
                            e.wait_ge(sem, val)
                        fn(e).then_inc(inc[0], inc[1])
                    if engname == "sp":
                        for (sem, val) in flush:
                            e.wait_ge(sem, val)
                return f
            block.tensor(body("pe"))
            block.scalar(body("act"))
            block.vector(body("dve"))
            block.gpsimd(body("pool"))
            block.sync(body("sp"))

LS = 4096; LP = 1024; NT = LS + LP; D = 1024; EPS = 1e-6
NQT = NT // 128


class Ctx:
    pass


def build(upto=99, debug=()):
    nc = bass.Bass("TRN2", target_bir_lowering=False)
    st = ExitStack()
    dsems = {}
    C = Ctx(); C.nc = nc; C.st = st; C.dsems = dsems

    def din(name, shape, dt=F32):
        return nc.dram_tensor(name, list(shape), dt, kind="ExternalInput").ap()

    def dout(name, shape, dt=F32):
        return nc.dram_tensor(name, list(shape), dt, kind="ExternalOutput").ap()

    def dscr(name, shape, dt=F32):
        kind = "ExternalOutput" if name in debug else "Internal"
        return nc.dram_tensor(name, list(shape), dt, kind=kind).ap()
    C.din, C.dout, C.dscr = din, dout, dscr

    C.x_all = din("x_all", [NT, D])
    C.cpair = din("cpair", [2, D])
    C.w_ada = din("w_ada", [D, 6 * D])
    C.b_ada = din("b_ada", [6 * D])
    C.g_norm1 = din("g_norm1", [D])
    C.ident = din("ident", [128, 128])
    C.mod_d = dscr("mod_d", [2, 6 * D])
    C.h1T_d = dscr("h1T_d", [D, NT], BF16)

    C.w_inx = din("w_inx", [D, NWC])
    C.rope_cos = din("rope_cos", [128, LS]); C.rope_sin = din("rope_sin", [128, LS])
    C.masks = din("masks", [128, 2, 256])
    C.attn_sink = din("attn_sink", [8])
    C.cache_k = din("cache_k", [512, 128]); C.cache_v = din("cache_v", [512, 128])
    C.qT_d = dscr("qT_d", [512, NT], BF16); C.kT_d = dscr("kT_d", [128, NT], BF16)
    C.V_d = dscr("V_d", [NT, 128], BF16); C.uc_d = dscr("uc_d", [NT // 8, 4096])
    C.aT_d = dscr("aT_d", [512, NT], BF16)
    C.new_k = dout("new_k", [LP, 128]); C.new_v = dout("new_v", [LP, 128])
    for nm in ("lamA_re", "lamA_im", "ldtA", "h0A_re", "h0A_im"):
        setattr(C, nm, din(nm, [128, 32]))
    for nm in ("lamB_re", "lamB_im", "ldtB"):
        setattr(C, nm, din(nm, [4096]))
    for nm in ("BpadT_re", "BpadT_im", "CTpad_re", "CTpad_im"):
        setattr(C, nm, din(nm, [128, 32, 128]))
    C.Dcol = din("Dcol", [128, 4]); C.wglu_bd = din("wglu_bd", [128, 4, 128])
    C.sT_d = dscr("sT_d", [512, NT], BF16)
    C.U_d = dscr("U_d", [128, 32, NCH]); C.S_d = dscr("S_d", [128, NCH, 64]); C.Hin_d = dscr("Hin_d", [128, NCH, 64])
    for nm in ("B_A_re", "B_A_im", "CT_A_re", "CT_A_im"):
        setattr(C, nm, din(nm, [128, 512]))
    C.wglu_k = din("wglu_k", [128, 32, 128]); C.maskFB = din("maskFB", [128, 2, 128]); C.Dsp = din("Dsp", [128, 32])
    C.new_sre = dout("new_sre", [4, 2, 32, 64]); C.new_sim = dout("new_sim", [4, 2, 32, 64])
    C.g_norm2 = din("g_norm2", [D]); C.gout_col = din("gout_col", [128, 8]); C.w_outx = din("w_outx", [D, D])
    C.w_up = din("w_up", [D, 5632]); C.w_down = din("w_down", [2816, D])
    C.convw_col = din("convw_col", [128, 3, 44]); C.convb_col = din("convb_col", [128, 44]); C.g_final = din("g_final", [D])
    C.xmid_d = dscr("xmid_d", [NT, D]); C.h2T_d = dscr("h2T_d", [D, H2COLS], BF16)
    C.y_out = dout("y_out", [NT, D]); C.halfsel = din("halfsel", [128, 2])
    phases = [phase0, phase1, phase2, phase4, phase5, phase6]
    for i, ph in enumerate(phases):
        if i > upto:
            break
        ph(C)
    st.close()
    return nc


def mk(nc, ph, tag):
    sb = lambda name, shape, dt=F32: ph.enter_context(nc.sbuf_tensor(f"{tag}_{name}", shape, dt))
    ps = lambda name, shape, dt=F32: ph.enter_context(nc.psum_tensor(f"{tag}_{name}", shape, dt))
    return sb, ps


def phase0(C):
    nc = C.nc
    with ExitStack() as ph:
        P = Prog(nc, C.st, "p0", C.dsems)
        sb, ps = mk(nc, ph, "p0")
        cT = sb("cT", [128, 2, 8]); sT = sb("sT", [128, 2, 8])
        bada = sb("bada", [2, 6 * D]); modrow = sb("modrow", [2, 6 * D])
        wa = [sb(f"wa{i}", [128, 8, 512]) for i in range(2)]
        pm = [ps(f"pm{i}", [128, 512]) for i in range(2)]
        for c in range(2):
            P.dma("sp", cT[:, c, :], C.cpair[c].rearrange("(k p) -> p k", p=128), key="cT", writes=["cT"],
                  allow_slow_non_contiguous=True)
        P.dma("sp", bada[:], C.b_ada.partition_broadcast(2), key="bada", writes=["bada"])
        P.act(sT[:], cT[:], AF.Silu, reads=["cT"], writes=["sT"])
        for n in range(12):
            s = n % 2
            P.dma("sp", wa[s][:], C.w_ada[:, n * 512:(n + 1) * 512].rearrange("(k p) n -> p k n", p=128),
                  key=f"wa{s}", writes=[f"wa{s}"])
            for k in range(8):
                P.mm(pm[s][0:2, :], sT[:, :, k], wa[s][:, k, :], start=(k == 0), stop=(k == 7),
                     reads=["sT", f"wa{s}"], writes=[f"pm{s}"])
            P.tt(modrow[0:2, n * 512:(n + 1) * 512], pm[s][0:2, :], bada[0:2, n * 512:(n + 1) * 512], ALU.add,
                 reads=[f"pm{s}", "bada"], writes=[f"modrow{n}"])
        P.dma("sp", C.mod_d[:, :], modrow[0:2, :], key="modst", reads=[f"modrow{n}" for n in range(12)],
              writes=["mod_d"])
        P.emit()


def load_modcols(C, P, sb, gname_ap, j_shift, j_scale, tag):
    modcol = sb(f"modcol{tag}", [128, 2, 48]); gcol = sb(f"gcol{tag}", [128, 8])
    msc = sb(f"msc{tag}", [128, 2, 8])
    for c in range(2):
        P.dma("sp", modcol[:, c, :], C.mod_d[c].rearrange("(j p) -> p j", p=128), key=f"modcol{tag}{c}",
              writes=[f"modcol{c}"], allow_slow_non_contiguous=True)
    P.dma("sp", gcol[:], gname_ap.rearrange("(k p) -> p k", p=128), key=f"gcol{tag}", writes=["gcol"],
          allow_slow_non_contiguous=True)
    for c in range(2):
        P.ts(msc[:, c, :], modcol[:, c, j_scale:j_scale + 8], 1.0, None, ALU.add, reads=[f"modcol{c}"],
             writes=[f"msc{c}"])
        P.tt(msc[:, c, :], msc[:, c, :], gcol[:], ALU.mult, reads=[f"msc{c}", "gcol"], writes=[f"msc{c}"])
    return msc, modcol


def norm_to_T(C, P, xt, xres, ss, rstd, xn, junk, pT, hT, msc, msh_ap_fn, cond, ident, sfx):
    P.act(junk[:], xt, AF.Square, reads=[xres], writes=["junk" + sfx, "ss" + sfx], accum_out=ss[:])
    P.ts(ss[:], ss[:], 1.0 / D, EPS, ALU.mult, ALU.add, reads=["ss" + sfx], writes=["ss" + sfx])
    P.act(ss[:], ss[:], AF.Sqrt, reads=["ss" + sfx], writes=["ss" + sfx])
    P.recip(rstd[:], ss[:], reads=["ss" + sfx], writes=["rstd" + sfx])
    P.ts(xn[:], xt, rstd[:, 0:1], None, ALU.mult, reads=[xres, "rstd" + sfx], writes=["xn" + sfx])
    for k in range(8):
        P.tr(pT[:, k * 128:(k + 1) * 128], xn[:, k * 128:(k + 1) * 128], ident[:], reads=["xn" + sfx, "ident"],
             writes=[f"pT{sfx}{k // 4}"])
    for k in range(8):
        P.act(hT[:, k, :], pT[:, k * 128:(k + 1) * 128], AF.Identity, reads=[f"pT{sfx}{k // 4}", "msc0", "msc1", "modcol0", "modcol1"],
              writes=[f"hT{sfx}{k}"], scale=msc[:, cond, k:k + 1], bias=msh_ap_fn(cond, k))


def pipeline(n, stages):
    ns = len(stages)
    for step in range(n + ns - 1):
        for k in reversed(range(ns)):
            i = step - k
            if 0 <= i < n:
                stages[k](i)


def phase1(C):
    nc = C.nc
    with ExitStack() as ph:
        P = Prog(nc, C.st, "p1", C.dsems)
        sb, ps = mk(nc, ph, "p1")
        ident = sb("ident", [128, 128])
        P.dma("sp", ident[:], C.ident, key="ident", writes=["ident"])
        msc, modcol = load_modcols(C, P, sb, C.g_norm1, 0, 8, "a")
        NX = 6
        xs = [sb(f"x{i}", [128, D]) for i in range(NX)]
        junk = sb("junk", [128, D]); xn = [sb(f"xn{i}", [128, D]) for i in range(2)]
        ss = [sb(f"ss{i}", [128, 1]) for i in range(8)]; rstd = [sb(f"rstd{i}", [128, 1]) for i in range(8)]
        hT = [sb(f"hT{i}", [128, 8, 512], BF16) for i in range(2)]
        pT = [ps(f"pT{i}", [128, 1024]) for i in range(2)]
        cond_of = lambda ti: 0 if ti < LS // 128 else 1

        def s0(i):
            P.dma("sp", xs[i % NX][:], C.x_all[i * 128:(i + 1) * 128, :], key=f"x{i % NX}", writes=[f"x{i % NX}"])

        def s1(i):
            P.act(junk[:], xs[i % NX][:], AF.Square, reads=[f"x{i % NX}"], writes=["junk", f"ss{i % 8}"], accum_out=ss[i % 8][:])

        def s2(i):
            P.ts(ss[i % 8][:], ss[i % 8][:], 1.0 / D, EPS, ALU.mult, ALU.add, reads=[f"ss{i % 8}"], writes=[f"ss{i % 8}"])

        def s3(i):
            P.act(ss[i % 8][:], ss[i % 8][:], AF.Sqrt, reads=[f"ss{i % 8}"], writes=[f"ss{i % 8}"])

        def s4(i):
            P.recip(rstd[i % 8][:], ss[i % 8][:], reads=[f"ss{i % 8}"], writes=[f"rstd{i % 8}"])
            P.ts(xn[i % 2][:], xs[i % NX][:], rstd[i % 8][:, 0:1], None, ALU.mult, reads=[f"x{i % NX}", f"rstd{i % 8}"],
                 writes=[f"xn{i % 2}"])

        def s5(i):
            for k in range(8):
                P.tr(pT[i % 2][:, k * 128:(k + 1) * 128], xn[i % 2][:, k * 128:(k + 1) * 128], ident[:],
                     reads=[f"xn{i % 2}", "ident"], writes=[f"pT{i % 2}_{k // 4}"])

        def s6(i):
            cond = cond_of(i); hb_ = (i // 4) % 2; h = hT[hb_]; pos = i % 4
            for k in range(8):
                P.act(h[:, k, pos * 128:(pos + 1) * 128], pT[i % 2][:, k * 128:(k + 1) * 128], AF.Identity,
                      reads=[f"pT{i % 2}_{k // 4}", "msc0", "msc1", "modcol0", "modcol1"], writes=[f"hT{hb_}"],
                      scale=msc[:, cond, k:k + 1], bias=modcol[:, cond, k:k + 1])
            if pos == 3:
                P.dma("sp", C.h1T_d[:, (i - 3) * 128:(i + 1) * 128].rearrange("(k p) t -> p k t", p=128), h[:], key=f"hTst{hb_}",
                      reads=[f"hT{hb_}"])
        pipeline(NQT, [s0, s1, s2, s3, s4, s5, s6])
        P.emit()


NWC = 1920


def phase2(C):
    nc = C.nc
    with ExitStack() as ph:
        P = Prog(nc, C.st, "p2", C.dsems)
        sb, ps = mk(nc, ph, "p2")
        w = sb("w", [128, 8, NWC], BF16)
        wst = [sb(f"wst{i}", [128, 960]) for i in range(2)]
        n = 0
        for k in range(8):
            for hh in range(2):
                s = n % 2; n += 1
                P.dma("sp", wst[s][:], C.w_inx[k * 128:(k + 1) * 128, hh * 960:(hh + 1) * 960], key=f"wst{s}",
                      writes=[f"wst{s}"])
                P.copy(w[:, k, hh * 960:(hh + 1) * 960], wst[s][:], reads=[f"wst{s}"], writes=["w"])
        hT = [sb(f"hT{i}", [128, 8, 512], BF16) for i in range(2)]
        cs = [sb(f"cs{i}", [128, 2, 512]) for i in range(2)]
        t1 = sb("t1", [128, 512]); t2 = sb("t2", [128, 512])
        qo = [sb(f"qo{i}", [128, 512], BF16) for i in range(2)]
        uall = [sb(f"uall{i}", [64, 32, 8, 16]) for i in range(2)]
        vo = [sb(f"vo{i}", [128, 128], BF16) for i in range(2)]
        kvo = [sb(f"kvo{i}", [128, 256]) for i in range(2)]
        pa = [ps(f"pa{i}", [128, 512]) for i in range(4)]
        pb = [ps(f"pb{i}", [128, 512]) for i in range(2)]
        ia = 0; ib = 0; iq = 0; iu = 0; iv = 0; ikv = 0
        for b in range(NT // 512):
            s = b % 2; t0 = b * 512; sample = t0 < LS
            P.dma("sp", hT[s][:], C.h1T_d[:, t0:t0 + 512].rearrange("(k p) t -> p k t", p=128), key=f"hTl{s}",
                  writes=[f"hT{s}"])
            if sample:
                P.dma("sp", cs[s][:, 0, :], C.rope_cos[:, t0:t0 + 512], key=f"cs{s}", writes=[f"cs{s}"])
                P.dma("sp", cs[s][:, 1, :], C.rope_sin[:, t0:t0 + 512], key=f"cs{s}", writes=[f"cs{s}"])

            def proj(col0, pt):
                for k in range(8):
                    P.mm(pt[:, :], w[:, k, col0:col0 + 128], hT[s][:, k, :], start=(k == 0), stop=(k == 7),
                         reads=["w", f"hT{s}"], writes=[pt.name])
            import os as _os
            _parts = _os.environ.get("P2PARTS", "quv")
            for j in range(5 if "q" in _parts else 0):
                c0 = j * 128 if j < 4 else 1152; c1 = 512 + j * 128 if j < 4 else 1024
                p0 = pa[ia % 4]; ia += 1
                proj(c0, p0)
                o = qo[iq % 2]; on = f"qo{iq % 2}"; iq += 1
                if sample:
                    p1 = pa[ia % 4]; ia += 1
                    proj(c1, p1)
                    P.tt(t1[:], p0[:, :], cs[s][:, 0, :], ALU.mult, reads=[p0.name, f"cs{s}"], writes=["t1"])
                    P.tt(t2[:], p1[:, :], cs[s][:, 1, :], ALU.mult, reads=[p1.name, f"cs{s}"], writes=["t2"])
                    P.tt(o[:], t1[:], t2[:], ALU.add, reads=["t1", "t2"], writes=[on])
                else:
                    P.act(o[:], p0[:, :], AF.Copy, reads=[p0.name], writes=[on])
                dst = C.qT_d[j * 128:(j + 1) * 128, t0:t0 + 512] if j < 4 else C.kT_d[:, t0:t0 + 512]
                P.dma("sp", dst, o[:], key=f"qst{(iq - 1) % 2}", reads=[on], writes=["qkT_d"])
            ua = uall[b % 2]; uan = f"uall{b % 2}"
            for sx in range(8):
                p0 = pa[ia % 4]; ia += 1
                for k in range(8):
                    P.mm(p0[0:64, :], hT[s][:, k, sx:512:8], w[:, k, 1408:1920], start=(k == 0), stop=(k == 7),
                         reads=["w", f"hT{s}"], writes=[p0.name])
                P.act(ua[0:64, :, sx, :], p0[0:64, :].rearrange("c (g x) -> c g x", g=32), AF.Copy, reads=[p0.name], writes=[uan])
            P.dma("sp", C.uc_d[t0 // 8:t0 // 8 + 64, :], ua[0:64].rearrange("c g s x -> c (g s x)"), key=f"ust{b % 2}",
                  reads=[uan], writes=["uc_d"])
            for i in range(4 if ("v" in _parts and not ("S" in _parts and not sample) and not ("P" in _parts and sample)) else 0):
                tk = t0 + i * 128
                p0 = pb[ib % 2]; ib += 1
                vc = 0 if sample else 128
                for k in range(8):
                    if sample:
                        P.mm(p0[:, 0:128], hT[s][:, k, i * 128:(i + 1) * 128], w[:, k, 1280:1408], start=(k == 0),
                             stop=(k == 7), reads=["w", f"hT{s}"], writes=[p0.name])
                    else:
                        P.mm(p0[:, 0:256], hT[s][:, k, i * 128:(i + 1) * 128], w[:, k, 1152:1408], start=(k == 0),
                             stop=(k == 7), reads=["w", f"hT{s}"], writes=[p0.name])
                o = vo[iv % 2]; on = f"vo{iv % 2}"; iv += 1
                P.act(o[:], p0[:, vc:vc + 128], AF.Copy, reads=[p0.name], writes=[on])
                P.dma("sp", C.V_d[tk:tk + 128, :], o[:], key=f"vst{(iv - 1) % 2}", reads=[on], writes=["V_d"])
                if not sample:
                    o2 = kvo[ikv % 2]; on2 = f"kvo{ikv % 2}"; ikv += 1
                    if "C" not in _parts:
                        P.act(o2[:], p0[:, 0:256], AF.Copy, reads=[p0.name], writes=[on2])
                if (not sample) and "N" not in _parts:
                    P.dma("sp", C.new_k[tk - LS:tk - LS + 128, :], o2[:, 0:128], key=f"kvst{(ikv - 1) % 2}",
                          reads=[on2], writes=["new_k"])
                    P.dma("sp", C.new_v[tk - LS:tk - LS + 128, :], o2[:, 128:256], key=f"kvst{(ikv - 1) % 2}",
                          reads=[on2], writes=["new_v"])
        P.emit()


def attn_ops(C, P, sb, ps):
    if True:
        nc = C.nc
        identf = sb("identf", [128, 128]); identb = sb("identb", [128, 128], BF16); onesb = sb("onesb", [128, 128], BF16)
        P.dma("sp", identf[:], C.ident, key="identf", writes=["identf"])
        P.copy(identb[:], identf[:], reads=["identf"], writes=["identb"])
        P.memset(onesb[:], 1.0, writes=["onesb"])
        maskf = sb("maskf", [128, 2, 256]); maskb = sb("maskb", [128, 2, 256], BF16)
        P.dma("sp", maskf[:], C.masks, key="maskf", writes=["maskf"])
        P.copy(maskb[:], maskf[:], reads=["maskf"], writes=["maskb"])
        es = sb("es", [128, 8])
        P.dma("sp", es[:], C.attn_sink.partition_broadcast(128), key="es", writes=["es"])
        P.act(es[:], es[:], AF.Exp, reads=["es"], writes=["es"])
        ckf = sb("ckf", [128, 4, 128]); cvf = sb("cvf", [128, 4, 128])
        ckT = sb("ckT", [128, 512], BF16); cv = sb("cv", [128, 4, 128], BF16)
        P.dma("sp", ckf[:], C.cache_k.rearrange("(b p) d -> p b d", p=128), key="ckf", writes=["ckf"])
        P.dma("sp", cvf[:], C.cache_v.rearrange("(b p) d -> p b d", p=128), key="cvf", writes=["cvf"])
        P.copy(cv[:], cvf[:], reads=["cvf"], writes=["cv"])
        pSS = [ps(f"pS{i}", [128, 1024]) for i in range(2)]
        pO = [ps(f"pO{i}", [128, 512]) for i in range(2)]; pD = [ps(f"pD{i}", [128, 512]) for i in range(2)]
        for b in range(4):
            P.tr(pSS[0][:, b * 128:(b + 1) * 128], ckf[:, b, :], identf[:], reads=["ckf", "identf"], writes=["pS0"])
        P.act(ckT[:], pSS[0][:, 0:512], AF.Copy, reads=["pS0"], writes=["ckT"])
        kT = sb("kT", [128, LS], BF16); V = sb("V", [128, LS // 128, 128], BF16)
        qT = [sb(f"qT{i}", [128, 4, 128], BF16) for i in range(2)]
        PT = [sb(f"PT{i}", [128, 256], BF16) for i in range(3)]
        rd = [sb(f"rd{i}", [128, 256]) for i in range(2)]; aT = [sb(f"aT{i}", [128, 4, 128], BF16) for i in range(2)]
        seqs = [(0, LS, True)] + [(LS + 256 * i, 256, False) for i in range(4)]
        iq = 0
        items = []
        for (s0, L, sample) in seqs:
            nb = L // 128
            for qb in range(nb):
                if sample:
                    tiles = [("b", kb, (0 if kb < qb else (1 if kb > qb else None))) for kb in (qb - 1, qb, qb + 1)
                             if 0 <= kb < nb] + [("c", kb, None) for kb in range(4)]
                else:
                    tiles = [("b", kb, None) for kb in range(nb)]
                for j in range(4):
                    for ti, tl in enumerate(tiles):
                        items.append(dict(s0=s0, L=L, nb=nb, qb=qb, j=j, ti=ti, nt=len(tiles), tile=tl,
                                          newseq=(qb == 0 and j == 0 and ti == 0), newq=(j == 0 and ti == 0)))
        state = dict(iq=0, ipair=-1)

        def front(n, it):
            if it["newseq"]:
                P.dma("sp", kT[:, 0:it["L"]], C.kT_d[:, it["s0"]:it["s0"] + it["L"]], key="kT", writes=["kT"])
                P.dma("sp", V[:, 0:it["nb"], :], C.V_d[it["s0"]:it["s0"] + it["L"], :].rearrange("(b p) d -> p b d", p=128), key="V",
                      writes=["V"])
            if it["newq"]:
                state["iq"] += 1
                qi = state["iq"] % 2
                tq = it["s0"] + it["qb"] * 128
                P.dma("sp", qT[qi][:], C.qT_d[:, tq:tq + 128].rearrange("(j p) t -> p j t", p=128), key=f"qT{qi}", writes=[f"qT{qi}"])
            if it["ti"] == 0:
                state["ipair"] += 1
            it["qi"] = state["iq"] % 2; it["pi"] = state["ipair"] % 2
            q = qT[it["qi"]]; qn = f"qT{it['qi']}"
            kind, kb, mk_ = it["tile"]; j = it["j"]
            SS = pSS[n % 2]; SSn = f"pS{n % 2}"
            ksrc = kT if kind == "b" else ckT; ksn = "kT" if kind == "b" else "ckT"
            for (c0, r0) in ((0, 0), (512, 64)):
                first = True
                if mk_ is not None:
                    P.mm(SS[:, c0:c0 + 128], identb[:], maskb[:, mk_, 0:128], start=True, stop=False, reads=["identb", "maskb"],
                         writes=[SSn]); first = False
                P.mm(SS[:, c0:c0 + 128], ksrc[r0:r0 + 64, kb * 128:(kb + 1) * 128], q[r0:r0 + 64, j, :], start=first, stop=True,
                     reads=[ksn, qn], writes=[SSn])
            pt = PT[n % 3]
            P.act(pt[:].rearrange("p (a b) -> p a b", a=2), mkap(SS[:, 0:128], [[512, 2], [1, 128]]), AF.Exp, reads=[SSn],
                  writes=[f"PT{n % 3}"], scale=0.125)

        def back(n, it):
            kind, kb, mk_ = it["tile"]; j = it["j"]; pi = it["pi"]
            pt = PT[n % 3]; ptn = f"PT{n % 3}"
            vsrc = V[:, kb, :] if kind == "b" else cv[:, kb, :]; vsn = "V" if kind == "b" else "cv"
            if it["ti"] == 0:
                for dd in [d_ for d_ in deferred if d_[1]["pi"] == pi]:
                    deferred.remove(dd); norm(dd[1])
            P.mm(pO[pi][:, 0:256], vsrc, pt[:], start=(it["ti"] == 0), stop=(it["ti"] == it["nt"] - 1), reads=[vsn, ptn], writes=[f"pO{pi}"])
            P.mm(pD[pi][:, 0:256], onesb[:], pt[:], start=(it["ti"] == 0), stop=(it["ti"] == it["nt"] - 1), reads=["onesb", ptn],
                 writes=[f"pD{pi}"])
            if it["ti"] == it["nt"] - 1:
                deferred.append([4, it])

        deferred = []

        def norm(it):
            if True:
                j = it["j"]; pi = it["pi"]
                a = aT[it["qi"]]; an = f"aT{it['qi']}"; r_ = rd[pi]; rn = f"rd{pi}"
                P.act(r_[:, 0:128], pD[pi][:, 0:128], AF.Ln, reads=[f"pD{pi}", "es"], writes=[rn], bias=es[:, j:j + 1])
                P.act(r_[:, 128:256], pD[pi][:, 128:256], AF.Ln, reads=[f"pD{pi}", "es"], writes=[rn], bias=es[:, 4 + j:5 + j])
                P.act(r_[:], r_[:], AF.Exp, reads=[rn], writes=[rn], scale=-1.0)
                P.tt(a[0:64, j, :], pO[pi][0:64, 0:128], r_[0:64, 0:128], ALU.mult, reads=[f"pO{pi}", rn], writes=[an])
                P.tt(a[64:128, j, :], pO[pi][64:128, 128:256], r_[64:128, 128:256], ALU.mult, reads=[f"pO{pi}", rn], writes=[an])
                if j == 3:
                    tq = it["s0"] + it["qb"] * 128
                    P.dma("sp", C.aT_d[:, tq:tq + 128].rearrange("(j p) t -> p j t", p=128), a[:], key=f"ast{it['qi']}", reads=[an])

        yield
        for n in range(len(items) + 1):
            flush_first = n < len(items) and items[n]["newseq"]
            if n >= 1 and flush_first:
                back(n - 1, items[n - 1])
                while deferred:
                    norm(deferred.pop(0)[1])
            if n < len(items):
                front(n, items[n])
            if n >= 1 and not flush_first:
                back(n - 1, items[n - 1])
            for dd in deferred:
                dd[0] -= 1
            while deferred and (deferred[0][0] <= 0 or n == len(items)):
                norm(deferred.pop(0)[1])
            yield


def _const_tables():
    perm = np.concatenate([np.arange(16, 32), np.arange(0, 16), np.arange(48, 64), np.arange(32, 48)])
    t = np.arange(LS); row = (t // 64).astype(np.float32); col = (t % 64).astype(np.float32)
    inv = (10000.0 ** (-np.arange(16, dtype=np.float32) / 16)).astype(np.float32)
    ar = row[None, :] * inv[:, None]; ac = col[None, :] * inv[:, None]
    cos64 = np.concatenate([np.cos(ar), np.cos(ar), np.cos(ac), np.cos(ac)], 0)
    sin64 = np.concatenate([-np.sin(ar), np.sin(ar), -np.sin(ac), np.sin(ac)], 0)
    cos = np.concatenate([cos64, cos64], 0).astype(np.float32); sin = np.concatenate([sin64, sin64], 0).astype(np.float32)
    j = np.arange(128)[:, None]; i = np.arange(128)[None, :]
    m0 = np.where(j >= i, 0.0, -30000.0); m1 = np.where(j <= i, 0.0, -30000.0)
    masks = np.stack([np.concatenate([m0, m0], 1), np.concatenate([m1, m1], 1)], 1).astype(np.float32)
    return perm, cos, sin, masks


def host_inputs(inp, core):
    b = core % 4
    perm, cos, sin, masks = _const_tables()
    d = {}
    d["x_all"] = np.concatenate([inp["x_sample"][b], inp["x_prompt"][4 * b:4 * b + 4].reshape(LP, D)], 0)
    d["cpair"] = np.stack([inp["c"][b], inp["c_ctx"]], 0)
    d["w_ada"] = inp["w_ada"][0]; d["b_ada"] = inp["b_ada"][0]; d["g_norm1"] = inp["g_norm1"][0]
    d["ident"] = np.eye(128, dtype=np.float32)
    w = inp["w_in"][0]
    q = w[:, 0:512].reshape(D, 8, 64); k = w[:, 512:640].reshape(D, 2, 64)
    qt = np.concatenate([np.concatenate([q[:, j], q[:, 4 + j]], 1) for j in range(4)], 1)
    qp = np.concatenate([np.concatenate([q[:, j][:, perm], q[:, 4 + j][:, perm]], 1) for j in range(4)], 1)
    kt = k.reshape(D, 128); kp = k[:, :, perm].reshape(D, 128)
    d["w_inx"] = np.concatenate([qt, qp, kp, kt, w[:, 640:768], w[:, 768:1280]], 1)
    d["rope_cos"] = cos; d["rope_sin"] = sin; d["masks"] = masks
    d["attn_sink"] = inp["attn_sink"][0]
    d["cache_k"] = inp["cache_k"][b, 0].reshape(512, 128); d["cache_v"] = inp["cache_v"][b, 0].reshape(512, 128)
    tA = lambda a: a.transpose(0, 2, 1).reshape(128, 32)
    tB = lambda a: a.transpose(1, 0, 2).reshape(4096)
    d["lamA_re"] = tA(inp["ssm_lam_re"][0]); d["lamA_im"] = tA(inp["ssm_lam_im"][0])
    ldt = np.repeat(inp["ssm_log_dt"][0][:, :, None], 64, 2)
    d["ldtA"] = tA(ldt); d["lamB_re"] = tB(inp["ssm_lam_re"][0]); d["lamB_im"] = tB(inp["ssm_lam_im"][0]); d["ldtB"] = tB(ldt)
    d["h0A_re"] = tA(inp["state_ssm_re"][b, 0]); d["h0A_im"] = tA(inp["state_ssm_im"][b, 0])
    for nm, src in (("BpadT_re", "ssm_b_re"), ("BpadT_im", "ssm_b_im")):
        B = inp[src][0]
        o = np.zeros((128, 32, 128), np.float32)
        for g in range(32):
            gl = g % 8
            o[gl * 16:(gl + 1) * 16, g, :] = B[:, g].transpose(2, 0, 1).reshape(16, 128)
        d[nm] = o
    for nm, src in (("CTpad_re", "ssm_c_re"), ("CTpad_im", "ssm_c_im")):
        Cm = inp[src][0]
        o = np.zeros((128, 32, 128), np.float32)
        for g in range(32):
            gl = g % 8
            o[:, g, gl * 16:(gl + 1) * 16] = Cm[:, g].transpose(0, 2, 1).reshape(128, 16)
        d[nm] = o
    d["Dcol"] = inp["ssm_d"][0].reshape(4, 128).T
    for nm, src in (("B_A_re", "ssm_b_re"), ("B_A_im", "ssm_b_im")):
        d[nm] = inp[src][0].transpose(0, 2, 1, 3).reshape(128, 512)
    for nm, src in (("CT_A_re", "ssm_c_re"), ("CT_A_im", "ssm_c_im")):
        d[nm] = inp[src][0].transpose(0, 3, 1, 2).reshape(128, 512)
    wk = np.zeros((128, 32, 128), np.float32)
    for s_ in range(8):
        wk[s_ * 16:(s_ + 1) * 16, :, s_ * 16:(s_ + 1) * 16] = inp["ssm_w_glu"][0].transpose(1, 0, 2)
    d["wglu_k"] = wk
    sp = np.arange(128) // 16
    d["maskFB"] = np.stack([(sp[:, None] <= sp[None, :]), (sp[:, None] >= sp[None, :])], 1).astype(np.float32)
    d["Dsp"] = np.tile(inp["ssm_d"][0].T, (8, 1))
    wg = np.zeros((128, 4, 128), np.float32)
    for g in range(32):
        gl = g % 8
        wg[gl * 16:(gl + 1) * 16, g // 8, gl * 16:(gl + 1) * 16] = inp["ssm_w_glu"][0][g]
    d["wglu_bd"] = wg
    d["g_norm2"] = inp["g_norm2"][0]; d["g_final"] = inp["g_final"]
    d["halfsel"] = np.tile(np.array([[1.0, 0.0]] if core < 4 else [[0.0, 1.0]], np.float32), (128, 1))
    wo = inp["w_out"][0]
    tp = lambda a: np.concatenate([np.concatenate([a[j * 64:(j + 1) * 64], a[(4 + j) * 64:(5 + j) * 64]], 0) for j in range(4)], 0)
    d["w_outx"] = np.concatenate([tp(wo[0:512]), wo[512:1024]], 0)
    gcat = np.concatenate([tp(inp["g_out_attn"][0]), inp["g_out_ssm"][0]], 0)
    d["gout_col"] = gcat.reshape(8, 128).T
    d["w_up"] = inp["w_up"][0]; d["w_down"] = inp["w_down"][0]
    d["convw_col"] = inp["conv_w"][0].reshape(3, 44, 128).transpose(2, 0, 1)
    d["convb_col"] = inp["conv_b"][0].reshape(44, 128).T
    return {k_: np.ascontiguousarray(v, dtype=np.float32) for k_, v in d.items()}


TB = 64
PI = float(np.pi)


def zoh(P, sb, lre, lim, ldt, n, tag, rd):
    cache = zoh.__dict__.setdefault("cache", {})
    def T(nm):
        k = (id(P.nc), tag, nm)
        if k not in cache:
            cache[k] = sb(f"z{tag}_{nm}", [128, n])
        return cache[k]
    dt = T("dt"); mag = T("mag"); ang = T("ang"); s = T("s"); c = T("c"); are = T("are"); aim = T("aim")
    den = T("den"); t1 = T("t1"); t2 = T("t2"); cre = T("cre"); cim = T("cim"); nre = T("nre")
    R = lambda *x: [f"z{tag}_{i}" for i in x]
    P.act(dt[:], ldt, AF.Exp, reads=rd, writes=R("dt"))
    P.tt(mag[:], lre, dt[:], ALU.mult, reads=rd + R("dt"), writes=R("mag"))
    P.act(mag[:], mag[:], AF.Exp, reads=R("mag"), writes=R("mag"))
    P.tt(ang[:], lim, dt[:], ALU.mult, reads=rd + R("dt"), writes=R("ang"))
    P.act(c[:], ang[:], AF.Sin, reads=R("ang"), writes=R("c"), scale=1.0 / 16)
    P.act(s[:], ang[:], AF.Sin, reads=R("ang"), writes=R("s"), scale=1.0 / 8)
    P.tt(c[:], c[:], c[:], ALU.mult, reads=R("c"), writes=R("c"))
    P.ts(c[:], c[:], -2.0, 1.0, ALU.mult, ALU.add, reads=R("c"), writes=R("c"))
    for _ in range(3):
        P.tt(t1[:], s[:], s[:], ALU.mult, reads=R("s"), writes=R("t1"))
        P.stt(s[:], s[:], 2.0, c[:], ALU.mult, ALU.mult, reads=R("s", "c"), writes=R("s"))
        P.ts(c[:], t1[:], -2.0, 1.0, ALU.mult, ALU.add, reads=R("t1"), writes=R("c"))
    P.tt(are[:], mag[:], c[:], ALU.mult, reads=R("mag", "c"), writes=R("are"))
    P.tt(aim[:], mag[:], s[:], ALU.mult, reads=R("mag", "s"), writes=R("aim"))
    P.tt(den[:], lre, lre, ALU.mult, reads=rd, writes=R("den"))
    P.tt(t1[:], lim, lim, ALU.mult, reads=rd, writes=R("t1"))
    P.tt(den[:], den[:], t1[:], ALU.add, reads=R("den", "t1"), writes=R("den"))
    P.recip(den[:], den[:], reads=R("den"), writes=R("den"))
    P.ts(nre[:], are[:], -1.0, None, ALU.add, reads=R("are"), writes=R("nre"))
    P.tt(t1[:], nre[:], lre, ALU.mult, reads=R("nre") + rd, writes=R("t1"))
    P.tt(t2[:], aim[:], lim, ALU.mult, reads=R("aim") + rd, writes=R("t2"))
    P.tt(t1[:], t1[:], t2[:], ALU.add, reads=R("t1", "t2"), writes=R("t1"))
    P.tt(cre[:], t1[:], den[:], ALU.mult, reads=R("t1", "den"), writes=R("cre"))
    P.tt(t1[:], aim[:], lre, ALU.mult, reads=R("aim") + rd, writes=R("t1"))
    P.tt(t2[:], nre[:], lim, ALU.mult, reads=R("nre") + rd, writes=R("t2"))
    P.tt(t1[:], t1[:], t2[:], ALU.subtract, reads=R("t1", "t2"), writes=R("t1"))
    P.tt(cim[:], t1[:], den[:], ALU.mult, reads=R("t1", "den"), writes=R("cim"))
    return are, aim, cre, cim, R("are", "aim", "cre", "cim")


NCH = NT // 8
CS = 64


def mkap(base, dims):
    return bass.AP(base.tensor, base.offset, [list(base.ap[0])] + [list(d) for d in dims])


def phase4(C):
    nc = C.nc
    with ExitStack() as outer:
        sbo, pso = mk(nc, outer, "q4")
        W = sbo("W", [128, 32, 128]); YS = sbo("YS", [128, 32, 2, 128]); WG = sbo("WG", [128, 32, 128])
        AB = sbo("AB", [128, 128]); h0x = sbo("h0x", [128, 128]); zer = sbo("zer", [128, 128])
        ident = sbo("ident", [128, 128])
        with ExitStack() as mid:
            sbm, _ = mk(nc, mid, "q4m")
            XS = sbm("XS", [128, 32, 2, 128])
            with ExitStack() as ph:
                P = Prog(nc, C.st, "q4a", C.dsems)
                sb, ps = mk(nc, ph, "q4a")
                P.dma("sp", ident[:], C.ident, key="q4ident", writes=["ident"])
                P.dma("sp", WG[:], C.wglu_k, key="WG", writes=["WG"])
                lA = sb("lA", [128, 3, 32])
                for i, nm in enumerate(("lamA_re", "lamA_im", "ldtA")):
                    P.dma("sp", lA[:, i, :], getattr(C, nm), key="lA2", writes=["lA"])
                are, aim, cre, cim, rr = zoh(P, sb, lA[:, 0, :], lA[:, 1, :], lA[:, 2, :], 32, "A2", ["lA"])
                tA = sb("tA", [128, 32]); tB = sb("tB", [128, 32]); rm2 = sb("rm2", [128, 32])

                def cmul(ore, oim, xr, xi, yr, yi, reads, writes, neg_im=False):
                    P.tt(tA[:], xr, yr, ALU.mult, reads=reads, writes=["tA"])
                    P.tt(tB[:], xi, yi, ALU.mult, reads=reads, writes=["tB"])
                    P.tt(ore, tA[:], tB[:], ALU.subtract, reads=["tA", "tB"], writes=writes)
                    P.tt(tA[:], xr, yi, ALU.mult, reads=reads, writes=["tA"])
                    P.tt(tB[:], xi, yr, ALU.mult, reads=reads, writes=["tB"])
                    P.tt(oim, tA[:], tB[:], ALU.add, reads=["tA", "tB"], writes=writes)
                    if neg_im:
                        P.ts(oim, oim, -1.0, None, ALU.mult, reads=writes, writes=writes)
                PW = sb("PW", [128, 9, 2, 32]); PIv = sb("PIv", [128, 8, 2, 32])
                P.memset(PW[:, 0, 0, :], 1.0, writes=["PW"]); P.memset(PW[:, 0, 1, :], 0.0, writes=["PW"])
                P.memset(PIv[:, 0, 0, :], 1.0, writes=["PIv"]); P.memset(PIv[:, 0, 1, :], 0.0, writes=["PIv"])
                P.copy(PW[:, 1, 0, :], are[:], reads=rr, writes=["PW"]); P.copy(PW[:, 1, 1, :], aim[:], reads=rr, writes=["PW"])
                P.tt(rm2[:], are[:], are[:], ALU.mult, reads=rr, writes=["rm2"])
                P.tt(tA[:], aim[:], aim[:], ALU.mult, reads=rr, writes=["tA"])
                P.tt(rm2[:], rm2[:], tA[:], ALU.add, reads=["rm2", "tA"], writes=["rm2"])
                P.recip(rm2[:], rm2[:], reads=["rm2"], writes=["rm2"])
                P.tt(PIv[:, 1, 0, :], are[:], rm2[:], ALU.mult, reads=rr + ["rm2"], writes=["PIv"])
                P.tt(PIv[:, 1, 1, :], aim[:], rm2[:], ALU.mult, reads=rr + ["rm2"], writes=["PIv"])
                P.ts(PIv[:, 1, 1, :], PIv[:, 1, 1, :], -1.0, None, ALU.mult, reads=["PIv"], writes=["PIv"])
                for k in range(2, 9):
                    cmul(PW[:, k, 0, :], PW[:, k, 1, :], PW[:, k - 1, 0, :], PW[:, k - 1, 1, :], PW[:, 1, 0, :], PW[:, 1, 1, :],
                         ["PW"], ["PW"])
                for k in range(2, 8):
                    cmul(PIv[:, k, 0, :], PIv[:, k, 1, :], PIv[:, k - 1, 0, :], PIv[:, k - 1, 1, :], PIv[:, 1, 0, :],
                         PIv[:, 1, 1, :], ["PIv"], ["PIv"])
                P.copy(AB[:, 0:32], PW[:, 8, 0, :], reads=["PW"], writes=["AB"]); P.copy(AB[:, 32:64], PW[:, 8, 0, :], reads=["PW"], writes=["AB"])
                P.ts(AB[:, 64:96], PW[:, 8, 1, :], -1.0, None, ALU.mult, reads=["PW"], writes=["AB"])
                P.copy(AB[:, 96:128], PW[:, 8, 1, :], reads=["PW"], writes=["AB"])
                for blk, nm in ((0, "h0A_re"), (1, "h0A_im"), (2, "h0A_re"), (3, "h0A_im")):
                    P.dma("sp", h0x[:, blk * 32:(blk + 1) * 32], getattr(C, nm), key="h0x", writes=["h0x"])
                P.memset(zer[:], 0.0, writes=["zer"])
                tabs = {}
                for nm, ff, fb in (("PX", lambda s_: (PW, 7 - s_), lambda s_: (PW, s_)),
                                   ("PY", lambda t: (PW, t + 1), lambda t: (PW, 8 - t)),
                                   ("PE", lambda s_: (PIv, s_), lambda s_: (PW, s_)),
                                   ("PF", lambda t: (PW, t), lambda t: (PIv, t))):
                    tb = sb(nm, [128, 8, 2, 32]); tabs[nm] = tb
                    for i in range(8):
                        src, k = ff(i)
                        P.copy(tb[0:64, i], src[0:64, k], reads=["PW", "PIv"], writes=[nm])
                        src, k = fb(i)
                        P.act(tb[64:128, i], src[64:128, k], AF.Copy, reads=["PW", "PIv"], writes=[nm])
                Braw = sb("Braw", [128, 2, 512]); Bb = sb("Bb", [128, 2, 512]); Craw = sb("Craw", [128, 2, 512])
                P.dma("sp", Braw[:, 0, :], C.B_A_re, key="Braw", writes=["Braw"]); P.dma("sp", Braw[:, 1, :], C.B_A_im, key="Braw", writes=["Braw"])
                P.dma("sp", Craw[:, 0, :], C.CT_A_re, key="Craw", writes=["Craw"]); P.dma("sp", Craw[:, 1, :], C.CT_A_im, key="Craw", writes=["Craw"])
                bc = lambda t2: mkap(t2, [[1, 32], [0, 16]])
                g3 = lambda t3: t3.rearrange("p (g x) -> p g x", g=32)
                u1 = sb("u1", [128, 512]); u2 = sb("u2", [128, 512])

                def cmul_b(ore, oim, tr_, ti_, xr, xi, reads, writes, neg_im=False):
                    P.tt(g3(u1[:]), bc(tr_), g3(xr), ALU.mult, reads=reads, writes=["u1"])
                    P.tt(g3(u2[:]), bc(ti_), g3(xi), ALU.mult, reads=reads, writes=["u2"])
                    P.tt(ore, g3(u1[:]), g3(u2[:]), ALU.subtract, reads=["u1", "u2"], writes=writes)
                    P.tt(g3(u1[:]), bc(tr_), g3(xi), ALU.mult, reads=reads, writes=["u1"])
                    P.tt(g3(u2[:]), bc(ti_), g3(xr), ALU.mult, reads=reads, writes=["u2"])
                    P.tt(oim, g3(u1[:]), g3(u2[:]), ALU.add if not neg_im else ALU.add, reads=["u1", "u2"], writes=writes)
                    if neg_im:
                        P.ts(oim, oim, -1.0, None, ALU.mult, reads=writes, writes=writes)
                cmul_b(g3(Bb[:, 0, :]), g3(Bb[:, 1, :]), cre[:], cim[:], Braw[:, 0, :], Braw[:, 1, :], rr + ["Braw"], ["Bb"])
                tmpA = sb("tmpA", [128, 2, 32, 8, 16]); tmpB = sb("tmpB", [128, 2, 32, 8, 16])
                pW = [ps(f"pW{i}", [128, 512]) for i in range(4)]
                for i in range(8):
                    cmul_b(tmpA[:, 0, :, i, :], tmpA[:, 1, :, i, :], tabs["PX"][:, i, 0, :], tabs["PX"][:, i, 1, :], Bb[:, 0, :], Bb[:, 1, :],
                           ["PX", "Bb"], ["tmpA"])
                n = 0
                for g in range(32):
                    for ri in range(2):
                        pw = pW[(n // 4) % 2]; pwn = f"pW{(n // 4) % 2}"
                        P.tr(pw[:, (n % 4) * 128:(n % 4 + 1) * 128], tmpA[:, ri, g].rearrange("p s x -> p (s x)"), ident[:],
                             reads=["tmpA", "ident"], writes=[pwn])
                        n += 1
                        if n % 4 == 0:
                            g0 = g - 1
                            P.act(XS[:, g0:g0 + 2].rearrange("p g r x -> p (g r x)"), pw[:, :], AF.Copy, reads=[pwn], writes=["XS"])
                ysv = YS[:].rearrange("p g r (t q) -> p g r t q", t=8)
                for i in range(8):
                    cmul_b(ysv[:, :, 0, i, :], ysv[:, :, 1, i, :], tabs["PY"][:, i, 0, :], tabs["PY"][:, i, 1, :], Craw[:, 0, :], Craw[:, 1, :],
                           ["PY", "Craw"], ["YS"], neg_im=True)
                for i in range(8):
                    cmul_b(tmpA[:, 0, :, i, :], tmpA[:, 1, :, i, :], tabs["PE"][:, i, 0, :], tabs["PE"][:, i, 1, :], Bb[:, 0, :], Bb[:, 1, :],
                           ["PE", "Bb"], ["tmpA"])
                    cmul_b(tmpB[:, 0, :, i, :], tmpB[:, 1, :, i, :], tabs["PF"][:, i, 0, :], tabs["PF"][:, i, 1, :], Craw[:, 0, :], Craw[:, 1, :],
                           ["PF", "Craw"], ["tmpB"], neg_im=True)
                mk2 = sb("mk2", [128, 2, 128]); Dsp = sb("Dsp", [128, 32]); w1 = sb("w1", [128, 128]); w2 = sb("w2", [128, 128])
                P.dma("sp", mk2[:], C.maskFB, key="mk2", writes=["mk2"]); P.dma("sp", Dsp[:], C.Dsp, key="Dsp", writes=["Dsp"])
                fl = lambda t5, ri, g, r0: t5[r0:r0 + 64, ri, g].rearrange("p s x -> p (s x)")
                for g in range(32):
                    for d, pw, pwn in ((0, pW[2], "pW2"), (1, pW[3], "pW3")):
                        r0 = d * 64
                        P.mm(pw[:, 0:128], fl(tmpA, 0, g, r0), fl(tmpB, 0, g, r0), start=True, stop=False,
                             reads=["tmpA", "tmpB"], writes=[pwn])
                        P.mm(pw[:, 0:128], fl(tmpA, 1, g, r0), fl(tmpB, 1, g, r0), start=False, stop=True,
                             reads=["tmpA", "tmpB"], writes=[pwn])
                    P.tt(w1[:], pW[2][:, 0:128], mk2[:, 0, :], ALU.mult, reads=["pW2", "mk2"], writes=["w1"])
                    P.tt(w2[:], pW[3][:, 0:128], mk2[:, 1, :], ALU.mult, reads=["pW3", "mk2"], writes=["w2"])
                    P.tt(w1[:], w1[:], w2[:], ALU.add, reads=["w1", "w2"], writes=["w1"])
                    P.stt(W[:, g, :], ident[:], Dsp[:, g:g + 1], w1[:], ALU.mult, ALU.add, reads=["ident", "Dsp", "w1"], writes=["W"])
                P.emit()
            import os as _os
            _p4 = int(_os.environ.get("P4UPTO", "9"))
            if _p4 < 1:
                return
            with ExitStack() as ph:
                P = Prog(nc, C.st, "q4b", C.dsems)
                sb, ps = mk(nc, ph, "q4b")
                ucs = [sb(f"ucs{i}", [128, 32, 128]) for i in range(2)]
                Ust = [sb(f"Ust{i}", [128, 32, 128]) for i in range(2)]
                Sst = [sb("Sst0", [128, 128, 2, 32])] * 2
                pU = [ps(f"pU{i}", [128, 512]) for i in range(2)]; pS_ = [ps(f"pS{i}", [128, 512]) for i in range(2)]
                for st_ in range(NCH // 128):
                    b = st_ % 2; c0 = st_ * 128
                    P.dma("sp", ucs[b][:], C.uc_d[c0:c0 + 128, :].rearrange("c (g x) -> c g x", g=32), key=f"ucs{b}", writes=[f"ucs{b}"])
                    for g in range(32):
                        pu = pU[(g // 4) % 2]; pun = f"pU{(g // 4) % 2}"
                        P.tr(pu[:, (g % 4) * 128:(g % 4 + 1) * 128], ucs[b][:, g, :], ident[:],
                             reads=[f"ucs{b}", "ident"], writes=[pun])
                        if g % 4 == 3:
                            P.act(Ust[b][:, g - 3:g + 1, :].rearrange("p g c -> p (g c)"), pu[:, :], AF.Copy, reads=[pun],
                                  writes=[f"Ust{b}"])
                    P.dma("sp", C.U_d[:, :, c0:c0 + 128], Ust[b][:], key=f"Ust{b}", reads=[f"Ust{b}"])
                    for g in range(32):
                        p_ = pS_[g % 2]; pn = f"pS{g % 2}"
                        for ri in range(2):
                            P.mm(p_[:, ri * 128:(ri + 1) * 128], XS[:, g, ri, :], Ust[b][:, g, :], start=True, stop=True,
                                 reads=["XS", f"Ust{b}"], writes=[pn])
                        P.act(Sst[b][:, :, :, g], p_[:, 0:256].rearrange("p (r c) -> p c r", r=2), AF.Copy, reads=[pn],
                              writes=["Sst"])
                    P.dma("sp", C.S_d[:, c0:c0 + 128, :], Sst[b][:].rearrange("p c r g -> p c (r g)"), key="Sst",
                          reads=["Sst"])
                P.emit()
        if _p4 < 2:
            return
        with ExitStack() as ph:
            sb, ps = mk(nc, ph, "q4c")
            Hs = [sb(f"Hs{i}", [128, 66, 128]) for i in range(2)]
            Sb = [sb(f"Sb{i}", [128, CS, 64]) for i in range(2)]
            mt = [sb(f"mt{d}", [128, 2, 64]) for d in range(2)]; sm = [sb(f"sm{d}", [128, 64]) for d in range(2)]
            fin = sb("fin", [128, 64])
            nS = (LS // 8) // CS
            stages = [("s", i * CS, (nS - 1 - i) * CS, i) for i in range(nS)]
            stages += [("p", LS // 8 + i * CS, LS // 8 + i * CS, i) for i in range(LP // 8 // CS)]
            P = Prog(nc, C.st, "q4c", C.dsems)
            sb3, ps3 = mk(nc, ph, "p3")
            gen_attn = attn_ops(C, P, sb3, ps3)

            def chain_ops():
              for si, (kind, cf, cb_, idx) in enumerate(stages):
                yield from chain_stage(si, kind, cf, cb_, idx)

            def chain_stage(si, kind, cf, cb_, idx):
                hb = si % 2; H = Hs[hb]; Hp = Hs[1 - hb]; S_ = Sb[hb]
                P.dma("sp", S_[0:64], C.S_d[0:64, cf:cf + CS, :], key=f"Sb{hb}f", writes=[f"Sb{hb}_0"])
                P.dma("sp", S_[64:128], C.S_d[64:128, cb_:cb_ + CS, :], key=f"Sb{hb}b", writes=[f"Sb{hb}_1"])
                engs = ("dve", "pool")
                nseq = 1 if kind == "s" else 2
                L_ = CS // nseq
                for k in range(nseq):
                    base = 33 * k if kind == "p" else 0
                    for d in range(2):
                        rs = slice(d * 64, (d + 1) * 64); hn = f"Hs{hb}_{d}"
                        slot0 = base if d == 0 else base + L_
                        if kind == "s" and idx > 0:
                            src = Hp[rs, CS if d == 0 else 0]; srn = f"Hs{1 - hb}_{d}"
                        elif kind == "s":
                            src = h0x[rs]; srn = "h0x"
                        else:
                            src = zer[rs]; srn = "zer"
                        P.copy(H[rs, slot0], src, reads=[srn], writes=[hn], eng=engs[d])
                    import os as _os2
                    P.relaxed = bool(_os2.environ.get("CHAIN_RELAXED"))
                    for j_ in range(L_):
                        for opi in range(3):
                            for d in range(2):
                                rs = slice(d * 64, (d + 1) * 64); hn = f"Hs{hb}_{d}"
                                eng = engs[d]
                                m = j_ if d == 0 else L_ - 1 - j_
                                sl_in = base + m if d == 0 else base + m + 1
                                sl_out = base + m + 1 if d == 0 else base + m
                                cl = k * L_ + m
                                if opi == 0:
                                    prev = H[rs, sl_in, 0:64]
                                    P.tt(mt[d][rs], mkap(AB[rs, 0:64], [[64, 2], [1, 64]]), mkap(prev, [[32, 2], [1, 64]]), ALU.mult,
                                         reads=["AB", hn], writes=[f"mt{d}"], eng=eng)
                                elif opi == 1:
                                    P.tt(sm[d][rs], mt[d][rs, 0, :], mt[d][rs, 1, :], ALU.add, reads=[f"mt{d}"], writes=[f"sm{d}"], eng=eng)
                                else:
                                    P.tt(H[rs, sl_out].rearrange("p (a b) -> p a b", a=2), mkap(sm[d][rs, :], [[0, 2], [1, 64]]),
                                         mkap(S_[rs, cl, :], [[0, 2], [1, 64]]), ALU.add, reads=[f"sm{d}", f"Sb{hb}_{d}"], writes=[hn], eng=eng)
                        yield
                    P.relaxed = False
                    for d in range(2):
                        rs = slice(d * 64, (d + 1) * 64); hn = f"Hs{hb}_{d}"
                        if kind == "p":
                            seq = idx * 2 + k
                            slf = base + L_ if d == 0 else base
                            P.copy(fin[rs], H[rs, slf, 0:64], reads=[hn], writes=[f"fin{d}"], eng=engs[d])
                            for ri, dst in ((0, C.new_sre), (1, C.new_sim)):
                                P.dma("sp", dst[seq, d].rearrange("g n -> n g"), fin[rs, ri * 32:(ri + 1) * 32],
                                      key=f"finst{d}", reads=[f"fin{d}"], allow_slow_non_contiguous=True)
                        cg0 = (cf if d == 0 else cb_) + k * L_
                        sl0 = base if d == 0 else base + 1
                        P.dma("sp", C.Hin_d[rs, cg0:cg0 + L_, :], H[rs, sl0:sl0 + L_, 0:64], key=f"Hin{hb}{d}", reads=[hn])

            gen_chain = chain_ops()
            import os as _os3
            alive = [not _os3.environ.get("NOATTN"), not _os3.environ.get("NOCHAIN")]
            n_attn = 0
            while alive[0] or alive[1]:
                for _ in range(3):
                    if alive[0]:
                        try:
                            next(gen_attn)
                        except StopIteration:
                            alive[0] = False
                for _ in range(2):
                    if alive[1]:
                        try:
                            next(gen_chain)
                        except StopIteration:
                            alive[1] = False
            P.emit()
        if _p4 < 3:
            return
        with ExitStack() as ph:
            P = Prog(nc, C.st, "q4d", C.dsems)
            sb, ps = mk(nc, ph, "q4d")
            Ust = [sb(f"Ust{i}", [128, 32, 128]) for i in range(2)]
            Hin = [sb(f"Hin{i}", [128, 128, 64]) for i in range(2)]
            xg = [sb(f"xg{i}", [128, 512]) for i in range(3)]; sg = [sb(f"sg{i}", [128, 512]) for i in range(2)]
            og = [sb(f"og{i}", [128, 512]) for i in range(2)]
            tm = sb("tm", [128, 8, 512]); sTs = [sb("sTs0", [128, 4, 1024], BF16)] * 2
            pYy = [ps(f"pY{i}", [128, 512]) for i in range(2)]; pZ = [ps(f"pZ{i}", [128, 512]) for i in range(2)]
            pR = [ps(f"pR{i}", [128, 512]) for i in range(2)]; pQ = [ps(f"pQ{i}", [128, 512]) for i in range(2)]
            NST = NCH // 128

            def ld(it):
                st_, gq = divmod(it, 8)
                if gq == 0:
                    b = st_ % 2; c0 = st_ * 128
                    P.dma("sp", Ust[b][:], C.U_d[:, :, c0:c0 + 128], key=f"dUst{b}", writes=[f"Ust{b}"])
                    P.dma("sp", Hin[b][:], C.Hin_d[:, c0:c0 + 128, :], key=f"dHin{b}", writes=[f"Hin{b}"])

            def sA(it):
                st_, gq = divmod(it, 8); b = st_ % 2; x = it % 2; py = pYy[x]
                for gi in range(4):
                    g = gq * 4 + gi; cs_ = slice(gi * 128, (gi + 1) * 128)
                    P.mm(py[:, cs_], W[:, g, :], Ust[b][:, g, :], start=True, stop=False, reads=["W", f"Ust{b}"], writes=[f"pY{x}"])
                    P.mm(py[:, cs_], YS[:, g, 0, :], Hin[b][:, :, g], start=False, stop=False, reads=["YS", f"Hin{b}"], writes=[f"pY{x}"])
                    P.mm(py[:, cs_], YS[:, g, 1, :], Hin[b][:, :, 32 + g], start=False, stop=True, reads=["YS", f"Hin{b}"], writes=[f"pY{x}"])

            def sB(it):
                x = it % 2
                P.act(xg[it % 3][:], pYy[x][:, :], AF.Gelu, reads=[f"pY{x}"], writes=[f"xg{it % 3}"])

            def sC(it):
                st_, gq = divmod(it, 8); x = it % 2
                for gi in range(4):
                    g = gq * 4 + gi; cs_ = slice(gi * 128, (gi + 1) * 128)
                    P.mm(pZ[x][:, cs_], WG[:, g, :], xg[it % 3][:, cs_], start=True, stop=True, reads=["WG", f"xg{it % 3}"], writes=[f"pZ{x}"])

            def sD(it):
                x = it % 2
                P.act(sg[x][:], pZ[x][:, :], AF.Sigmoid, reads=[f"pZ{x}"], writes=[f"sg{x}"])

            def sE(it):
                x = it % 2
                P.tt(og[x][:], xg[it % 3][:], sg[x][:], ALU.mult, reads=[f"xg{it % 3}", f"sg{x}"], writes=[f"og{x}"])

            def sF(it):
                x = it % 2
                for gi in range(4):
                    cs_ = slice(gi * 128, (gi + 1) * 128)
                    P.tr(pR[x][:, cs_], og[x][:, cs_], ident[:], reads=[f"og{x}", "ident"], writes=[f"pR{x}"])

            def sG(it):
                st_, gq = divmod(it, 8); x = it % 2; b = st_ % 2; c0 = st_ * 128
                for gi in range(4):
                    g = gq * 4 + gi
                    P.ts(tm[:, :, g * 16:(g + 1) * 16], pR[x][:, gi * 128:(gi + 1) * 128].rearrange("c (t q) -> c t q", t=8), 1.0, None,
                         ALU.mult, reads=[f"pR{x}"], writes=["tm"])
                if gq == 7:
                    n = 0
                    for t in range(8):
                        for ct in range(4):
                            pq = pQ[n % 2]; n += 1
                            P.tr(pq[:, 0:128], tm[:, t, ct * 128:(ct + 1) * 128], ident[:], reads=["tm", "ident"], writes=[f"pQ{(n - 1) % 2}"])
                            P.act(sTs[b][:, ct, t:1024:8], pq[:, 0:128], AF.Copy, reads=[f"pQ{(n - 1) % 2}"], writes=["sTs"])
                    P.dma("sp", C.sT_d[:, c0 * 8:c0 * 8 + 1024].rearrange("(c p) t -> p c t", p=128), sTs[b][:], key="sTs",
                          reads=["sTs"])
            pipeline(NST * 8, [ld, sA, sB, sC, sD, sE, sF, sG])
            P.emit()


def phase4_old(C):
    nc = C.nc
    with ExitStack() as outer:
        sbo, pso = mk(nc, outer, "p4")
        BTm = sbo("BTm", [128, 32, 2, 128]); CTm = sbo("CTm", [128, 32, 2, 128])
        AA = sbo("AA", [128, 2, 32]); BB = sbo("BB", [128, 2, 32]); h0 = sbo("h0", [128, 3, 32])
        zero3 = sbo("zero3", [128, 3, 32]); Dcol = sbo("Dcol", [128, 4])
        with ExitStack() as ph:
            P = Prog(nc, C.st, "p4a", C.dsems)
            sb, ps = mk(nc, ph, "p4a")
            lA = sb("lA", [128, 3, 32])
            for i, nm in enumerate(("lamA_re", "lamA_im", "ldtA")):
                P.dma("sp", lA[:, i, :], getattr(C, nm), key="lA", writes=["lA"])
            are, aim, _, _, rr = zoh(P, sb, lA[:, 0, :], lA[:, 1, :], lA[:, 2, :], 32, "A", ["lA"])
            P.copy(AA[:, 0, :], are[:], reads=rr, writes=["AA"]); P.copy(AA[:, 1, :], are[:], reads=rr, writes=["AA"])
            P.ts(BB[:, 0, :], aim[:], -1.0, None, ALU.mult, reads=rr, writes=["BB"])
            P.copy(BB[:, 1, :], aim[:], reads=rr, writes=["BB"])
            P.dma("sp", h0[:, 0, :], C.h0A_re, key="h0", writes=["h0"])
            P.dma("sp", h0[:, 1, :], C.h0A_im, key="h0", writes=["h0"])
            P.dma("sp", h0[:, 2, :], C.h0A_re, key="h0", writes=["h0"])
            P.memset(zero3[:], 0.0, writes=["zero3"])
            P.dma("sp", Dcol[:], C.Dcol, key="Dcol", writes=["Dcol"])
            P.dma("sp", CTm[:, :, 0, :], C.CTpad_re, key="CTm", writes=["CTm"])
            P.dma("sp", CTm[:, :, 1, :], C.CTpad_im, key="CTm", writes=["CTm"])
            P.ts(CTm[:, :, 1, :], CTm[:, :, 1, :], -1.0, None, ALU.mult, reads=["CTm"], writes=["CTm"])
            lB = sb("lB", [128, 3, 1024]); Bp = sb("Bp", [128, 2, 8, 128]); u1 = sb("u1", [128, 1024]); u2 = sb("u2", [128, 1024])
            for gt in range(4):
                for i, nm in enumerate(("lamB_re", "lamB_im", "ldtB")):
                    P.dma("sp", lB[:, i, :], getattr(C, nm)[gt * 1024:(gt + 1) * 1024].partition_broadcast(128), key="lB",
                          writes=["lB"])
                P.dma("sp", Bp[:, 0], C.BpadT_re[:, gt * 8:(gt + 1) * 8, :], key="Bp", writes=["Bp"])
                P.dma("sp", Bp[:, 1], C.BpadT_im[:, gt * 8:(gt + 1) * 8, :], key="Bp", writes=["Bp"])
                _, _, cre, cim, rr = zoh(P, sb, lB[:, 0, :], lB[:, 1, :], lB[:, 2, :], 1024, "B", ["lB"])
                bre = Bp[:, 0].rearrange("p g n -> p (g n)"); bim = Bp[:, 1].rearrange("p g n -> p (g n)")
                o_re = BTm[:, gt * 8:(gt + 1) * 8, 0, :]; o_im = BTm[:, gt * 8:(gt + 1) * 8, 1, :]
                c3 = lambda t: t[:].rearrange("p (g n) -> p g n", g=8)
                P.tt(u1[:], cre[:], bre, ALU.mult, reads=rr + ["Bp"], writes=["u1"])
                P.tt(u2[:], cim[:], bim, ALU.mult, reads=rr + ["Bp"], writes=["u2"])
                P.tt(o_re, c3(u1), c3(u2), ALU.subtract, reads=["u1", "u2"], writes=["BTm"])
                P.tt(u1[:], cre[:], bim, ALU.mult, reads=rr + ["Bp"], writes=["u1"])
                P.tt(u2[:], cim[:], bre, ALU.mult, reads=rr + ["Bp"], writes=["u2"])
                P.tt(o_im, c3(u1), c3(u2), ALU.add, reads=["u1", "u2"], writes=["BTm"])
            P.emit()
        H = [sbo(f"H{i}", [128, TB, 3, 32]) for i in range(2)]
        Bu = [sbo(f"Bu{i}", [128, TB, 2, 32]) for i in range(2)]
        uT = [[sbo(f"uT{d}{i}", [128, 4, TB]) for i in range(2)] for d in range(2)]
        yo = [[sbo(f"yo{d}{i}", [128, 4, TB]) for i in range(2)] for d in range(2)]
        m1 = [sbo(f"m1{d}", [128, 2, 32]) for d in range(2)]; m2 = [sbo(f"m2{d}", [128, 2, 32]) for d in range(2)]
        fin = sbo("fin", [128, 2, 32])
        pB = [pso(f"pB{i}", [128, 512]) for i in range(2)]
        pYd = [[pso(f"pY{d}{i}", [128, 512]) for i in range(3)] for d in range(2)]
        seqs = [(0, LS, True, -1)] + [(LS + 256 * i, 256, False, i) for i in range(4)]
        stage = 0
        for (s0, L, sample, pi) in seqs:
            nb = L // TB
            BPP = 16
            for b0 in range(0, nb, BPP):
                P = Prog(nc, C.st, f"p4s{s0}_{b0}", C.dsems)
                for bi in range(b0, min(nb, b0 + BPP)):
                    hb = stage % 2; stage += 1
                    Hc = H[hb]; Hp = H[1 - hb]
                    blk = (bi, nb - 1 - bi)
                    for d in range(2):
                        t0 = s0 + blk[d] * TB
                        P.dma("sp", uT[d][hb][:], C.uT_d[:, t0:t0 + TB].rearrange("(c p) t -> p c t", p=128),
                              key=f"uT{d}{hb}", writes=[f"uT{d}{hb}"])
                    for g in range(32):
                        pb = pB[g % 2]
                        for ri in range(2):
                            for d in range(2):
                                P.mm(pb[d * 64:(d + 1) * 64, ri * TB:(ri + 1) * TB], BTm[:, g, ri, d * 64:(d + 1) * 64],
                                     uT[d][hb][:, g // 8, :], start=True, stop=True, reads=["BTm", f"uT{d}{hb}"],
                                     writes=[f"pB{g % 2}"])
                        P.act(Bu[hb][:, :, :, g], pb[:, 0:2 * TB].rearrange("p (r t) -> p t r", r=2), AF.Copy,
                              reads=[f"pB{g % 2}"], writes=[f"Bu{hb}"])
                    for d, eng in ((0, "dve"), (1, "pool")):
                        rs = slice(d * 64, (d + 1) * 64)
                        for st_ in range(TB):
                            t = st_ if d == 0 else TB - 1 - st_
                            if st_ == 0:
                                if bi == 0:
                                    prev = (h0 if sample else zero3)[rs]; pn = "h0"
                                else:
                                    prev = Hp[rs, TB - 1 if d == 0 else 0]; pn = f"H{1 - hb}_{d}"
                            else:
                                prev = Hc[rs, t - 1 if d == 0 else t + 1]; pn = f"H{hb}_{d}"
                            hn = f"H{hb}_{d}"
                            P.tt(m1[d][rs], AA[rs], prev[:, 0:2, :], ALU.mult, reads=["AA", pn, "zero3"], writes=[f"m1{d}"], eng=eng)
                            P.tt(m2[d][rs], BB[rs], prev[:, 1:3, :], ALU.mult, reads=["BB", pn, "zero3"], writes=[f"m2{d}"], eng=eng)
                            P.tt(m1[d][rs], m1[d][rs], m2[d][rs], ALU.add, reads=[f"m1{d}", f"m2{d}"], writes=[f"m1{d}"], eng=eng)
                            P.tt(Hc[rs, t, 0:2, :], m1[d][rs], Bu[hb][rs, t], ALU.add, reads=[f"m1{d}", f"Bu{hb}"], writes=[hn], eng=eng)
                            P.copy(Hc[rs, t, 2, :], Hc[rs, t, 0, :], reads=[hn], writes=[hn], eng=eng)
                    for d in range(2):
                        rs = slice(d * 64, (d + 1) * 64)
                        t0 = s0 + blk[d] * TB
                        for ct in range(4):
                            py = pYd[d][ct % 3]; pyn = f"pY{d}{ct % 3}"
                            n = 0
                            for gl in range(8):
                                for ri in range(2):
                                    g = ct * 8 + gl
                                    P.mm(py[:, d * TB:(d + 1) * TB], CTm[rs, g, ri, :], Hc[rs, :, ri, g], start=(n == 0),
                                         stop=(n == 15), reads=["CTm", f"H{hb}_{d}"], writes=[pyn])
                                    n += 1
                            if d == 0:
                                P.stt(yo[d][hb][:, ct, :], uT[d][hb][:, ct, :], Dcol[:, ct:ct + 1], py[:, 0:TB], ALU.mult, ALU.add,
                                      reads=[f"uT{d}{hb}", "Dcol", pyn], writes=[f"yo{d}{hb}"])
                            else:
                                P.ts(yo[d][hb][:, ct, :], py[:, TB:2 * TB], 1.0, None, ALU.mult, reads=[pyn], writes=[f"yo{d}{hb}"])
                        dst = (C.yf_d if d == 0 else C.yb_d)[:, t0:t0 + TB].rearrange("(c p) t -> p c t", p=128)
                        P.dma("sp", dst, yo[d][hb][:], key=f"yst{d}{hb}", reads=[f"yo{d}{hb}"], writes=["y_d"])
                    if (not sample) and bi == nb - 1:
                        P.copy(fin[0:64], Hc[0:64, TB - 1, 0:2, :], reads=[f"H{hb}_0"], writes=["fin"])
                        P.copy(fin[64:128], Hc[64:128, 0, 0:2, :], reads=[f"H{hb}_1"], writes=["fin"], eng="pool")
                        for d in range(2):
                            for ri, dst in ((0, C.new_sre), (1, C.new_sim)):
                                P.dma("sp", dst[pi, d].rearrange("g n -> n g"), fin[d * 64:(d + 1) * 64, ri, :],
                                      key="finst", reads=["fin"], writes=["new_s"], allow_slow_non_contiguous=True)
                P.emit()
        with ExitStack() as ph:
            P = Prog(nc, C.st, "p4c", C.dsems)
            sb, ps = mk(nc, ph, "p4c")
            wg = sb("wg", [128, 4, 128])
            P.dma("sp", wg[:], C.wglu_bd, key="wg", writes=["wg"])
            yf = [sb(f"yf{i}", [128, 4, 512]) for i in range(2)]; yb = [sb(f"yb{i}", [128, 4, 512]) for i in range(2)]
            sg = sb("sg", [128, 512]); so = [sb(f"so{i}", [128, 4, 512], BF16) for i in range(2)]
            pz = pB[0:2]
            for b in range(NT // 512):
                s = b % 2; t0 = b * 512
                P.dma("sp", yf[s][:], C.yf_d[:, t0:t0 + 512].rearrange("(c p) t -> p c t", p=128), key=f"yf{s}", writes=[f"yf{s}"])
                P.dma("sp", yb[s][:], C.yb_d[:, t0:t0 + 512].rearrange("(c p) t -> p c t", p=128), key=f"yb{s}", writes=[f"yb{s}"])
                P.tt(yf[s][:], yf[s][:], yb[s][:], ALU.add, reads=[f"yf{s}", f"yb{s}"], writes=[f"yf{s}"])
                P.act(yf[s][:], yf[s][:], AF.Gelu, reads=[f"yf{s}"], writes=[f"yf{s}"])
                for ct in range(4):
                    P.mm(pz[ct % 2][:, :], wg[:, ct, :], yf[s][:, ct, :], start=True, stop=True, reads=["wg", f"yf{s}"],
                         writes=[f"pz{ct % 2}"])
                    P.act(sg[:], pz[ct % 2][:, :], AF.Sigmoid, reads=[f"pz{ct % 2}"], writes=["sg"])
                    P.tt(so[s][:, ct, :], yf[s][:, ct, :], sg[:], ALU.mult, reads=[f"yf{s}", "sg"], writes=[f"so{s}"])
                P.dma("sp", C.sT_d[:, t0:t0 + 512].rearrange("(c p) t -> p c t", p=128), so[s][:], key=f"sost{s}",
                      reads=[f"so{s}"], writes=["sT_d"])
            P.emit()


H2COLS = LS + 2 + 4 * 258


def h2col(tok):
    if tok < LS:
        return 1 + tok
    i, r = divmod(tok - LS, 256)
    return LS + 2 + i * 258 + 1 + r


def phase5(C):
    nc = C.nc
    with ExitStack() as ph:
        P = Prog(nc, C.st, "p5", C.dsems)
        sb, ps = mk(nc, ph, "p5")
        ident = sb("ident", [128, 128]); onesb = sb("onesb", [128, 2], BF16)
        P.dma("sp", ident[:], C.ident, key="ident5", writes=["ident"])
        P.memset(onesb[:], 1.0, writes=["onesb"])
        msc, modcol = load_modcols(C, P, sb, C.g_norm2, 24, 32, "b")
        wo = sb("wo", [128, 8, D], BF16); stg = [sb(f"stg{i}", [128, D]) for i in range(2)]
        gcol = sb("gocol", [128, 8])
        P.dma("sp", gcol[:], C.gout_col, key="gocol", writes=["gocol"])
        for kt in range(8):
            s = kt % 2
            P.dma("sp", stg[s][:], C.w_outx[kt * 128:(kt + 1) * 128, :], key=f"stg{s}", writes=[f"stg{s}"])
            P.ts(wo[:, kt, :], stg[s][:], gcol[:, kt:kt + 1], None, ALU.mult, reads=[f"stg{s}", "gocol"], writes=["wo"])
        g1 = sb("g1", [128, 2, D])
        for c in range(2):
            P.dma("sp", g1[:, c, :], C.mod_d[c, 2048:3072].partition_broadcast(128), key="g1", writes=["g1"])
        zc = sb("zc", [128, 8, 1], BF16)
        P.memset(zc[:], 0.0, writes=["zc"])
        pads = [0, LS + 1] + [LS + 2 + i * 258 for i in range(4)] + [LS + 2 + i * 258 + 257 for i in range(4)]
        for pc in pads:
            P.dma("sp", C.h2T_d[:, pc:pc + 1].rearrange("(k p) t -> p k t", p=128), zc[:], key="zc", reads=["zc"],
                  allow_slow_non_contiguous=True)
        NA = 3
        am = [sb(f"am{i}", [128, 8, 512], BF16) for i in range(NA)]
        sq = [sb(f"sq{i}", [128, 8, 128], BF16) for i in range(2)]
        xs = [sb(f"x{i}", [128, D]) for i in range(3)]
        t1 = [sb(f"t1{i}", [128, D]) for i in range(2)]; t2 = [sb(f"t2{i}", [128, D]) for i in range(2)]
        NM = 6
        xm = [sb(f"xm{i}", [128, D]) for i in range(NM)]
        junk = sb("junk", [128, D]); xn = [sb(f"xn{i}", [128, D]) for i in range(2)]
        r2 = [sb(f"r2{i}", [128, 4]) for i in range(8)]; ss = [sb(f"ss{i}", [128, 1]) for i in range(8)]
        rstd = [sb(f"rstd{i}", [128, 1]) for i in range(8)]
        hT = [sb(f"hT{i}", [128, 8, 512], BF16) for i in range(2)]
        pA = ps("pA", [128, 1024]); pS = ps("pS", [128, 1024]); pq = ps("pq", [128, 512]); pT = ps("pT", [128, 1024])

        def grp(i):
            if i < LS // 128:
                return i // 4, i % 4, 4
            return LS // 512 + (i - LS // 128) // 2, (i - LS // 128) % 2, 2
        cond_of = lambda ti: 0 if ti < LS // 128 else 1

        def s0(i):
            tk = i * 128; a_ = am[(i // 4) % NA]; an = f"am{(i // 4) % NA}"; pos = i % 4
            if pos == 0:
                P.dma("sp", a_[:, 0:4, :], C.aT_d[:, tk:tk + 512].rearrange("(j p) t -> p j t", p=128), key=an, writes=[an])
                P.dma("sp", a_[:, 4:8, :], C.sT_d[:, tk:tk + 512].rearrange("(j p) t -> p j t", p=128), key=an, writes=[an])
            P.act(sq[i % 2][:], a_[:, :, pos * 128:(pos + 1) * 128], AF.Square, reads=[an], writes=[f"sq{i % 2}"])

        def s1(i):
            for part in range(2):
                for kt in range(4):
                    P.mm(pq[:, part * 2:part * 2 + 2], sq[i % 2][:, part * 4 + kt, :], onesb[:], start=(kt == 0), stop=(kt == 3),
                         reads=[f"sq{i % 2}", "onesb"], writes=["pq"])

        def s2(i):
            P.ts(r2[i % 8][:], pq[:, 0:4], 1.0 / 512, EPS, ALU.mult, ALU.add, reads=["pq"], writes=[f"r2{i % 8}"])

        def s3(i):
            P.act(r2[i % 8][:], r2[i % 8][:], AF.Sqrt, reads=[f"r2{i % 8}"], writes=[f"r2{i % 8}"])

        def s4(i):
            P.recip(r2[i % 8][:], r2[i % 8][:], reads=[f"r2{i % 8}"], writes=[f"r2{i % 8}"])

        def s5(i):
            tk = i * 128; a_ = am[(i // 4) % NA]; an = f"am{(i // 4) % NA}"; pos = i % 4
            P.dma("sp", xs[i % 3][:], C.x_all[tk:tk + 128, :], key=f"x5{i % 3}", writes=[f"x{i % 3}"])
            for part, pp, pn in ((0, pA, "pA"), (1, pS, "pS")):
                for hf in range(2):
                    for kt in range(4):
                        P.mm(pp[:, hf * 512:(hf + 1) * 512], a_[:, part * 4 + kt, pos * 128:(pos + 1) * 128], wo[:, part * 4 + kt, hf * 512:(hf + 1) * 512],
                             start=(kt == 0), stop=(kt == 3), reads=[an, "wo"], writes=[f"{pn}{hf}"])

        def s6(i):
            r = r2[i % 8]; rn = f"r2{i % 8}"
            P.act(t1[i % 2][:], pA[:, :], AF.Copy, reads=["pA0", "pA1", rn], writes=[f"t1{i % 2}"], scale=r[:, 0:1])
            P.ts(t2[i % 2][:], pS[:, :], r[:, 2:3], None, ALU.mult, reads=["pS0", "pS1", rn], writes=[f"t2{i % 2}"])

        def s7(i):
            cond = cond_of(i); tk = i * 128; x_ = xm[i % NM]; xn_ = f"xm{i % NM}"
            P.tt(t2[i % 2][:], t2[i % 2][:], t1[i % 2][:], ALU.add, reads=[f"t1{i % 2}", f"t2{i % 2}"], writes=[f"t2{i % 2}"])
            P.tt(t2[i % 2][:], t2[i % 2][:], g1[:, cond, :], ALU.mult, reads=[f"t2{i % 2}", "g1"], writes=[f"t2{i % 2}"])
            P.tt(x_[:], xs[i % 3][:], t2[i % 2][:], ALU.add, reads=[f"x{i % 3}", f"t2{i % 2}"], writes=[xn_])
            P.dma("sp", C.xmid_d[tk:tk + 128, :], x_[:], key=f"xmst{i % NM}", reads=[xn_])

        def s8(i):
            P.act(junk[:], xm[i % NM][:], AF.Square, reads=[f"xm{i % NM}"], writes=["junk", f"ss{i % 8}"], accum_out=ss[i % 8][:])

        def s9(i):
            P.ts(ss[i % 8][:], ss[i % 8][:], 1.0 / D, EPS, ALU.mult, ALU.add, reads=[f"ss{i % 8}"], writes=[f"ss{i % 8}"])

        def s10(i):
            P.act(ss[i % 8][:], ss[i % 8][:], AF.Sqrt, reads=[f"ss{i % 8}"], writes=[f"ss{i % 8}"])

        def s11(i):
            P.recip(rstd[i % 8][:], ss[i % 8][:], reads=[f"ss{i % 8}"], writes=[f"rstd{i % 8}"])
            P.ts(xn[i % 2][:], xm[i % NM][:], rstd[i % 8][:, 0:1], None, ALU.mult, reads=[f"xm{i % NM}", f"rstd{i % 8}"],
                 writes=[f"xn{i % 2}"])

        def s12(i):
            for k in range(8):
                P.tr(pT[:, k * 128:(k + 1) * 128], xn[i % 2][:, k * 128:(k + 1) * 128], ident[:], reads=[f"xn{i % 2}", "ident"],
                     writes=[f"pT_{k // 4}"])

        def s13(i):
            cond = cond_of(i); gid, pos, gsz = grp(i); h = hT[gid % 2]; hn_ = f"hT{gid % 2}"
            for k in range(8):
                P.act(h[:, k, pos * 128:(pos + 1) * 128], pT[:, k * 128:(k + 1) * 128], AF.Identity,
                      reads=[f"pT_{k // 4}", "msc0", "msc1", "modcol0", "modcol1"], writes=[hn_], scale=msc[:, cond, k:k + 1],
                      bias=modcol[:, cond, 24 + k:25 + k])
            if pos == gsz - 1:
                c0 = h2col((i - pos) * 128)
                P.dma("sp", C.h2T_d[:, c0:c0 + gsz * 128].rearrange("(k p) t -> p k t", p=128), h[:, :, 0:gsz * 128], key=f"h2st{gid % 2}",
                      reads=[hn_])
        pipeline(NQT, [s0, s1, s2, s3, s4, s5, s6, s7, s8, s9, s10, s11, s12, s13])
        P.emit()


def phase6(C):
    nc = C.nc
    with ExitStack() as ph:
        P = Prog(nc, C.st, "p6", C.dsems)
        sb, ps = mk(nc, ph, "p6")
        wup = sb("wup", [128, 8, 5632], BF16); wdn = sb("wdn", [128, 22, D], BF16)
        t1 = sb("t1", [128, D]); junk = sb("junk", [128, D])
        stg = [t1, junk]; stn = ["t1", "junk"]
        cw = sb("cw", [128, 3, 44]); cb = sb("cb", [128, 44])
        P.dma("sp", cw[:], C.convw_col, key="cw", writes=["cw"]); P.dma("sp", cb[:], C.convb_col, key="cb", writes=["cb"])
        g2 = sb("g2", [128, 2, D]); gf = sb("gf", [128, D])
        for c in range(2):
            P.dma("sp", g2[:, c, :], C.mod_d[c, 5120:6144].partition_broadcast(128), key="g2", writes=["g2"])
        P.dma("sp", gf[:], C.g_final.partition_broadcast(128), key="gf", writes=["gf"])
        NBM = 512
        h2 = [sb(f"h2{i}", [128, 8, NBM + 2], BF16) for i in range(1)]
        up = [sb(f"up{i}", [128, NBM + 2]) for i in range(2)]
        acc = [[sb(f"acc{hv}{i}", [128, NBM]) for i in range(2)] for hv in range(2)]
        actT = sb("actT", [128, 22, NBM], BF16)
        xm = [sb("xm0", [128, D])] * 2
        ss = sb("ss", [128, 1]); rstd = sb("rstd", [128, 1])
        pu = [ps(f"pu{i}", [128, 512]) for i in range(3)]; phh = [ps(f"phh{i}", [128, 512]) for i in range(2)]
        pd = ps("pd", [128, 1024])
        def load_weights():
            n = 0
            for q in (0, 4, 1, 5, 2, 6, 3, 7):
                for k in range(8):
                    s = n % 2; n += 1
                    P.dma("sp", stg[s][:, 0:704], C.w_up[k * 128:(k + 1) * 128, q * 704:(q + 1) * 704], key=f"stg6{s}", writes=[stn[s]])
                    P.copy(wup[:, k, q * 704:(q + 1) * 704], stg[s][:, 0:704], reads=[stn[s]], writes=[f"wup{q}"], eng="pool")
                yield
            for j in range(22):
                s = n % 2; n += 1
                P.dma("sp", stg[s][:, 0:D], C.w_down[j * 128:(j + 1) * 128, :], key=f"stg6{s}", writes=[stn[s]])
                P.copy(wdn[:, j, :], stg[s][:, 0:D], reads=[stn[s]], writes=["wdn"], eng="pool")
                yield

        hsel = sb("hsel", [128, 2])
        P.dma("sp", hsel[:], C.halfsel, key="hsel", writes=["hsel"])
        h2c = sb("h2c", [128, 8, NBM + 2], BF16)
        xmc = junk
        blocks = [(512 * v, LS // 2 + 512 * v, 512, 0) for v in range(4)] + \
                 [(LS + 256 * i, LS + 512 + 256 * i, 256, 1) for i in range(2)]
        iu = 0; ix = 0
        for bi, (tA, tB, NB, cond) in enumerate(blocks):
            s = 0
            cA = h2col(tA) - 1; cB = h2col(tB) - 1
            P.dma("sp", h2[s][:, :, 0:NB + 2], C.h2T_d[:, cA:cA + NB + 2].rearrange("(k p) t -> p k t", p=128), key="h2cA", writes=[f"h2{s}"])
            P.dma("sp", h2c[:, :, 0:NB + 2], C.h2T_d[:, cB:cB + NB + 2].rearrange("(k p) t -> p k t", p=128), key="h2cB", writes=["h2c"])
            P.ts(h2[s][:, :, 0:NB + 2], h2[s][:, :, 0:NB + 2], hsel[:, 0:1], None, ALU.mult, reads=[f"h2{s}", "hsel"], writes=[f"h2{s}"])
            P.stt(h2[s][:, :, 0:NB + 2], h2c[:, :, 0:NB + 2], hsel[:, 1:2], h2[s][:, :, 0:NB + 2], ALU.mult, ALU.add,
                  reads=["h2c", "hsel", f"h2{s}"], writes=[f"h2{s}"])
            if bi == 0:
                wgen = load_weights()
                next(wgen); next(wgen)
            for j in range(22):
                if bi == 0:
                    next(wgen, None)
                a2 = j % 2
                for hv in range(2):
                    m = j + 22 * hv
                    u = iu % 3; ub = iu % 2; hb_ = iu % 2; hcol = 0; iu += 1
                    wr = sorted({f"wup{(m * 128) // 704}", f"wup{((m + 1) * 128 - 1) // 704}"})
                    for k in range(8):
                        P.mm(pu[u][:, 0:NB], wup[:, k, m * 128:(m + 1) * 128], h2[s][:, k, 1:NB + 1], start=(k == 0), stop=(k == 7),
                             reads=wr + [f"h2{s}"], writes=[f"pu{u}"])
                    for k in range(8):
                        P.mm(phh[hb_][:, hcol:hcol + 2], wup[:, k, m * 128:(m + 1) * 128], h2[s][:, k, 0:NB + 2:NB + 1], start=(k == 0),
                             stop=(k == 7), reads=wr + [f"h2{s}"], writes=[f"phh{hb_}"])
                    ac = acc[hv][a2]; acn = f"acc{hv}{a2}"
                    P.act(ac[:, 0:NB], pu[u][:, 0:NB], AF.Identity, reads=[f"pu{u}", "cw", "cb"], writes=[acn],
                          scale=cw[:, 1, m:m + 1], bias=cb[:, m:m + 1])
                    P.act(up[ub][:, 1:NB + 1], pu[u][:, 0:NB], AF.Copy, reads=[f"pu{u}"], writes=[f"up{ub}"])
                    P.act(up[ub][:, 0:NB + 2:NB + 1], phh[hb_][:, hcol:hcol + 2], AF.Copy, reads=[f"phh{hb_}"], writes=[f"up{ub}"])
                    P.stt(ac[:, 0:NB], up[ub][:, 0:NB], cw[:, 0, m:m + 1], ac[:, 0:NB], ALU.mult, ALU.add,
                          reads=[f"up{ub}", "cw", acn], writes=[acn])
                    P.stt(ac[:, 0:NB], up[ub][:, 2:NB + 2], cw[:, 2, m:m + 1], ac[:, 0:NB], ALU.mult, ALU.add,
                          reads=[f"up{ub}", "cw", acn], writes=[acn])
                P.act(acc[0][a2][:, 0:NB], acc[0][a2][:, 0:NB], AF.Silu, reads=[f"acc0{a2}"], writes=[f"acc0{a2}"])
                P.tt(actT[:, j, 0:NB], acc[0][a2][:, 0:NB], acc[1][a2][:, 0:NB], ALU.mult, reads=[f"acc0{a2}", f"acc1{a2}"],
                     writes=["actT"], eng="pool")
            if bi == 0:
                for _ in wgen:
                    pass
            for i in range(NB // 128):
                x = 0; ix += 1
                tkA = tA + i * 128; tkB = tB + i * 128
                P.dma("sp", xm[x][:], C.xmid_d[tkA:tkA + 128, :], key=f"xmA{x}", writes=[f"xm{x}"])
                P.dma("sp", xmc[:], C.xmid_d[tkB:tkB + 128, :], key="xmc", writes=["junk"])
                P.tt(xm[x][:], xm[x][:], mkap(hsel[:, 0:1], [[0, D]]), ALU.mult, reads=[f"xm{x}", "hsel"], writes=[f"xm{x}"], eng="pool")
                P.tt(xmc[:], xmc[:], mkap(hsel[:, 1:2], [[0, D]]), ALU.mult, reads=["junk", "hsel"], writes=["junk"], eng="pool")
                P.tt(xm[x][:], xm[x][:], xmc[:], ALU.add, reads=[f"xm{x}", "junk"], writes=[f"xm{x}"], eng="pool")
                for hf in range(2):
                    for j in range(22):
                        P.mm(pd[:, hf * 512:(hf + 1) * 512], actT[:, j, i * 128:(i + 1) * 128], wdn[:, j, hf * 512:(hf + 1) * 512],
                             start=(j == 0), stop=(j == 21), reads=["actT", "wdn"], writes=[f"pd{hf}"])
                P.tt(t1[:], pd[:, :], g2[:, cond, :], ALU.mult, reads=["pd0", "pd1", "g2"], writes=["t1"])
                P.tt(t1[:], t1[:], xm[x][:], ALU.add, reads=["t1", f"xm{x}"], writes=["t1"])
                P.act(junk[:], t1[:], AF.Square, reads=["t1"], writes=["junk", "ss"], accum_out=ss[:])
                P.ts(ss[:], ss[:], 1.0 / D, EPS, ALU.mult, ALU.add, reads=["ss"], writes=["ss"])
                P.act(ss[:], ss[:], AF.Sqrt, reads=["ss"], writes=["ss"])
                P.recip(rstd[:], ss[:], reads=["ss"], writes=["rstd"])
                P.stt(xm[x][:], t1[:], rstd[:, 0:1], gf[:], ALU.mult, ALU.mult, reads=["t1", "rstd", "gf"], writes=[f"xm{x}"])
                P.dma("sp", C.y_out[tkA:tkA + 128, :], xm[x][:], key=f"yost{x}", reads=[f"xm{x}"])
                P.dma("sp", C.y_out[tkB:tkB + 128, :], xm[x][:], key=f"yost{x}", reads=[f"xm{x}"])
        P.emit()


_NC_CACHE = {}


def kernel(**inp):
    inp = {k: np.asarray(v) for k, v in inp.items()}
    if "nc" not in _NC_CACHE:
        _NC_CACHE["nc"] = build()
    nc = _NC_CACHE["nc"]
    in_maps = [host_inputs(inp, c) for c in range(8)]
    res = run_bass_kernel_spmd(nc, in_maps, core_ids=list(range(8)))
    yp = np.zeros((16, 256, D), np.float32); ysm = np.zeros((4, LS, D), np.float32)
    nk = np.zeros((16, 1, 256, 2, 64), np.float32); nv = np.zeros_like(nk)
    sre = np.zeros((16, 1, 2, 32, 64), np.float32); sim = np.zeros_like(sre)
    for b in range(4):
        r = res.results[b]
        r4 = res.results[b + 4]["y_out"]
        yfull = np.concatenate([r["y_out"][:LS // 2], r4[LS // 2:LS], r["y_out"][LS:LS + 512], r4[LS + 512:]], 0)
        ysm[b] = yfull[:LS]; yp[4 * b:4 * b + 4] = yfull[LS:].reshape(4, 256, D)
        nk[4 * b:4 * b + 4, 0] = r["new_k"].reshape(4, 256, 2, 64); nv[4 * b:4 * b + 4, 0] = r["new_v"].reshape(4, 256, 2, 64)
        sre[4 * b:4 * b + 4, 0] = r["new_sre"]; sim[4 * b:4 * b + 4, 0] = r["new_sim"]
    return (yp, ysm, nk, nv, sre, sim)
```
